# Optimizing a Trainium2 kernel written in Bass

```python
import math
import jax, jax.numpy as jnp
from jax import lax
import numpy as np

D_MODEL = 2048
BATCH = 4
SEQ = 4096
DEPTH = 4

WIDTH_A = D_MODEL // 4
WIDTH_B = D_MODEL // 2
WIDTH_C = D_MODEL // 4
MIX_WIDTH = WIDTH_A + WIDTH_B + WIDTH_C
CHUNK = 128
GROUP_A = 128
N_GROUPS_A = WIDTH_A // GROUP_A
DIFF_HEAD_DIM = 64
V_HEAD_DIM = 2 * DIFF_HEAD_DIM
N_HEADS_B = WIDTH_B // V_HEAD_DIM
Q_BLOCK = 128
CONV_WIDTH = 3
N_GROUPS_C = WIDTH_C // 128
D_FF = ((8 * D_MODEL // 3 + 255) // 256) * 256
EPS = 1e-6

SPLIT_SIZES = [WIDTH_A, WIDTH_A,
               WIDTH_B, WIDTH_B, WIDTH_B,
               WIDTH_C, WIDTH_C, WIDTH_C]
IN_COLS = sum(SPLIT_SIZES)

kernel_name = "hybrid_gmlp_diffattn_shortconv_macaron"


def rms_norm(x, g):
    xf = x.astype(jnp.float32)
    y = xf * lax.rsqrt(jnp.mean(xf * xf, axis=-1, keepdims=True) + EPS)
    return (y * g.astype(jnp.float32)).astype(x.dtype)


def swiglu(h, w_gate, w_up, w_down):
    return (jax.nn.silu(h @ w_gate) * (h @ w_up)) @ w_down


def alibi_slopes(n_heads):
    return jnp.exp2(-8.0 * jnp.arange(1, n_heads + 1, dtype=jnp.float32) / n_heads)


def spatial_gating(u, v, g_v, w_s, b_s):
    bsz, s_len, _ = u.shape
    v = rms_norm(v, g_v)
    vc = v.reshape(bsz, s_len // CHUNK, CHUNK, N_GROUPS_A, GROUP_A)
    causal = jnp.tril(jnp.ones((CHUNK, CHUNK), dtype=bool))
    ws = jnp.where(causal[None], w_s, jnp.zeros_like(w_s))
    mixed = jnp.einsum('gts,bnsgc->bntgc', ws, vc) + b_s.T[None, None, :, :, None]
    return u * mixed.reshape(bsz, s_len, WIDTH_A)


def diff_attention(q, k, v, lam, g_sub, lambda_init):
    bsz, s_len, _ = q.shape
    nb = s_len // Q_BLOCK
    scale = DIFF_HEAD_DIM ** -0.5
    qb = q.reshape(bsz, nb, Q_BLOCK, N_HEADS_B, 2, DIFF_HEAD_DIM).transpose(1, 0, 3, 4, 2, 5)
    k = k.reshape(bsz, s_len, N_HEADS_B, 2, DIFF_HEAD_DIM)
    v = v.reshape(bsz, s_len, N_HEADS_B, V_HEAD_DIM)
    slopes = alibi_slopes(N_HEADS_B)
    kpos = jnp.arange(s_len)

    def block(args):
        q_blk, start = args
        s = jnp.einsum('bhmqd,bkhmd->bhmqk', q_blk, k).astype(jnp.float32) * scale
        dist = (start + jnp.arange(Q_BLOCK))[:, None] - kpos[None, :]
        bias = -slopes[:, None, None] * dist.astype(jnp.float32)
        s = jnp.where((dist >= 0)[None, None, None], s + bias[None, :, None], -jnp.inf)
        p = jax.nn.softmax(s, axis=-1)
        a = p[:, :, 0] - lam * p[:, :, 1]
        return jnp.einsum('bhqk,bkhe->bqhe', a.astype(v.dtype), v)

    starts = jnp.arange(nb) * Q_BLOCK
    out = lax.map(block, (qb, starts))
    out = out.transpose(1, 0, 2, 3, 4).reshape(bsz, s_len, N_HEADS_B, V_HEAD_DIM)
    out = rms_norm(out, g_sub) * (1.0 - lambda_init)
    return out.reshape(bsz, s_len, WIDTH_B)


def short_gated_conv(b_gate, c_gate, x_c, w_conv):
    s_len = x_c.shape[1]
    z = c_gate * x_c
    zp = jnp.pad(z, ((0, 0), (CONV_WIDTH - 1, 0), (0, 0)))
    y = zp[:, 0:s_len] * w_conv[0]
    for j in range(1, CONV_WIDTH):
        y = y + zp[:, j:j + s_len] * w_conv[j]
    return b_gate * y


def setup_inputs(seed: int = 0) -> dict:
    key = jax.random.key(seed)
    ks = jax.random.split(key, 20)
    f32 = jnp.float32
    n = lambda k, shape, s: jax.random.normal(k, shape, f32) * s
    gain = lambda k, shape: 1.0 + 0.01 * jax.random.normal(k, shape, f32)
    return {
        "x": jax.random.normal(ks[0], (BATCH, SEQ, D_MODEL), f32),
        "g_ffn1": gain(ks[1], (DEPTH, D_MODEL)),
        "w_ffn1_gate": n(ks[2], (DEPTH, D_MODEL, D_FF), D_MODEL ** -0.5),
        "w_ffn1_up": n(ks[3], (DEPTH, D_MODEL, D_FF), D_MODEL ** -0.5),
        "w_ffn1_down": n(ks[4], (DEPTH, D_FF, D_MODEL), D_FF ** -0.5),
        "g_mix": gain(ks[5], (DEPTH, D_MODEL)),
        "w_in": n(ks[6], (DEPTH, D_MODEL, IN_COLS), D_MODEL ** -0.5),
        "g_sga_v": gain(ks[7], (DEPTH, WIDTH_A)),
        "w_sga_s": n(ks[8], (DEPTH, N_GROUPS_A, CHUNK, CHUNK), CHUNK ** -0.5),
        "b_sga_s": 1.0 + 0.1 * jax.random.normal(ks[9], (DEPTH, N_GROUPS_A, CHUNK), f32),
        "lambda_qk": n(ks[10], (DEPTH, 4, DIFF_HEAD_DIM), 0.1),
        "g_diff_sub": gain(ks[11], (DEPTH, V_HEAD_DIM)),
        "w_conv": n(ks[12], (DEPTH, CONV_WIDTH, WIDTH_C), CONV_WIDTH ** -0.5),
        "w_out": n(ks[13], (DEPTH, MIX_WIDTH, D_MODEL), MIX_WIDTH ** -0.5),
        "g_ffn2": gain(ks[14], (DEPTH, D_MODEL)),
        "w_ffn2_gate": n(ks[15], (DEPTH, D_MODEL, D_FF), D_MODEL ** -0.5),
        "w_ffn2_up": n(ks[16], (DEPTH, D_MODEL, D_FF), D_MODEL ** -0.5),
        "w_ffn2_down": n(ks[17], (DEPTH, D_FF, D_MODEL), D_FF ** -0.5),
        "g_final": gain(ks[18], (D_MODEL,)),
    }


def reference(x, g_ffn1, w_ffn1_gate, w_ffn1_up, w_ffn1_down, g_mix, w_in, g_sga_v,
              w_sga_s, b_sga_s, lambda_qk, g_diff_sub, w_conv, w_out, g_ffn2,
              w_ffn2_gate, w_ffn2_up, w_ffn2_down, g_final):
    split_points = [int(p) for p in np.cumsum(SPLIT_SIZES)[:-1]]
    for l in range(DEPTH):
        h = rms_norm(x, g_ffn1[l])
        x = x + 0.5 * swiglu(h, w_ffn1_gate[l], w_ffn1_up[l], w_ffn1_down[l])

        h = rms_norm(x, g_mix[l])
        proj = h @ w_in[l]
        u, v, q, k, vv, b_gate, c_gate, x_c = jnp.split(proj, split_points, axis=-1)

        y_a = spatial_gating(u, v, g_sga_v[l], w_sga_s[l], b_sga_s[l])

        lambda_init = 0.8 - 0.6 * math.exp(-0.3 * l)
        lq = lambda_qk[l].astype(jnp.float32)
        lam = jnp.exp(jnp.sum(lq[0] * lq[1])) - jnp.exp(jnp.sum(lq[2] * lq[3])) + lambda_init
        y_b = diff_attention(q, k, vv, lam, g_diff_sub[l], lambda_init)

        y_c = short_gated_conv(b_gate, c_gate, x_c, w_conv[l])

        x = x + jnp.concatenate([y_a, y_b, y_c], axis=-1) @ w_out[l]

        h = rms_norm(x, g_ffn2[l])
        x = x + 0.5 * swiglu(h, w_ffn2_gate[l], w_ffn2_up[l], w_ffn2_down[l])
    return rms_norm(x, g_final)
```

```python
import math
import os
import numpy as np
import ml_dtypes
import concourse.bass as bass
import concourse.mybir as mybir
from concourse.bass_utils import run_bass_kernel_spmd

F32 = mybir.dt.float32
BF16 = mybir.dt.bfloat16
AF = mybir.ActivationFunctionType
ALU = mybir.AluOpType

D = 2048
DFF = 5632
NCH = D // 128
NFF = DFF // 128
SEQ = 4096
NTOK = 2048
NBLK = 16
DEPTH = 4
INCOLS = 5632
EPS = 1e-6
SQRT_D = math.sqrt(float(D))
AG_ROWS = 2056
MASK_BIG = 131072.0
KIB = 1024
REPLICA_GROUPS = [[0, 1]] if os.environ.get('KSIM2') else [[0, 1], [2, 3], [4, 5], [6, 7]]


class Op:
    __slots__ = ("eng", "fn", "deps", "sem", "val", "signal", "idx", "inc")

    def __init__(self, eng, fn):
        self.eng = eng
        self.fn = fn
        self.deps = []
        self.sem = None
        self.val = None
        self.signal = True
        self.inc = 1


class Planner:
    ENGS = ("pe", "act", "dve", "pool", "sp")

    def __init__(self):
        self.ops = {e: [] for e in self.ENGS}
        self.last_writer = {}
        self.readers = {}
        self.sem_names = {}
        self.sem_count = {}
        self.last_on_sem = {}
        for e in ("pe", "act", "dve"):
            self.sem_names["E_" + e] = None

    def _collect(self, reads, writes):
        deps = {}

        def add(op):
            if op is None:
                return
            deps[id(op)] = op

        for r in reads:
            add(self.last_writer.get(r))
        for w in writes:
            add(self.last_writer.get(w))
            for op in self.readers.get(w, {}).values():
                add(op)
        return list(deps.values())

    def _register(self, op, reads, writes):
        for r in reads:
            self.readers.setdefault(r, {})[op.sem if op.sem else ("E", op.eng)] = op
        for w in writes:
            self.last_writer[w] = op
            self.readers[w] = {}

    def op(self, eng, fn, reads=(), writes=(), signal=True):
        o = Op(eng, fn)
        o.signal = signal
        o.sem = "E_" + eng
        deps = self._collect(reads, writes)
        if eng == "pe":
            deps = [d for d in deps if not (d.eng == "pe" and d.sem == "E_pe")]
        o.deps = deps
        self.ops[eng].append(o)
        self._register(o, reads, writes)
        self.last_on_sem[o.sem] = o
        return o

    def dma(self, queue, semkey, fn, reads=(), writes=(), extra_deps=()):
        o = Op(queue, fn)
        o.sem = semkey
        o.inc = 16
        self.sem_names.setdefault(semkey, None)
        self.sem_count[semkey] = self.sem_count.get(semkey, 0) + 16
        o.val = self.sem_count[semkey]
        o.deps = self._collect(reads, writes) + list(extra_deps)
        self.ops[queue].append(o)
        self._register(o, reads, writes)
        self.last_on_sem[semkey] = o
        return o

    def alias_group(self, ops):
        if not ops:
            return
        v = max(o.val for o in ops)
        ids = set(id(o) for o in ops)
        for o in ops:
            o.val = v
            o.deps = [d for d in o.deps if id(d) not in ids]

    def collective(self, semkey, fn, reads=(), writes=()):
        o = Op("pool", fn)
        o.sem = semkey
        o.inc = 1
        self.sem_names.setdefault(semkey, None)
        self.sem_count[semkey] = self.sem_count.get(semkey, 0) + 1
        o.val = self.sem_count[semkey]
        o.deps = self._collect(reads, writes)
        self.ops["pool"].append(o)
        self._register(o, reads, writes)
        self.last_on_sem[semkey] = o
        return o

    def barrier(self):
        lasts = []
        for k, o in self.last_on_sem.items():
            if o is not None:
                if o.sem.startswith("E_"):
                    o.signal = True
                lasts.append(o)
        for e in self.ENGS:
            b = Op(e, None)
            b.signal = False
            b.sem = None
            b.deps = [o for o in lasts if not (o.eng == e and e == "pe" and o.sem == "E_pe")]
            self.ops[e].append(b)
        self.last_writer = {}
        self.readers = {}

    def finalize(self):
        for e in ("pe", "act", "dve"):
            ops = [o for o in self.ops[e] if o.sem == "E_" + e]
            cnt = 0
            for o in ops:
                if o.signal:
                    cnt += 1
                    o.val = cnt
            nxt = None
            for o in reversed(ops):
                if o.signal:
                    nxt = o.val
                else:
                    assert nxt is not None, "trailing non-signalling op with dependants?"
                    o.val = nxt
            self.sem_count["E_" + e] = cnt

    def replay(self, nc, sems, engines):
        pass


def _emit_engine(planner, eng_name, e, sems):
    waited = {}
    for o in planner.ops[eng_name]:
        need = {}
        for d in o.deps:
            if d.val is None:
                continue
            if need.get(d.sem, 0) < d.val:
                need[d.sem] = d.val
        for s, v in need.items():
            if waited.get(s, 0) < v:
                e.wait_ge(sems[s], v)
                waited[s] = v
        if o.fn is None:
            continue
        ins = o.fn(e)
        if o.sem is not None and (o.signal or not o.sem.startswith("E_")):
            if o.sem.startswith("CC"):
                ins.then_inc(sems[o.sem])
            else:
                ins.then_inc(sems[o.sem], o.inc)


def lambda_init(l):
    return 0.8 - 0.6 * math.exp(-0.3 * l)


def build_program(sublayers, final_norm=True, x_in_name="xT", debug=False):
    nc = bass.Bass("TRN2", target_bir_lowering=False)
    P = Planner()

    def din(name, shape, dt=F32):
        return nc.dram_tensor(name, list(shape), dt, kind="ExternalInput").ap()

    layers = sorted(set(l for _, l in sublayers))
    kinds = set(k for k, _ in sublayers)
    xT_in = din(x_in_name, [D, NTOK])
    outT = nc.dram_tensor("outT", [D, NTOK], F32, kind="ExternalOutput").ap()
    gcols_d = din("gcols", [128, 13 * NCH])
    W = {}
    for l in layers:
        if ("ffn1", l) in sublayers:
            W[("g1", l)] = din(f"w1g_{l}", [D, DFF])
            W[("u1", l)] = din(f"w1u_{l}", [D, DFF])
            W[("d1", l)] = din(f"w1d_{l}", [DFF, D])
        if ("ffn2", l) in sublayers:
            W[("g2", l)] = din(f"w2g_{l}", [D, DFF])
            W[("u2", l)] = din(f"w2u_{l}", [D, DFF])
            W[("d2", l)] = din(f"w2d_{l}", [DFF, D])
        if ("mix", l) in sublayers:
            W[("in", l)] = din(f"win_{l}", [D, INCOLS])
            W[("out", l)] = din(f"wout_{l}", [D, D])
            W[("gv", l)] = din(f"gv_{l}", [1, 512])
            W[("ws", l)] = din(f"ws_{l}", [4, 128, 128])
            W[("bs", l)] = din(f"bs_{l}", [1, 512])
            W[("lq", l)] = din(f"lq_{l}", [1, 256])
            W[("gsub", l)] = din(f"gsub_{l}", [1, 128])
            W[("wc", l)] = din(f"wc_{l}", [128, 12])
    has_mix = "mix" in kinds
    if has_mix:
        ident_d = din("ident", [128, 128], BF16)
        identf_d = din("identf", [128, 128], F32)
        tril_d = din("tril01", [128, 128], F32)
        maskA_d = din("maskA", [128, 128], BF16)
        maskB_d = din("maskB", [128, 128], BF16)
        kaug_d = din("kaug", [4, SEQ], BF16)
        qaug_d = din("qaug", [4, NTOK], BF16)
        sel_d = din("sel", [128, 2])
    xs = nc.dram_tensor("xs", [D, NTOK], F32).ap()
    if has_mix:
        qs = nc.dram_tensor("qs", [1024, NTOK], BF16).ap()
        ag_src = nc.dram_tensor("ag_src", [AG_ROWS, NTOK], BF16).ap()
        AG_PARTS = [(0, 512), (512, 1024), (1024, 1536), (1536, 2048), (2048, 2056)]
        ag_dstp = [nc.dram_tensor(f"ag_dst{i}", [2 * (r1 - r0), NTOK], BF16).ap() for i, (r0, r1) in enumerate(AG_PARTS)]
    dbg = {}

    ARENA_BYTES = 190 * KIB
    PERS_BYTES = 15 * KIB

    import contextlib
    with contextlib.ExitStack() as ctx:
        arena = ctx.enter_context(nc.sbuf_tensor("arena", [128, ARENA_BYTES // 2], BF16))
        pers = ctx.enter_context(nc.sbuf_tensor("pers", [128, PERS_BYTES // 2], BF16))
        banks = [ctx.enter_context(nc.psum_tensor(f"bank{i}", [128, 512], F32)) for i in range(8)]

        def carve(base, off, shape, dt):
            n = int(np.prod(shape[1:]))
            esz = 4 if dt == F32 else 2
            assert off % 4 == 0
            a = base[:, off // 2: off // 2 + n * esz // 2]
            if dt == F32:
                a = a.bitcast(F32)
            if len(shape) == 3:
                a = a.rearrange("p (a b) -> p a b", a=shape[1])
            elif len(shape) == 4:
                a = a.rearrange("p (a b c) -> p a b c", a=shape[1], b=shape[2])
            return a

        class Alloc:
            def __init__(self, base, start, limit):
                self.base, self.off, self.limit = base, start, limit

            def __call__(self, shape, dt):
                n = int(np.prod(shape[1:])) * (4 if dt == F32 else 2)
                n = (n + 31) // 32 * 32
                a = carve(self.base, self.off, shape, dt)
                self.off += n
                assert self.off <= self.limit, (self.off, self.limit)
                return a

        pal = Alloc(pers, 0, PERS_BYTES)
        ones_bf = pal([128, 128], BF16)
        gcols = pal([128, 13, NCH], F32)
        epscol = pal([128, 8], F32)
        if has_mix:
            ident = pal([128, 128], BF16)
            identf = pal([128, 128], F32)
            tril = pal([128, 128], F32)
            maskA = pal([128, 128], BF16)
            maskB = pal([128, 128], BF16)
            sel = pal([128, 2], F32)
            gvb = pal([128, 512], F32)
            gsub = pal([128, 128], F32)
            wst = pal([128, 4, 128], BF16)
            bsrow = pal([128, 512], BF16)
            bsrow32 = pal([128, 512], F32)
            lamb = pal([128, 4, 64], F32)
            lamprod = pal([128, 4, 64], F32)
            lamcol = pal([128, 8], F32)
            wconv = pal([128, 3, 4], F32)
            wsnat = pal([128, 4, 128], F32)

        def setup():
            P.op("dve", lambda e: e.memset(ones_bf, 1.0), writes=["ones"])
            P.dma("sp", "D_const", lambda e: e.dma_start(out=gcols.rearrange("p a c -> p (a c)"), in_=gcols_d),
                  writes=["gcols"])
            P.op("dve", lambda e: e.memset(epscol, EPS), writes=["epscol"])
            if has_mix:
                for nm, t, d in (("ident", ident, ident_d), ("identf", identf, identf_d), ("tril", tril, tril_d),
                                 ("maskA", maskA, maskA_d), ("maskB", maskB, maskB_d), ("sel", sel, sel_d)):
                    P.dma("sp", "D_const", (lambda t, d: (lambda e: e.dma_start(out=t, in_=d)))(t, d), writes=[nm])

        def emit_norm(XT, HT, SQ, RSTD, T, gidx, ss_banks, xres, hres, inplace=False):
            nh = T // 512
            for c in range(NCH):
                sl = c % 2
                P.op("act", (lambda c, sl: lambda e: e.activation(out=SQ[:, sl, :], in_=XT[:, c, :], func=AF.Square))(c, sl),
                     reads=[f"{xres}{c}"], writes=[f"SQ{sl}"])
                for h in range(nh):
                    P.op("pe", (lambda c, sl, h: lambda e: e.matmul(banks[ss_banks[h]][:, :], lhsT=ones_bf,
                                                                    rhs=SQ[:, sl, h * 512:(h + 1) * 512],
                                                                    start=(c == 0), stop=(c == NCH - 1)))(c, sl, h),
                         reads=[f"SQ{sl}", "ones"], writes=[f"ps{ss_banks[h]}"], signal=(c == NCH - 1 or True))
            for h in range(nh):
                P.op("act", (lambda h: lambda e: e.activation(out=RSTD[:, h * 512:(h + 1) * 512],
                                                              in_=banks[ss_banks[h]][:, :], func=AF.Sqrt,
                                                              scale=1.0 / float(D), bias=epscol[:, 0:1]))(h),
                     reads=[f"ps{ss_banks[h]}", "epscol"], writes=[f"RSTD{h}"])
                P.op("dve", (lambda h: lambda e: e.reciprocal(out=RSTD[:, h * 512:(h + 1) * 512],
                                                              in_=RSTD[:, h * 512:(h + 1) * 512]))(h),
                     reads=[f"RSTD{h}"], writes=[f"RSTD{h}"])
            for c in range(NCH):
                dst = XT if inplace else HT
                P.op("dve", (lambda c, dst: lambda e: e.scalar_tensor_tensor(out=dst[:, c, :], in0=XT[:, c, :],
                                                                             scalar=gcols[:, gidx, c:c + 1],
                                                                             in1=RSTD[:, 0:T],
                                                                             op0=ALU.mult, op1=ALU.mult))(c, dst),
                     reads=[f"{xres}{c}", "gcols"] + [f"RSTD{h}" for h in range(nh)],
                     writes=[f"{xres if inplace else hres}{c}"])

        def emit_ffn(l, which, x_src, x_dst, final):
            TG = 1024
            NH = 2
            al = Alloc(arena, 0, ARENA_BYTES)
            XT = al([128, NCH, TG], F32)
            HT = al([128, NCH, TG], BF16)
            AT = al([128, 4, TG], BF16)
            SG = al([128, 2, 512], BF16)
            SQ = al([128, 2, TG], BF16)
            RSTD = al([128, TG], F32)
            WGU = [al([128, 2, NCH, 256], BF16) for _ in range(2)]
            WD = [al([128, 4, D], BF16) for _ in range(2)]
            wg, wu, wd = W[("g%d" % which, l)], W[("u%d" % which, l)], W[("d%d" % which, l)]
            wgv = wg.rearrange("(k p) n -> p k n", p=128)
            wuv = wu.rearrange("(k p) n -> p k n", p=128)
            wdv = wd.rearrange("(c p) n -> p c n", p=128)
            gidx = (0 if which == 1 else 2) * 4 + l
            xsv = x_src.rearrange("(c p) t -> p c t", p=128)
            xdv = x_dst.rearrange("(c p) t -> p c t", p=128)
            outv = outT.rearrange("(c p) t -> p c t", p=128)
            ps_gu = [(0, 1), (2, 3), (4, 5)]
            ps_y = [6, 7]
            gu_i = 0
            y_i = 0
            wgu_i = 0
            wd_i = 0
            sg_i = 0
            DBG = int(os.environ.get('KDBG', '9'))
            for tg in range(NTOK // TG):
                t0 = tg * TG
                grp = []
                for q in range(2):
                    grp.append(P.dma("sp", "D_xt", (lambda q, t0: lambda e: e.dma_start(
                        out=XT[:, q * 8:(q + 1) * 8, :], in_=xsv[:, q * 8:(q + 1) * 8, t0:t0 + TG]))(q, t0),
                        reads=["xdram"], writes=[f"XT{c}" for c in range(q * 8, q * 8 + 8)]))
                P.alias_group(grp)
                if DBG >= 1:
                    emit_norm(XT, HT, SQ, RSTD, TG, gidx, [6, 7], "XT", "HT")
                for g in range(NFF // 4 if DBG >= 9 else (1 if DBG >= 2 else 0)):
                    for pair in range(2):
                        slot = wgu_i % 2
                        wgu_i += 1
                        ff0 = (g * 4 + pair * 2) * 128
                        grp = []
                        for gi, wv in enumerate((wgv, wuv)):
                            grp.append(P.dma("pool", f"D_wgu{slot}", (lambda gi, wv, slot, ff0: lambda e: e.dma_start(
                                out=WGU[slot][:, gi, :, :], in_=wv[:, :, ff0:ff0 + 256]))(gi, wv, slot, ff0),
                                writes=[f"WGU{slot}"]))
                        P.alias_group(grp)
                        for cc in range(2):
                            c = pair * 2 + cc
                            for h in range(NH):
                                bg, bu = ps_gu[gu_i % 3]
                                gu_i += 1
                                for gi, bank in ((0, bg), (1, bu)):
                                    for k in range(NCH):
                                        P.op("pe", (lambda slot, gi, k, cc, h, bank: lambda e: e.matmul(
                                            banks[bank][:, :], lhsT=WGU[slot][:, gi, k, cc * 128:(cc + 1) * 128],
                                            rhs=HT[:, k, h * 512:(h + 1) * 512], start=(k == 0), stop=(k == NCH - 1)))(
                                            slot, gi, k, cc, h, bank),
                                            reads=[f"WGU{slot}", f"HT{k}"], writes=[f"ps{bank}"], signal=(k == NCH - 1))
                                s = sg_i % 2
                                sg_i += 1
                                P.op("act", (lambda s, bg: lambda e: e.activation(out=SG[:, s, :], in_=banks[bg][:, :],
                                                                                 func=AF.Silu))(s, bg),
                                     reads=[f"ps{bg}"], writes=[f"SG{s}"])
                                P.op("dve", (lambda s, bu, c, h: lambda e: e.tensor_tensor(
                                    out=AT[:, c, h * 512:(h + 1) * 512], in0=SG[:, s, :], in1=banks[bu][:, :],
                                    op=ALU.mult))(s, bu, c, h),
                                    reads=[f"SG{s}", f"ps{bu}"], writes=[f"AT{c}_{h}"])
                    if DBG == 2:
                        continue
                    slot = wd_i % 2
                    wd_i += 1
                    P.dma("pool", f"D_wd{slot}", (lambda slot, g: lambda e: e.dma_start(
                        out=WD[slot][:, :, :], in_=wdv[:, g * 4:(g + 1) * 4, :]))(slot, g), writes=[f"WD{slot}"])
                    for j in range(NCH):
                        for h in range(NH):
                            bank = ps_y[y_i % 2]
                            y_i += 1
                            for c in range(4):
                                P.op("pe", (lambda slot, c, j, h, bank: lambda e: e.matmul(
                                    banks[bank][:, :], lhsT=WD[slot][:, c, j * 128:(j + 1) * 128],
                                    rhs=AT[:, c, h * 512:(h + 1) * 512], start=(c == 0), stop=(c == 3)))(slot, c, j, h, bank),
                                    reads=[f"WD{slot}", f"AT{c}_{h}"], writes=[f"ps{bank}"], signal=(c == 3))
                            P.op("dve", (lambda j, h, bank: lambda e: e.scalar_tensor_tensor(
                                out=XT[:, j, h * 512:(h + 1) * 512], in0=banks[bank][:, :], scalar=0.5,
                                in1=XT[:, j, h * 512:(h + 1) * 512], op0=ALU.mult, op1=ALU.add))(j, h, bank),
                                reads=[f"ps{bank}", f"XT{j}"], writes=[f"XT{j}"])
                if final:
                    emit_norm(XT, None, SQ, RSTD, TG, 12, [6, 7], "XT", None, inplace=True)
                    dstv = outv
                    dres = "outdram"
                else:
                    dstv = xdv
                    dres = "xdram"
                grp = []
                for q in range(2):
                    grp.append(P.dma("sp", "D_xst", (lambda q, dstv, t0: lambda e: e.dma_start(
                        out=dstv[:, q * 8:(q + 1) * 8, t0:t0 + TG], in_=XT[:, q * 8:(q + 1) * 8, :]))(q, dstv, t0),
                        reads=[f"XT{c}" for c in range(q * 8, q * 8 + 8)], writes=[dres]))
                P.alias_group(grp)

        MIX = {}

        def B(f, *a):
            return lambda e: f(e, *a)

        AX = mybir.AxisListType.X

        def emit_mix(l, x_src, x_dst):
            TG = 512
            NTG = NTOK // TG
            al = Alloc(arena, 0, ARENA_BYTES)
            CAT = al([128, NCH, NTOK], BF16)
            ZT = al([128, 4, NBLK, 130], BF16)
            base = al.off
            win = W[("in", l)].rearrange("(k p) n -> p k n", p=128)
            wout = W[("out", l)].rearrange("(k p) n -> p k n", p=128)
            xsv = x_src.rearrange("(c p) t -> p c t", p=128)
            xdv = x_dst.rearrange("(c p) t -> p c t", p=128)
            linit = lambda_init(l)
            vsrc = ag_src[1024:2048, :].rearrange("r (two f) -> (r two) f", two=2)
            tsrc = ag_src[2048:2056, :].rearrange("r (a f) -> (r a) f", f=32).rearrange("(ch p) f -> p ch f", p=128)

            smalls = []

            def small(nm, t, d):
                smalls.append(P.dma("sp", "D_const", B(lambda e, t, d: e.dma_start(out=t, in_=d), t, d), writes=[nm]))
            small("gvb", gvb, W[("gv", l)].partition_broadcast(128))
            small("gsub", gsub, W[("gsub", l)].partition_broadcast(128))
            small("lamb", lamb.rearrange("p a b -> p (a b)"), W[("lq", l)].partition_broadcast(128))
            small("bsrow32", bsrow32[0:1, :], W[("bs", l)])
            small("wconv", wconv.rearrange("p a b -> p (a b)"), W[("wc", l)])
            small("wsnat", wsnat, W[("ws", l)].rearrange("g t s -> t g s"))
            P.alias_group(smalls)
            P.op("dve", lambda e: e.tensor_scalar(out=gsub, in0=gsub, scalar1=float(1.0 - linit), scalar2=None,
                                                  op0=ALU.mult), reads=["gsub"], writes=["gsub"])
            P.op("act", lambda e: e.activation(out=bsrow[0:1, :], in_=bsrow32[0:1, :], func=AF.Copy),
                 reads=["bsrow32"], writes=["bsrow"])
            P.op("dve", lambda e: e.tensor_tensor(out=lamprod[:, 0:2, :], in0=lamb[:, 0:4:2, :], in1=lamb[:, 1:4:2, :],
                                                  op=ALU.mult), reads=["lamb"], writes=["lamprod"])
            P.op("dve", lambda e: e.reduce_sum(out=lamcol[:, 0:2], in_=lamprod[:, 0:2, :], axis=AX),
                 reads=["lamprod"], writes=["lamcol"])
            P.op("act", lambda e: e.activation(out=lamcol[:, 2:4], in_=lamcol[:, 0:2], func=AF.Exp),
                 reads=["lamcol"], writes=["lamcol"])
            P.op("dve", lambda e: e.tensor_tensor(out=lamcol[:, 4:5], in0=lamcol[:, 2:3], in1=lamcol[:, 3:4],
                                                  op=ALU.subtract), reads=["lamcol"], writes=["lamcol"])
            P.op("dve", lambda e: e.tensor_scalar(out=lamcol[:, 5:6], in0=lamcol[:, 4:5], scalar1=float(linit),
                                                  scalar2=None, op0=ALU.add), reads=["lamcol"], writes=["lamcol"])
            for g in range(4):
                P.op("pe", B(lambda e, g: e.transpose(out=banks[6][:, g * 128:(g + 1) * 128], in_=wsnat[:, g, :],
                                                      identity=identf), g),
                     reads=["wsnat", "identf"], writes=["ps6"])
            for g in range(4):
                P.op("dve", B(lambda e, g: e.tensor_tensor(out=wst[:, g, :], in0=banks[6][:, g * 128:(g + 1) * 128],
                                                           in1=tril, op=ALU.mult), g),
                     reads=["ps6", "tril"], writes=["wst"])

            XT = al([128, NCH, TG], F32)
            HT = al([128, NCH, TG], BF16)
            WS = [al([128, NCH, 256], BF16) for _ in range(2)]
            UT = al([128, 4, TG], BF16)
            SQ = al([128, 2, TG], BF16)
            RSTD = al([128, TG], F32)
            QKST = al([128, 3, TG], BF16)
            CZ = al([128, 4, TG], BF16)
            V32 = al([128, 4, 512], F32)
            VN = al([128, 2, 512], BF16)
            VST = al([128, 2, 4, 1024], BF16)
            TAILS = al([128, 4, NBLK, 2], BF16)
            SSV = al([128, 16], F32)
            JUNK = al([128, 512], BF16)
            m1_end = al.off
            fm_banks = [0, 1, 2, 3]
            tm_banks = [4, 5]
            cnt = {"fm": 0, "tm": 0, "ws": 0, "st": 0, "vn": 0}
            agsrc_res = []
            for tg in range(NTG):
                t0 = tg * TG
                grp = []
                for q in range(2):
                    grp.append(P.dma("sp", "D_xt", B(lambda e, q, t0: e.dma_start(
                        out=XT[:, q * 8:(q + 1) * 8, :], in_=xsv[:, q * 8:(q + 1) * 8, t0:t0 + TG]), q, t0),
                        reads=["xdram"], writes=[f"XT{c}" for c in range(q * 8, q * 8 + 8)]))
                P.alias_group(grp)
                emit_norm(XT, HT, SQ, RSTD, TG, 4 + l, [7], "XT", "HT")
                for cg in range(INCOLS // 256):
                    slot = cnt["ws"] % 2
                    cnt["ws"] += 1
                    P.dma("pool", f"D_ws{slot}", B(lambda e, slot, cg: e.dma_start(
                        out=WS[slot][:, :, :], in_=win[:, :, cg * 256:(cg + 1) * 256]), slot, cg), writes=[f"WS{slot}"])
                    ch0 = 2 * cg
                    is_tm = (4 <= ch0 < 8) or (24 <= ch0 < 32)
                    if not is_tm:
                        for cc in range(2):
                            ch = ch0 + cc
                            bank = fm_banks[cnt["fm"] % 4]
                            cnt["fm"] += 1
                            for k in range(NCH):
                                P.op("pe", B(lambda e, slot, k, cc, bank: e.matmul(
                                    banks[bank][:, :], lhsT=WS[slot][:, k, cc * 128:(cc + 1) * 128], rhs=HT[:, k, :],
                                    start=(k == 0), stop=(k == NCH - 1)), slot, k, cc, bank),
                                    reads=[f"WS{slot}", f"HT{k}"], writes=[f"ps{bank}"], signal=(k == NCH - 1))
                            if ch < 4:
                                P.op("act", B(lambda e, ch, bank: e.activation(out=UT[:, ch, :], in_=banks[bank][:, :],
                                                                                func=AF.Copy), ch, bank),
                                     reads=[f"ps{bank}"], writes=[f"UT{ch}"])
                            elif ch < 16:
                                h = ch - 8
                                s = cnt["st"] % 3
                                cnt["st"] += 1
                                P.op("act", B(lambda e, s, bank, h: e.activation(out=QKST[:, s, :], in_=banks[bank][:, :],
                                                                                  func=AF.Copy, scale=float(2.0 ** (h - 2))),
                                              s, bank, h),
                                     reads=[f"ps{bank}"], writes=[f"QKST{s}"])
                                P.dma("sp", f"D_qkst{s}", B(lambda e, s, h, t0: e.dma_start(
                                    out=qs[h * 128:(h + 1) * 128, t0:t0 + TG], in_=QKST[:, s, :]), s, h, t0),
                                    reads=[f"QKST{s}"], writes=[f"qs{h}_{tg}"])
                            elif ch < 24:
                                h = ch - 16
                                s = cnt["st"] % 3
                                cnt["st"] += 1
                                P.op("dve", B(lambda e, s, bank: e.tensor_copy(out=QKST[:, s, :], in_=banks[bank][:, :]),
                                              s, bank),
                                     reads=[f"ps{bank}"], writes=[f"QKST{s}"])
                                P.dma("sp", f"D_qkst{s}", B(lambda e, s, h, t0: e.dma_start(
                                    out=ag_src[h * 128:(h + 1) * 128, t0:t0 + TG], in_=QKST[:, s, :]), s, h, t0),
                                    reads=[f"QKST{s}"], writes=[f"agk{h}_{tg}"])
                                agsrc_res.append(f"agk{h}_{tg}")
                            elif ch < 36:
                                i = ch - 32
                                P.op("act", B(lambda e, i, bank, t0: e.activation(out=CAT[:, 12 + i, t0:t0 + TG],
                                                                                   in_=banks[bank][:, :], func=AF.Copy),
                                              i, bank, t0),
                                     reads=[f"ps{bank}"], writes=[f"CAT{12 + i}_{tg}"])
                            elif ch < 40:
                                i = ch - 36
                                P.op("act", B(lambda e, i, bank: e.activation(out=CZ[:, i, :], in_=banks[bank][:, :],
                                                                               func=AF.Copy), i, bank),
                                     reads=[f"ps{bank}"], writes=[f"CZ{i}"])
                            else:
                                i = ch - 40
                                P.op("dve", B(lambda e, i, bank, tg: e.tensor_tensor(
                                    out=ZT[:, i, 4 * tg:4 * tg + 4, 2:130],
                                    in0=CZ[:, i, :].rearrange("p (a b) -> p a b", a=4),
                                    in1=banks[bank][:, :].rearrange("p (a b) -> p a b", a=4), op=ALU.mult), i, bank, tg),
                                    reads=[f"CZ{i}", f"ps{bank}"], writes=[f"ZT{i}_{tg}"])
                    else:
                        for tb in range(4):
                            bank = tm_banks[cnt["tm"] % 2]
                            cnt["tm"] += 1
                            for k in range(NCH):
                                P.op("pe", B(lambda e, slot, k, tb, bank: e.matmul(
                                    banks[bank][:, 0:256], lhsT=HT[:, k, tb * 128:(tb + 1) * 128], rhs=WS[slot][:, k, :],
                                    start=(k == 0), stop=(k == NCH - 1)), slot, k, tb, bank),
                                    reads=[f"WS{slot}", f"HT{k}"], writes=[f"ps{bank}"], signal=(k == NCH - 1))
                            if ch0 < 8:
                                c0 = (ch0 - 4) * 128
                                P.op("act", B(lambda e, tb, c0, bank: e.activation(out=V32[:, tb, c0:c0 + 256],
                                                                                    in_=banks[bank][:, 0:256],
                                                                                    func=AF.Copy), tb, c0, bank),
                                     reads=[f"ps{bank}"], writes=[f"V32_{tb}"])
                            else:
                                c0 = (ch0 - 24) * 128
                                vs = tg % 2
                                P.op("dve", B(lambda e, vs, tb, c0, bank: e.tensor_copy(out=VST[:, vs, tb, c0:c0 + 256],
                                                                                         in_=banks[bank][:, 0:256]),
                                              vs, tb, c0, bank),
                                     reads=[f"ps{bank}"], writes=[f"VST{vs}_{tb}_{c0}"])
                        if ch0 == 6:
                            for tb in range(4):
                                blk = 4 * tg + tb
                                P.op("act", B(lambda e, tb: e.activation(out=JUNK, in_=V32[:, tb, :], func=AF.Square,
                                                                         accum_out=SSV[:, tb:tb + 1]), tb),
                                     reads=[f"V32_{tb}"], writes=["JUNK", f"SSV{tb}"])
                                P.op("act", B(lambda e, tb: e.activation(out=SSV[:, 4 + tb:5 + tb], in_=SSV[:, tb:tb + 1],
                                                                         func=AF.Sqrt, scale=1.0 / 512.0,
                                                                         bias=epscol[:, 0:1]), tb),
                                     reads=[f"SSV{tb}", "epscol"], writes=[f"SSR{tb}"])
                                P.op("dve", B(lambda e, tb: e.reciprocal(out=SSV[:, 8 + tb:9 + tb],
                                                                         in_=SSV[:, 4 + tb:5 + tb]), tb),
                                     reads=[f"SSR{tb}"], writes=[f"SSI{tb}"])
                                vn = cnt["vn"] % 2
                                cnt["vn"] += 1
                                P.op("dve", B(lambda e, tb, vn: e.scalar_tensor_tensor(
                                    out=VN[:, vn, :], in0=V32[:, tb, :], scalar=SSV[:, 8 + tb:9 + tb], in1=gvb,
                                    op0=ALU.mult, op1=ALU.mult), tb, vn),
                                    reads=[f"V32_{tb}", f"SSI{tb}", "gvb"], writes=[f"VN{vn}"])
                                for g in range(4):
                                    P.op("pe", B(lambda e, vn, g: e.matmul(
                                        banks[6][:, g * 128:(g + 1) * 128], lhsT=VN[:, vn, g * 128:(g + 1) * 128],
                                        rhs=wst[:, g, :], start=(g == 0), stop=False, skip_group_check=True), vn, g),
                                        reads=[f"VN{vn}", "wst"], writes=["ps6"], signal=False)
                                    P.op("pe", B(lambda e, g: e.matmul(
                                        banks[6][:, g * 128:(g + 1) * 128], lhsT=ones_bf[0:1, 0:128],
                                        rhs=bsrow[0:1, g * 128:(g + 1) * 128], start=False, stop=(g == 3),
                                        skip_group_check=True), g),
                                        reads=["bsrow", "ones"], writes=["ps6"], signal=(g == 3))
                                P.op("dve", B(lambda e, tb, blk: e.tensor_tensor(
                                    out=CAT[:, 0:4, blk * 128:(blk + 1) * 128], in0=UT[:, 0:4, tb * 128:(tb + 1) * 128],
                                    in1=banks[6][:, :].rearrange("p (a b) -> p a b", a=4), op=ALU.mult), tb, blk),
                                    reads=["ps6"] + [f"UT{i}" for i in range(4)], writes=[f"CATA_{blk}"])
                        if ch0 == 30:
                            vs = tg % 2
                            P.dma("sp", f"D_vst{vs}", B(lambda e, vs, t0: e.dma_start(
                                out=vsrc[t0:t0 + TG, :].rearrange("(tb p) f -> p tb f", p=128), in_=VST[:, vs, :, :]), vs, t0),
                                reads=[f"VST{vs}_{tb}_{c0}" for tb in range(4) for c0 in (0, 256, 512, 768)], writes=[f"agv_{tg}"])
                            agsrc_res.append(f"agv_{tg}")
                for i in range(4):
                    P.op("dve", B(lambda e, tg, i: e.tensor_copy(out=TAILS[:, i, 4 * tg:4 * tg + 4, :],
                                                                 in_=ZT[:, i, 4 * tg:4 * tg + 4, 128:130]), tg, i),
                         reads=[f"ZT{i}_{tg}"], writes=[f"TAILS{tg}_{i}"])
            P.dma("sp", "D_tail", lambda e: e.dma_start(out=tsrc, in_=TAILS.rearrange("p c b j -> p c (b j)")),
                  reads=[f"TAILS{tg}_{i}" for tg in range(NTG) for i in range(4)], writes=["agtail"])
            agsrc_res.append("agtail")

            P.barrier()
            KMIX = int(os.environ.get("KMIX", "9"))
            for pi, (r0, r1) in enumerate(AG_PARTS if KMIX >= 2 else []):
                P.collective("CC_ag", B(lambda e, pi, r0, r1: e.collective_compute(
                    "AllGather", ALU.bypass, replica_groups=REPLICA_GROUPS,
                    ins=[ag_src[r0:r1, :].opt()], outs=[ag_dstp[pi].opt()]), pi, r0, r1), reads=[], writes=[f"agdst{pi}"])

            al.off = base
            KH = [[al([128, SEQ], BF16) for m in range(2)] for b in range(2)]
            QH = [[al([128, NTOK], BF16) for m in range(2)] for b in range(2)]
            VH = [al([128, 32, 130], BF16) for b in range(2)]
            PT = al([128, 4, 512], BF16)
            RL = al([128, 2, 4], F32)
            T1 = al([128, 2, 128], F32)
            OSB = al([128, 2, 128], F32)
            SQJ = al([128, 2, 128], F32)
            SSQ = al([128, 2, 4], F32)
            YB = al([128, 2, 128], BF16)
            TA = al([128, 4, 32], BF16)
            TB = al([128, 4, 32], BF16)
            TMPH = al([128, 4, NBLK, 2], F32)
            ACC = al([128, NBLK, 128], F32)
            assert al.off <= ARENA_BYTES

            augs = []
            for b in range(2 if KMIX >= 3 else 0):
                for m in range(2):
                    augs.append(P.dma("sp", "D_const", B(lambda e, b, m: e.dma_start(out=KH[b][m][64:68, :], in_=kaug_d), b, m),
                                      writes=[f"KHaug{b}"]))
                    augs.append(P.dma("sp", "D_const", B(lambda e, b, m: e.dma_start(out=QH[b][m][64:68, :], in_=qaug_d), b, m),
                                      writes=[f"QHaug{b}"]))
            P.alias_group(augs)
            for b in range(2 if KMIX >= 3 else 0):
                P.op("dve", B(lambda e, b: e.memset(VH[b][:, :, 128:129], 1.0), b), writes=[f"VHaug{b}"])

            def tview(r):
                return ag_dstp[4][r * 8:(r + 1) * 8, :].rearrange(
                    "r (a f) -> (r a) f", f=32).rearrange("(ch p) f -> p ch f", p=128)
            if KMIX < 3:
                def _skip(*a, **k):
                    return None
                P_op_saved, P_dma_saved = P.op, P.dma
                P.op, P.dma = _skip, _skip
            P.dma("sp", "D_ta", lambda e: e.dma_start(out=TA, in_=tview(0)), reads=["agdst4"], writes=["TA"])
            P.dma("sp", "D_tb", lambda e: e.dma_start(out=TB, in_=tview(1)), reads=["agdst4"], writes=["TB"])
            TA4 = TA.rearrange("p c (b j) -> p c b j", j=2)
            TB4 = TB.rearrange("p c (b j) -> p c b j", j=2)
            P.op("dve", lambda e: e.tensor_scalar(out=TMPH.rearrange("p c b j -> p (c b j)"),
                                                  in0=TA.rearrange("p c f -> p (c f)"), scalar1=sel[:, 0:1],
                                                  scalar2=None, op0=ALU.mult),
                 reads=["TA", "sel"], writes=["TMPH"])
            for i in range(4):
                P.op("dve", B(lambda e, i: e.scalar_tensor_tensor(out=ZT[:, i, 1:NBLK, 0:2], in0=TB4[:, i, 0:NBLK - 1, :],
                                                                  scalar=sel[:, 1:2], in1=TMPH[:, i, 1:NBLK, :],
                                                                  op0=ALU.mult, op1=ALU.add), i),
                     reads=["TB", "TMPH", "sel"], writes=[f"ZTh{i}"])
            P.op("dve", lambda e: e.tensor_copy(out=ZT[:, :, 0, 0:2], in_=TMPH[:, :, 0, :]),
                 reads=["TMPH"], writes=["ZTh0"])
            zres = ["ZTh0"] + [f"ZTh{i}" for i in range(4)]
            for i in range(4):
                P.op("dve", B(lambda e, i: e.tensor_scalar(out=ACC, in0=ZT[:, i, :, 2:130], scalar1=wconv[:, 2, i:i + 1],
                                                           scalar2=None, op0=ALU.mult), i),
                     reads=zres + ["wconv"], writes=["ACC"])
                P.op("dve", B(lambda e, i: e.scalar_tensor_tensor(out=ACC, in0=ZT[:, i, :, 1:129],
                                                                  scalar=wconv[:, 1, i:i + 1], in1=ACC,
                                                                  op0=ALU.mult, op1=ALU.add), i),
                     reads=zres + ["ACC", "wconv"], writes=["ACC"])
                P.op("dve", B(lambda e, i: e.scalar_tensor_tensor(out=ACC, in0=ZT[:, i, :, 0:128],
                                                                  scalar=wconv[:, 0, i:i + 1], in1=ACC,
                                                                  op0=ALU.mult, op1=ALU.add), i),
                     reads=zres + ["ACC", "wconv"], writes=["ACC"])
                P.op("dve", B(lambda e, i: e.tensor_tensor(
                    out=CAT[:, 12 + i, :].rearrange("p (b t) -> p b t", t=128), in0=ACC,
                    in1=CAT[:, 12 + i, :].rearrange("p (b t) -> p b t", t=128), op=ALU.mult), i),
                    reads=["ACC"], writes=[f"CATC{i}"])

            if KMIX < 3:
                P.op, P.dma = P_op_saved, P_dma_saved
            def vview(r, half):
                return ag_dstp[2 + half][r * 512:(r + 1) * 512, :].rearrange("r (two f) -> (r two) f", two=2)

            def load_head(h):
                b = h % 2
                grp = []
                for m in range(2):
                    for r in range(2):
                        row = h * 128 + m * 64
                        part, loc = row // 512, row % 512
                        row0 = r * 512 + loc
                        grp.append(P.dma("sp", f"D_kh{b}", B(lambda e, b, m, r, part, row0: e.dma_start(
                            out=KH[b][m][0:64, r * NTOK:(r + 1) * NTOK], in_=ag_dstp[part][row0:row0 + 64, :]),
                            b, m, r, part, row0),
                            reads=[f"agdst{part}"], writes=[f"KH{b}"]))
                P.alias_group(grp)
                grp = []
                for m in range(2):
                    row0 = h * 128 + m * 64
                    grp.append(P.dma("sp", f"D_qh{b}", B(lambda e, b, m, row0: e.dma_start(
                        out=QH[b][m][0:64, :], in_=qs[row0:row0 + 64, :]), b, m, row0),
                        reads=[f"qs{h}_{tg}" for tg in range(NTG)], writes=[f"QH{b}"]))
                P.alias_group(grp)
                grp = []
                for r in range(2):
                    for half in range(2):
                        grp.append(P.dma("sp", f"D_vh{b}", B(lambda e, b, r, h, half: e.dma_start(
                            out=VH[b][:, r * 16 + half * 8:r * 16 + half * 8 + 8, 0:128],
                            in_=vview(r, half)[:, h * 128:(h + 1) * 128].rearrange("(t p) f -> p t f", p=128)),
                            b, r, h, half),
                            reads=[f"agdst{2 + half}"], writes=[f"VH{b}"]))
                P.alias_group(grp)

            s_banks = [4, 5, 6, 7]
            cs = {"s": 0, "pt": 0, "sm": 0}

            def attn_head(h):
                b = h % 2
                slope = 2.0 ** (-(h + 1))
                for qg in range(4):
                    units = []
                    for tp in range(4 * qg + 4):
                        for r in range(2):
                            for m in range(2):
                                units.append((tp, r, m))
                    first_o = [True] * 4
                    pend = []

                    def qk(u):
                        tp, r, m = u
                        kt = r * 16 + tp
                        tmin = max(tp, 4 * qg)
                        n = (4 * qg + 4 - tmin) * 128
                        c0 = tmin * 128
                        sb = s_banks[cs["s"] % 4]
                        cs["s"] += 1
                        diag = tp >= 4 * qg
                        P.op("pe", B(lambda e, b, m, kt, c0, n, sb, diag: e.matmul(
                            banks[sb][:, 0:n], lhsT=KH[b][m][0:68, kt * 128:(kt + 1) * 128],
                            rhs=QH[b][m][0:68, c0:c0 + n], start=True, stop=(not diag)), b, m, kt, c0, n, sb, diag),
                            reads=[f"KH{b}", f"KHaug{b}", f"QH{b}", f"QHaug{b}"], writes=[f"ps{sb}"], signal=(not diag))
                        if diag:
                            mk = maskA if r == 0 else maskB
                            P.op("pe", B(lambda e, sb, mk: e.matmul(banks[sb][:, 0:128], lhsT=ident, rhs=mk,
                                                                     start=False, stop=True), sb, mk),
                                 reads=["ident", "maskA", "maskB"], writes=[f"ps{sb}"], signal=True)
                        pt = cs["pt"] % 4
                        cs["pt"] += 1
                        P.op("act", B(lambda e, pt, sb, n: e.activation(out=PT[:, pt, 0:n], in_=banks[sb][:, 0:n],
                                                                         func=AF.Exp, scale=float(slope)), pt, sb, n),
                             reads=[f"ps{sb}"], writes=[f"PT{pt}"])
                        return (u, pt, tmin, n)

                    def pv(info):
                        (tp, r, m), pt, tmin, n = info
                        kt = r * 16 + tp
                        for t in range(tmin, 4 * qg + 4):
                            ob = t - 4 * qg
                            st = first_o[ob]
                            first_o[ob] = False
                            last = (tp == t and r == 1)
                            off = (t - tmin) * 128
                            P.op("pe", B(lambda e, ob, m, pt, off, b, kt, st, last: e.matmul(
                                banks[ob][:, m * 129:(m + 1) * 129], lhsT=PT[:, pt, off:off + 128],
                                rhs=VH[b][:, kt, 0:129], start=st, stop=last, skip_group_check=True),
                                ob, m, pt, off, b, kt, st, last),
                                reads=[f"PT{pt}", f"VH{b}", f"VHaug{b}"], writes=[f"ps{ob}"],
                                signal=(last or t == 4 * qg + 3))
                            if last and m == 1:
                                post(h, t, ob)

                    def post(h, t, ob):
                        sm = cs["sm"] % 2
                        cs["sm"] += 1
                        bk = banks[ob]
                        P.op("dve", B(lambda e, sm, bk: e.reciprocal(out=RL[:, sm, 0:2], in_=bk[:, 128:258:129]), sm, bk),
                             reads=[f"ps{ob}"], writes=[f"RL{sm}"])
                        P.op("dve", B(lambda e, sm: e.tensor_tensor(out=RL[:, sm, 2:3], in0=RL[:, sm, 1:2],
                                                                    in1=lamcol[:, 5:6], op=ALU.mult), sm),
                             reads=[f"RL{sm}", "lamcol"], writes=[f"RLb{sm}"])
                        P.op("dve", B(lambda e, sm, bk: e.tensor_scalar(out=T1[:, sm, :], in0=bk[:, 129:257],
                                                                        scalar1=RL[:, sm, 2:3], scalar2=None,
                                                                        op0=ALU.mult), sm, bk),
                             reads=[f"ps{ob}", f"RLb{sm}"], writes=[f"T1{sm}"])
                        P.op("dve", B(lambda e, sm, bk: e.scalar_tensor_tensor(out=OSB[:, sm, :], in0=bk[:, 0:128],
                                                                               scalar=RL[:, sm, 0:1], in1=T1[:, sm, :],
                                                                               op0=ALU.mult, op1=ALU.subtract), sm, bk),
                             reads=[f"ps{ob}", f"RL{sm}", f"T1{sm}"], writes=[f"OSB{sm}"])
                        P.op("dve", B(lambda e, sm: e.tensor_tensor(out=SQJ[:, sm, :], in0=OSB[:, sm, :],
                                                                    in1=OSB[:, sm, :], op=ALU.mult), sm),
                             reads=[f"OSB{sm}"], writes=[f"SQJ{sm}"])
                        P.op("dve", B(lambda e, sm: e.reduce_sum(out=SSQ[:, sm, 0:1], in_=SQJ[:, sm, :], axis=AX), sm),
                             reads=[f"SQJ{sm}"], writes=[f"SSQ{sm}"])
                        P.op("act", B(lambda e, sm: e.activation(out=SSQ[:, sm, 1:2], in_=SSQ[:, sm, 0:1], func=AF.Ln,
                                                                 scale=1.0 / 128.0, bias=epscol[:, 0:1]), sm),
                             reads=[f"SSQ{sm}", "epscol"], writes=[f"SSL{sm}"])
                        P.op("act", B(lambda e, sm: e.activation(out=SSQ[:, sm, 2:3], in_=SSQ[:, sm, 1:2], func=AF.Exp,
                                                                 scale=-0.5), sm),
                             reads=[f"SSL{sm}"], writes=[f"SSE{sm}"])
                        P.op("dve", B(lambda e, sm: e.scalar_tensor_tensor(out=YB[:, sm, :], in0=OSB[:, sm, :],
                                                                           scalar=SSQ[:, sm, 2:3], in1=gsub,
                                                                           op0=ALU.mult, op1=ALU.mult), sm),
                             reads=[f"OSB{sm}", f"SSE{sm}", "gsub"], writes=[f"YB{sm}"])
                        sb = s_banks[cs["s"] % 4]
                        cs["s"] += 1
                        P.op("pe", B(lambda e, sm, sb: e.transpose(out=banks[sb][:, 0:64].bitcast(BF16), in_=YB[:, sm, :],
                                                                    identity=ident), sm, sb),
                             reads=[f"YB{sm}", "ident"], writes=[f"ps{sb}"], signal=True)
                        P.op("dve", B(lambda e, sb, h, t: e.tensor_copy(out=CAT[:, 4 + h, t * 128:(t + 1) * 128],
                                                                        in_=banks[sb][:, 0:64].bitcast(BF16)), sb, h, t),
                             reads=[f"ps{sb}"], writes=[f"CATB{h}_{t}"])

                    SKEW = 2
                    for i, u in enumerate(units):
                        pend.append(qk(u))
                        if len(pend) > SKEW:
                            pv(pend.pop(0))
                    while pend:
                        pv(pend.pop(0))

            NHEADS = 8 if KMIX >= 9 else (KMIX - 3 if KMIX >= 4 else 0)
            if NHEADS:
                load_head(0)
            for h in range(NHEADS):
                if h + 1 < NHEADS:
                    load_head(h + 1)
                attn_head(h)

            P.barrier()
            al.off = base
            XT6 = al([128, NCH, TG], F32)
            WS6 = [al([128, NCH, 256], BF16) for _ in range(2)]
            c6 = {"ws": 0, "bk": 0}
            for tg in range(NTG):
                t0 = tg * TG
                grp = []
                for q in range(2):
                    grp.append(P.dma("sp", "D_xt", B(lambda e, q, t0: e.dma_start(
                        out=XT6[:, q * 8:(q + 1) * 8, :], in_=xsv[:, q * 8:(q + 1) * 8, t0:t0 + TG]), q, t0),
                        reads=["xdram"], writes=[f"XT{c}" for c in range(q * 8, q * 8 + 8)]))
                P.alias_group(grp)
                for cg in range(D // 256):
                    slot = c6["ws"] % 2
                    c6["ws"] += 1
                    P.dma("pool", f"D_ws{slot}", B(lambda e, slot, cg: e.dma_start(
                        out=WS6[slot][:, :, :], in_=wout[:, :, cg * 256:(cg + 1) * 256]), slot, cg), writes=[f"WS{slot}"])
                    for cc in range(2):
                        j = 2 * cg + cc
                        bank = c6["bk"] % 4
                        c6["bk"] += 1
                        for k in range(NCH):
                            P.op("pe", B(lambda e, slot, k, cc, bank, t0: e.matmul(
                                banks[bank][:, :], lhsT=WS6[slot][:, k, cc * 128:(cc + 1) * 128],
                                rhs=CAT[:, k, t0:t0 + TG], start=(k == 0), stop=(k == NCH - 1)), slot, k, cc, bank, t0),
                                reads=[f"WS{slot}"], writes=[f"ps{bank}"], signal=(k == NCH - 1))
                        P.op("dve", B(lambda e, j, bank: e.tensor_tensor(out=XT6[:, j, :], in0=banks[bank][:, :],
                                                                         in1=XT6[:, j, :], op=ALU.add), j, bank),
                             reads=[f"ps{bank}", f"XT{j}"], writes=[f"XT{j}"])
                grp = []
                for q in range(2):
                    grp.append(P.dma("sp", "D_xst", B(lambda e, q, t0: e.dma_start(
                        out=xdv[:, q * 8:(q + 1) * 8, t0:t0 + TG], in_=XT6[:, q * 8:(q + 1) * 8, :]), q, t0),
                        reads=[f"XT{c}" for c in range(q * 8, q * 8 + 8)], writes=["xdram"]))
                P.alias_group(grp)

        MIX["emit"] = emit_mix

        setup()
        P.barrier()
        cur_src = xT_in
        n_sub = len(sublayers)
        for i, (kind, l) in enumerate(sublayers):
            last = (i == n_sub - 1)
            if kind in ("ffn1", "ffn2"):
                emit_ffn(l, 1 if kind == "ffn1" else 2, cur_src, xs, final=(last and final_norm))
            else:
                MIX["emit"](l, cur_src, xs)
            cur_src = xs
            P.barrier()
        if not final_norm:
            al = Alloc(arena, 0, ARENA_BYTES)
            XT = al([128, NCH, 1024], F32)
            xsv = xs.rearrange("(c p) t -> p c t", p=128)
            outv = outT.rearrange("(c p) t -> p c t", p=128)
            for tg in range(2):
                t0 = tg * 1024
                P.dma("sp", "D_xt", (lambda t0: lambda e: e.dma_start(out=XT, in_=xsv[:, :, t0:t0 + 1024]))(t0),
                      reads=["xdram"], writes=["XTall"])
                P.dma("sp", "D_xst", (lambda t0: lambda e: e.dma_start(out=outv[:, :, t0:t0 + 1024], in_=XT))(t0),
                      reads=["XTall"], writes=["outdram"])
            P.barrier()

        P.barrier()
        P.finalize()

        sems = {}
        for k in P.sem_names:
            sems[k] = ctx.enter_context(nc.semaphore(k))
        block = ctx.enter_context(nc.Block())

        @block.tensor
        def _(e):
            _emit_engine(P, "pe", e, sems)

        @block.scalar
        def _(e):
            _emit_engine(P, "act", e, sems)

        @block.vector
        def _(e):
            _emit_engine(P, "dve", e, sems)

        @block.gpsimd
        def _(e):
            _emit_engine(P, "pool", e, sems)

        @block.sync
        def _(e):
            _emit_engine(P, "sp", e, sems)

    nc._planner_stats = {e: len(P.ops[e]) for e in P.ENGS}
    nc._sem_counts = dict(P.sem_count)
    return nc


def core_token_index(r):
    t = np.arange(NBLK)
    return ((2 * t[:, None] + r) * 128 + np.arange(128)[None, :]).reshape(-1)


def make_common_inputs(inputs, sublayers):
    f32 = np.float32
    com = {}
    g = np.zeros((13, D), f32)
    g[0:4] = np.asarray(inputs["g_ffn1"], f32)
    g[4:8] = np.asarray(inputs["g_mix"], f32)
    g[8:12] = np.asarray(inputs["g_ffn2"], f32)
    g[12] = np.asarray(inputs["g_final"], f32)
    com["gcols"] = np.ascontiguousarray(g.reshape(13, NCH, 128).transpose(2, 0, 1).reshape(128, 13 * NCH))
    if any(k == "mix" for k, _ in sublayers):
        bf = ml_dtypes.bfloat16
        com["ident"] = np.eye(128, dtype=f32).astype(bf)
        com["identf"] = np.eye(128, dtype=f32)
        ii = np.arange(128)
        com["tril01"] = (ii[:, None] <= ii[None, :]).astype(f32)
        n = np.arange(SEQ)
        rr, tt, i_ = n // NTOK, (n % NTOK) // 128, n % 128
        com["kaug"] = np.stack([np.ones(SEQ), np.ones(SEQ), 2 * tt + rr, i_]).astype(f32).astype(bf)
    for kind, l in sublayers:
        if kind == "mix":
            com[f"win_{l}"] = np.asarray(inputs["w_in"][l], f32)
            com[f"wout_{l}"] = np.asarray(inputs["w_out"][l], f32)
            com[f"gv_{l}"] = np.asarray(inputs["g_sga_v"][l], f32).reshape(1, 512)
            com[f"ws_{l}"] = np.asarray(inputs["w_sga_s"][l], f32)
            com[f"bs_{l}"] = np.asarray(inputs["b_sga_s"][l], f32).reshape(1, 512)
            com[f"lq_{l}"] = np.asarray(inputs["lambda_qk"][l], f32).reshape(1, 256)
            com[f"gsub_{l}"] = np.asarray(inputs["g_diff_sub"][l], f32).reshape(1, 128)
            com[f"wc_{l}"] = np.ascontiguousarray(
                np.asarray(inputs["w_conv"][l], f32).reshape(3, 4, 128).transpose(2, 0, 1).reshape(128, 12))
    for kind, l in sublayers:
        if kind == "ffn1":
            com[f"w1g_{l}"] = np.asarray(inputs["w_ffn1_gate"][l], f32)
            com[f"w1u_{l}"] = np.asarray(inputs["w_ffn1_up"][l], f32)
            com[f"w1d_{l}"] = np.asarray(inputs["w_ffn1_down"][l], f32)
        elif kind == "ffn2":
            com[f"w2g_{l}"] = np.asarray(inputs["w_ffn2_gate"][l], f32)
            com[f"w2u_{l}"] = np.asarray(inputs["w_ffn2_up"][l], f32)
            com[f"w2d_{l}"] = np.asarray(inputs["w_ffn2_down"][l], f32)
    return com


def core_constants(r):
    f32 = np.float32
    bf = ml_dtypes.bfloat16
    n = np.arange(NTOK)
    t, i_ = n // 128, n % 128
    qaug = np.stack([-128.0 * (2 * t + r), -1.0 * i_, np.full(NTOK, 128.0), np.ones(NTOK)]).astype(f32).astype(bf)
    ii = np.arange(128)
    tri = np.where(ii[:, None] > ii[None, :], -MASK_BIG, 0.0).astype(f32)
    allm = np.full((128, 128), -MASK_BIG, f32)
    zero = np.zeros((128, 128), f32)
    if r == 0:
        mA, mB = tri, allm
        sel = np.tile(np.array([[0.0, 1.0]], f32), (128, 1))
    else:
        mA, mB = zero, tri
        sel = np.tile(np.array([[1.0, 0.0]], f32), (128, 1))
    return {"qaug": qaug, "maskA": mA.astype(bf), "maskB": mB.astype(bf), "sel": np.ascontiguousarray(sel)}


_NC_CACHE = {}


def run_sublayers(inputs, sublayers, x_cores, final_norm):
    key = (tuple(sublayers), final_norm)
    if key not in _NC_CACHE:
        _NC_CACHE[key] = build_program(list(sublayers), final_norm=final_norm)
    nc = _NC_CACHE[key]
    com = make_common_inputs(inputs, sublayers)
    in_maps = []
    has_mix = any(k == "mix" for k, _ in sublayers)
    for c in range(8):
        m = dict(com)
        m["xT"] = x_cores[c]
        if has_mix:
            m.update(core_constants(c % 2))
        in_maps.append(m)
    res = run_bass_kernel_spmd(nc, in_maps, core_ids=list(range(8)))
    return [np.asarray(res.results[c]["outT"]) for c in range(8)]


def shard_x(x):
    x = np.asarray(x, np.float32)
    outs = []
    for c in range(8):
        b, r = c // 2, c % 2
        idx = core_token_index(r)
        outs.append(np.ascontiguousarray(x[b][idx].T))
    return outs


def unshard_x(outs):
    y = np.zeros((4, SEQ, D), np.float32)
    for c in range(8):
        b, r = c // 2, c % 2
        idx = core_token_index(r)
        y[b][idx] = outs[c].T
    return y


def kernel(**inputs):
    subl = []
    for l in range(DEPTH):
        subl += [("ffn1", l), ("mix", l), ("ffn2", l)]
    xc = shard_x(inputs["x"])
    outs = run_sublayers(inputs, subl, xc, final_norm=True)
    return unshard_x(outs)
```

```python
import math
import os
import numpy as np
import ml_dtypes
import concourse.bass as bass
import concourse.mybir as mybir
from concourse.bass_utils import run_bass_kernel_spmd

F32 = mybir.dt.float32
BF16 = mybir.dt.bfloat16
AF = mybir.ActivationFunctionType
ALU = mybir.AluOpType

D = 2048
DFF = 5632
NCH = D // 128
NFF = DFF // 128
SEQ = 4096
NTOK = 2048
NBLK = 16
DEPTH = 4
INCOLS = 5632
EPS = 1e-6
SQRT_D = math.sqrt(float(D))
AG_ROWS = 2056
MASK_BIG = 131072.0
KIB = 1024
REPLICA_GROUPS = [[0, 1]] if os.environ.get('KSIM2') else [[0, 1], [2, 3], [4, 5], [6, 7]]


class Op:
    __slots__ = ("eng", "fn", "deps", "sem", "val", "signal", "idx", "inc")

    def __init__(self, eng, fn):
        self.eng = eng
        self.fn = fn
        self.deps = []
        self.sem = None
        self.val = None
        self.signal = True
        self.inc = 1


class Planner:
    ENGS = ("pe", "act", "dve", "pool", "sp")

    def __init__(self):
        self.ops = {e: [] for e in self.ENGS}
        self.last_writer = {}
        self.readers = {}
        self.sem_names = {}
        self.sem_count = {}
        self.last_on_sem = {}
        for e in ("pe", "act", "dve"):
            self.sem_names["E_" + e] = None

    def _collect(self, reads, writes):
        deps = {}

        def add(op):
            if op is None:
                return
            deps[id(op)] = op

        for r in reads:
            add(self.last_writer.get(r))
        for w in writes:
            add(self.last_writer.get(w))
            for op in self.readers.get(w, {}).values():
                add(op)
        return list(deps.values())

    def _register(self, op, reads, writes):
        for r in reads:
            self.readers.setdefault(r, {})[op.sem if op.sem else ("E", op.eng)] = op
        for w in writes:
            self.last_writer[w] = op
            self.readers[w] = {}

    def op(self, eng, fn, reads=(), writes=(), signal=True):
        o = Op(eng, fn)
        o.signal = signal
        o.sem = "E_" + eng
        deps = self._collect(reads, writes)
        if eng == "pe":
            deps = [d for d in deps if not (d.eng == "pe" and d.sem == "E_pe")]
        o.deps = deps
        self.ops[eng].append(o)
        self._register(o, reads, writes)
        self.last_on_sem[o.sem] = o
        return o

    def dma(self, queue, semkey, fn, reads=(), writes=(), extra_deps=()):
        o = Op(queue, fn)
        o.sem = semkey
        o.inc = 16
        self.sem_names.setdefault(semkey, None)
        self.sem_count[semkey] = self.sem_count.get(semkey, 0) + 16
        o.val = self.sem_count[semkey]
        o.deps = self._collect(reads, writes) + list(extra_deps)
        self.ops[queue].append(o)
        self._register(o, reads, writes)
        self.last_on_sem[semkey] = o
        return o

    def alias_group(self, ops):
        if not ops:
            return
        v = max(o.val for o in ops)
        ids = set(id(o) for o in ops)
        for o in ops:
            o.val = v
            o.deps = [d for d in o.deps if id(d) not in ids]

    def collective(self, semkey, fn, reads=(), writes=()):
        o = Op("pool", fn)
        o.sem = semkey
        o.inc = 1
        self.sem_names.setdefault(semkey, None)
        self.sem_count[semkey] = self.sem_count.get(semkey, 0) + 1
        o.val = self.sem_count[semkey]
        o.deps = self._collect(reads, writes)
        self.ops["pool"].append(o)
        self._register(o, reads, writes)
        self.last_on_sem[semkey] = o
        return o

    def barrier(self):
        lasts = []
        for k, o in self.last_on_sem.items():
            if o is not None:
                if o.sem.startswith("E_"):
                    o.signal = True
                lasts.append(o)
        for e in self.ENGS:
            b = Op(e, None)
            b.signal = False
            b.sem = None
            b.deps = [o for o in lasts if not (o.eng == e and e == "pe" and o.sem == "E_pe")]
            self.ops[e].append(b)
        self.last_writer = {}
        self.readers = {}

    def finalize(self):
        for e in ("pe", "act", "dve"):
            ops = [o for o in self.ops[e] if o.sem == "E_" + e]
            cnt = 0
            for o in ops:
                if o.signal:
                    cnt += 1
                    o.val = cnt
            nxt = None
            for o in reversed(ops):
                if o.signal:
                    nxt = o.val
                else:
                    assert nxt is not None, "trailing non-signalling op with dependants?"
                    o.val = nxt
            self.sem_count["E_" + e] = cnt

    def replay(self, nc, sems, engines):
        pass


def _emit_engine(planner, eng_name, e, sems):
    waited = {}
    for o in planner.ops[eng_name]:
        need = {}
        for d in o.deps:
            if d.val is None:
                continue
            if need.get(d.sem, 0) < d.val:
                need[d.sem] = d.val
        for s, v in need.items():
            if waited.get(s, 0) < v:
                e.wait_ge(sems[s], v)
                waited[s] = v
        if o.fn is None:
            continue
        ins = o.fn(e)
        if o.sem is not None and (o.signal or not o.sem.startswith("E_")):
            if o.sem.startswith("CC"):
                ins.then_inc(sems[o.sem])
            else:
                ins.then_inc(sems[o.sem], o.inc)


def lambda_init(l):
    return 0.8 - 0.6 * math.exp(-0.3 * l)


def build_program(sublayers, final_norm=True, x_in_name="xT", debug=False):
    nc = bass.Bass("TRN2", target_bir_lowering=False)
    P = Planner()

    def din(name, shape, dt=F32):
        return nc.dram_tensor(name, list(shape), dt, kind="ExternalInput").ap()

    layers = sorted(set(l for _, l in sublayers))
    kinds = set(k for k, _ in sublayers)
    xT_in = din(x_in_name, [D, NTOK])
    outT = nc.dram_tensor("outT", [D, NTOK], F32, kind="ExternalOutput").ap()
    gcols_d = din("gcols", [128, 13 * NCH])
    W = {}
    for l in layers:
        if ("ffn1", l) in sublayers:
            W[("g1", l)] = din(f"w1g_{l}", [D, DFF])
            W[("u1", l)] = din(f"w1u_{l}", [D, DFF])
            W[("d1", l)] = din(f"w1d_{l}", [DFF, D])
        if ("ffn2", l) in sublayers:
            W[("g2", l)] = din(f"w2g_{l}", [D, DFF])
            W[("u2", l)] = din(f"w2u_{l}", [D, DFF])
            W[("d2", l)] = din(f"w2d_{l}", [DFF, D])
        if ("mix", l) in sublayers:
            W[("in", l)] = din(f"win_{l}", [D, INCOLS])
            W[("out", l)] = din(f"wout_{l}", [D, D])
            W[("gv", l)] = din(f"gv_{l}", [1, 512])
            W[("ws", l)] = din(f"ws_{l}", [4, 128, 128])
            W[("bs", l)] = din(f"bs_{l}", [1, 512])
            W[("lq", l)] = din(f"lq_{l}", [1, 256])
            W[("gsub", l)] = din(f"gsub_{l}", [1, 128])
            W[("wc", l)] = din(f"wc_{l}", [128, 12])
    has_mix = "mix" in kinds
    if has_mix:
        ident_d = din("ident", [128, 128], BF16)
        identf_d = din("identf", [128, 128], F32)
        tril_d = din("tril01", [128, 128], F32)
        maskA_d = din("maskA", [128, 128], BF16)
        maskB_d = din("maskB", [128, 128], BF16)
        kaug_d = din("kaug", [4, SEQ], BF16)
        qaug_d = din("qaug", [4, NTOK], BF16)
        sel_d = din("sel", [128, 2])
    xs = nc.dram_tensor("xs", [D, NTOK], F32).ap()
    if has_mix:
        qs = nc.dram_tensor("qs", [1024, NTOK], BF16).ap()
        ag_src = nc.dram_tensor("ag_src", [AG_ROWS, NTOK], BF16).ap()
        AG_PARTS = [(0, 512), (512, 1024), (1024, 1536), (1536, 2048), (2048, 2056)]
        ag_dstp = [nc.dram_tensor(f"ag_dst{i}", [2 * (r1 - r0), NTOK], BF16).ap() for i, (r0, r1) in enumerate(AG_PARTS)]
    dbg = {}

    ARENA_BYTES = 190 * KIB
    PERS_BYTES = 15 * KIB

    import contextlib
    with contextlib.ExitStack() as ctx:
        arena = ctx.enter_context(nc.sbuf_tensor("arena", [128, ARENA_BYTES // 2], BF16))
        pers = ctx.enter_context(nc.sbuf_tensor("pers", [128, PERS_BYTES // 2], BF16))
        banks = [ctx.enter_context(nc.psum_tensor(f"bank{i}", [128, 512], F32)) for i in range(8)]

        def carve(base, off, shape, dt):
            n = int(np.prod(shape[1:]))
            esz = 4 if dt == F32 else 2
            assert off % 4 == 0
            a = base[:, off // 2: off // 2 + n * esz // 2]
            if dt == F32:
                a = a.bitcast(F32)
            if len(shape) == 3:
                a = a.rearrange("p (a b) -> p a b", a=shape[1])
            elif len(shape) == 4:
                a = a.rearrange("p (a b c) -> p a b c", a=shape[1], b=shape[2])
            return a

        class Alloc:
            def __init__(self, base, start, limit):
                self.base, self.off, self.limit = base, start, limit

            def __call__(self, shape, dt):
                n = int(np.prod(shape[1:])) * (4 if dt == F32 else 2)
                n = (n + 31) // 32 * 32
                a = carve(self.base, self.off, shape, dt)
                self.off += n
                assert self.off <= self.limit, (self.off, self.limit)
                return a

        pal = Alloc(pers, 0, PERS_BYTES)
        ones_bf = pal([128, 128], BF16)
        gcols = pal([128, 13, NCH], F32)
        epscol = pal([128, 8], F32)
        if has_mix:
            ident = pal([128, 128], BF16)
            identf = pal([128, 128], F32)
            tril = pal([128, 128], F32)
            maskA = pal([128, 128], BF16)
            maskB = pal([128, 128], BF16)
            sel = pal([128, 2], F32)
            gvb = pal([128, 512], F32)
            gsub = pal([128, 128], F32)
            wst = pal([128, 4, 128], BF16)
            bsrow = pal([128, 512], BF16)
            bsrow32 = pal([128, 512], F32)
            lamb = pal([128, 4, 64], F32)
            lamprod = pal([128, 4, 64], F32)
            lamcol = pal([128, 8], F32)
            wconv = pal([128, 3, 4], F32)
            wsnat = pal([128, 4, 128], F32)

        def setup():
            P.op("dve", lambda e: e.memset(ones_bf, 1.0), writes=["ones"])
            P.dma("sp", "D_const", lambda e: e.dma_start(out=gcols.rearrange("p a c -> p (a c)"), in_=gcols_d),
                  writes=["gcols"])
            P.op("dve", lambda e: e.memset(epscol, EPS), writes=["epscol"])
            if has_mix:
                for nm, t, d in (("ident", ident, ident_d), ("identf", identf, identf_d), ("tril", tril, tril_d),
                                 ("maskA", maskA, maskA_d), ("maskB", maskB, maskB_d), ("sel", sel, sel_d)):
                    P.dma("sp", "D_const", (lambda t, d: (lambda e: e.dma_start(out=t, in_=d)))(t, d), writes=[nm])

        def emit_norm(XT, HT, SQ, RSTD, T, gidx, ss_banks, xres, hres, inplace=False):
            nh = T // 512
            for c in range(NCH):
                sl = c % 2
                P.op("act", (lambda c, sl: lambda e: e.activation(out=SQ[:, sl, :], in_=XT[:, c, :], func=AF.Square))(c, sl),
                     reads=[f"{xres}{c}"], writes=[f"SQ{sl}"])
                for h in range(nh):
                    P.op("pe", (lambda c, sl, h: lambda e: e.matmul(banks[ss_banks[h]][:, :], lhsT=ones_bf,
                                                                    rhs=SQ[:, sl, h * 512:(h + 1) * 512],
                                                                    start=(c == 0), stop=(c == NCH - 1)))(c, sl, h),
                         reads=[f"SQ{sl}", "ones"], writes=[f"ps{ss_banks[h]}"], signal=(c == NCH - 1 or True))
            for h in range(nh):
                P.op("act", (lambda h: lambda e: e.activation(out=RSTD[:, h * 512:(h + 1) * 512],
                                                              in_=banks[ss_banks[h]][:, :], func=AF.Sqrt,
                                                              scale=1.0 / float(D), bias=epscol[:, 0:1]))(h),
                     reads=[f"ps{ss_banks[h]}", "epscol"], writes=[f"RSTD{h}"])
                P.op("dve", (lambda h: lambda e: e.reciprocal(out=RSTD[:, h * 512:(h + 1) * 512],
                                                              in_=RSTD[:, h * 512:(h + 1) * 512]))(h),
                     reads=[f"RSTD{h}"], writes=[f"RSTD{h}"])
            for c in range(NCH):
                dst = XT if inplace else HT
                P.op("dve", (lambda c, dst: lambda e: e.scalar_tensor_tensor(out=dst[:, c, :], in0=XT[:, c, :],
                                                                             scalar=gcols[:, gidx, c:c + 1],
                                                                             in1=RSTD[:, 0:T],
                                                                             op0=ALU.mult, op1=ALU.mult))(c, dst),
                     reads=[f"{xres}{c}", "gcols"] + [f"RSTD{h}" for h in range(nh)],
                     writes=[f"{xres if inplace else hres}{c}"])

        def emit_ffn(l, which, x_src, x_dst, final):
            TG = 1024
            NH = 2
            al = Alloc(arena, 0, ARENA_BYTES)
            XT = al([128, NCH, TG], F32)
            HT = al([128, NCH, TG], BF16)
            AT = al([128, 4, TG], BF16)
            SG = al([128, 2, 512], BF16)
            SQ = al([128, 2, TG], BF16)
            RSTD = al([128, TG], F32)
            WGU = [al([128, 2, NCH, 256], BF16) for _ in range(2)]
            WD = [al([128, 4, D], BF16) for _ in range(2)]
            wg, wu, wd = W[("g%d" % which, l)], W[("u%d" % which, l)], W[("d%d" % which, l)]
            wgv = wg.rearrange("(k p) n -> p k n", p=128)
            wuv = wu.rearrange("(k p) n -> p k n", p=128)
            wdv = wd.rearrange("(c p) n -> p c n", p=128)
            gidx = (0 if which == 1 else 2) * 4 + l
            xsv = x_src.rearrange("(c p) t -> p c t", p=128)
            xdv = x_dst.rearrange("(c p) t -> p c t", p=128)
            outv = outT.rearrange("(c p) t -> p c t", p=128)
            ps_gu = [(0, 1), (2, 3), (4, 5)]
            ps_y = [6, 7]
            gu_i = 0
            y_i = 0
            wgu_i = 0
            wd_i = 0
            sg_i = 0
            DBG = int(os.environ.get('KDBG', '9'))
            NTG_F = NTOK // TG

            def load_chunk(tg, c):
                t0 = tg * TG
                P.dma("sp", f"D_xt{c}", B(lambda e, c, t0: e.dma_start(out=XT[:, c, :], in_=xsv[:, c, t0:t0 + TG]), c, t0),
                      reads=[f"xd{tg}_{c}"], writes=[f"XT{c}"])

            def store_chunk(tg, c, dstv, dres):
                t0 = tg * TG
                P.dma("sp", f"D_xst{c}", B(lambda e, c, t0, dstv: e.dma_start(out=dstv[:, c, t0:t0 + TG], in_=XT[:, c, :]),
                                           c, t0, dstv),
                      reads=[f"XT{c}"], writes=[f"{dres}{tg}_{c}"])

            for tg in range(NTG_F):
                t0 = tg * TG
                if tg == 0:
                    for c in range(NCH):
                        load_chunk(0, c)
                if DBG >= 1:
                    emit_norm(XT, HT, SQ, RSTD, TG, gidx, [6, 7], "XT", "HT")
                for g in range(NFF // 4 if DBG >= 9 else (1 if DBG >= 2 else 0)):
                    for pair in range(2):
                        slot = wgu_i % 2
                        wgu_i += 1
                        ff0 = (g * 4 + pair * 2) * 128
                        grp = []
                        for gi, wv in enumerate((wgv, wuv)):
                            grp.append(P.dma("pool", f"D_wgu{slot}", (lambda gi, wv, slot, ff0: lambda e: e.dma_start(
                                out=WGU[slot][:, gi, :, :], in_=wv[:, :, ff0:ff0 + 256]))(gi, wv, slot, ff0),
                                writes=[f"WGU{slot}"]))
                        P.alias_group(grp)
                        for cc in range(2):
                            c = pair * 2 + cc
                            for h in range(NH):
                                bg, bu = ps_gu[gu_i % 3]
                                gu_i += 1
                                for gi, bank in ((0, bg), (1, bu)):
                                    for k in range(NCH):
                                        P.op("pe", (lambda slot, gi, k, cc, h, bank: lambda e: e.matmul(
                                            banks[bank][:, :], lhsT=WGU[slot][:, gi, k, cc * 128:(cc + 1) * 128],
                                            rhs=HT[:, k, h * 512:(h + 1) * 512], start=(k == 0), stop=(k == NCH - 1)))(
                                            slot, gi, k, cc, h, bank),
                                            reads=[f"WGU{slot}", f"HT{k}"], writes=[f"ps{bank}"], signal=(k == NCH - 1))
                                s = sg_i % 2
                                sg_i += 1
                                P.op("act", (lambda s, bg: lambda e: e.activation(out=SG[:, s, :], in_=banks[bg][:, :],
                                                                                 func=AF.Silu))(s, bg),
                                     reads=[f"ps{bg}"], writes=[f"SG{s}"])
                                P.op("dve", (lambda s, bu, c, h: lambda e: e.tensor_tensor(
                                    out=AT[:, c, h * 512:(h + 1) * 512], in0=SG[:, s, :], in1=banks[bu][:, :],
                                    op=ALU.mult))(s, bu, c, h),
                                    reads=[f"SG{s}", f"ps{bu}"], writes=[f"AT{c}_{h}"])
                    if DBG == 2:
                        continue
                    slot = wd_i % 2
                    wd_i += 1
                    P.dma("pool", f"D_wd{slot}", (lambda slot, g: lambda e: e.dma_start(
                        out=WD[slot][:, :, :], in_=wdv[:, g * 4:(g + 1) * 4, :]))(slot, g), writes=[f"WD{slot}"])
                    for j in range(NCH):
                        for h in range(NH):
                            bank = ps_y[y_i % 2]
                            y_i += 1
                            for c in range(4):
                                P.op("pe", (lambda slot, c, j, h, bank: lambda e: e.matmul(
                                    banks[bank][:, :], lhsT=WD[slot][:, c, j * 128:(j + 1) * 128],
                                    rhs=AT[:, c, h * 512:(h + 1) * 512], start=(c == 0), stop=(c == 3)))(slot, c, j, h, bank),
                                    reads=[f"WD{slot}", f"AT{c}_{h}"], writes=[f"ps{bank}"], signal=(c == 3))
                            P.op("dve", (lambda j, h, bank: lambda e: e.scalar_tensor_tensor(
                                out=XT[:, j, h * 512:(h + 1) * 512], in0=banks[bank][:, :], scalar=0.5,
                                in1=XT[:, j, h * 512:(h + 1) * 512], op0=ALU.mult, op1=ALU.add))(j, h, bank),
                                reads=[f"ps{bank}", f"XT{j}"], writes=[f"XT{j}"])
                if final:
                    emit_norm(XT, None, SQ, RSTD, TG, 12, [6, 7], "XT", None, inplace=True)
                    dstv, dres = outv, "od"
                else:
                    dstv, dres = xdv, "xd"
                LAGC = 3
                for c in range(NCH + LAGC):
                    if c < NCH:
                        store_chunk(tg, c, dstv, dres)
                    if tg + 1 < NTG_F and c >= LAGC:
                        load_chunk(tg + 1, c - LAGC)

        MIX = {}

        def B(f, *a):
            return lambda e: f(e, *a)

        AX = mybir.AxisListType.X

        def emit_mix(l, x_src, x_dst):
            TG = 512
            NTG = NTOK // TG
            al = Alloc(arena, 0, ARENA_BYTES)
            CAT = al([128, NCH, NTOK], BF16)
            ZT = al([128, 4, NBLK, 130], BF16)
            base = al.off
            win = W[("in", l)].rearrange("(k p) n -> p k n", p=128)
            wout = W[("out", l)].rearrange("(k p) n -> p k n", p=128)
            xsv = x_src.rearrange("(c p) t -> p c t", p=128)
            xdv = x_dst.rearrange("(c p) t -> p c t", p=128)
            linit = lambda_init(l)
            vsrc = ag_src[1024:2048, :].rearrange("r (two f) -> (r two) f", two=2)
            tsrc = ag_src[2048:2056, :].rearrange("r (a f) -> (r a) f", f=32).rearrange("(ch p) f -> p ch f", p=128)

            smalls = []

            def small(nm, t, d):
                smalls.append(P.dma("sp", "D_const", B(lambda e, t, d: e.dma_start(out=t, in_=d), t, d), writes=[nm]))
            small("gvb", gvb, W[("gv", l)].partition_broadcast(128))
            small("gsub", gsub, W[("gsub", l)].partition_broadcast(128))
            small("lamb", lamb.rearrange("p a b -> p (a b)"), W[("lq", l)].partition_broadcast(128))
            small("bsrow32", bsrow32[0:1, :], W[("bs", l)])
            small("wconv", wconv.rearrange("p a b -> p (a b)"), W[("wc", l)])
            small("wsnat", wsnat, W[("ws", l)].rearrange("g t s -> t g s"))
            P.alias_group(smalls)
            P.op("dve", lambda e: e.tensor_scalar(out=gsub, in0=gsub, scalar1=float(1.0 - linit), scalar2=None,
                                                  op0=ALU.mult), reads=["gsub"], writes=["gsub"])
            P.op("act", lambda e: e.activation(out=bsrow[0:1, :], in_=bsrow32[0:1, :], func=AF.Copy),
                 reads=["bsrow32"], writes=["bsrow"])
            P.op("dve", lambda e: e.tensor_tensor(out=lamprod[:, 0:2, :], in0=lamb[:, 0:4:2, :], in1=lamb[:, 1:4:2, :],
                                                  op=ALU.mult), reads=["lamb"], writes=["lamprod"])
            P.op("dve", lambda e: e.reduce_sum(out=lamcol[:, 0:2], in_=lamprod[:, 0:2, :], axis=AX),
                 reads=["lamprod"], writes=["lamcol"])
            P.op("act", lambda e: e.activation(out=lamcol[:, 2:4], in_=lamcol[:, 0:2], func=AF.Exp),
                 reads=["lamcol"], writes=["lamcol"])
            P.op("dve", lambda e: e.tensor_tensor(out=lamcol[:, 4:5], in0=lamcol[:, 2:3], in1=lamcol[:, 3:4],
                                                  op=ALU.subtract), reads=["lamcol"], writes=["lamcol"])
            P.op("dve", lambda e: e.tensor_scalar(out=lamcol[:, 5:6], in0=lamcol[:, 4:5], scalar1=float(linit),
                                                  scalar2=None, op0=ALU.add), reads=["lamcol"], writes=["lamcol"])
            for g in range(4):
                P.op("pe", B(lambda e, g: e.transpose(out=banks[6][:, g * 128:(g + 1) * 128], in_=wsnat[:, g, :],
                                                      identity=identf), g),
                     reads=["wsnat", "identf"], writes=["ps6"])
            for g in range(4):
                P.op("dve", B(lambda e, g: e.tensor_tensor(out=wst[:, g, :], in0=banks[6][:, g * 128:(g + 1) * 128],
                                                           in1=tril, op=ALU.mult), g),
                     reads=["ps6", "tril"], writes=["wst"])

            XT = al([128, NCH, TG], F32)
            HT = al([128, NCH, TG], BF16)
            WS = [al([128, NCH, 256], BF16) for _ in range(3)]
            UT = al([128, 4, TG], BF16)
            SQ = al([128, 2, TG], BF16)
            RSTD = al([128, TG], F32)
            QKST = al([128, 3, TG], BF16)
            CZ = al([128, 4, TG], BF16)
            V32 = al([128, 4, 512], F32)
            VN = al([128, 2, 512], BF16)
            VST = al([128, 1, 4, 1024], BF16)
            TAILS = al([128, 4, NBLK, 2], BF16)
            SSV = al([128, 16], F32)
            JUNK = al([128, 512], BF16)
            m1_end = al.off
            fm_banks = [0, 1, 2, 3]
            tm_banks = [4, 5]
            cnt = {"fm": 0, "tm": 0, "ws": 0, "st": 0, "vn": 0}
            agsrc_res = []
            for tg in range(NTG):
                t0 = tg * TG
                grp = []
                for q in range(2):
                    grp.append(P.dma("sp", "D_xt", B(lambda e, q, t0: e.dma_start(
                        out=XT[:, q * 8:(q + 1) * 8, :], in_=xsv[:, q * 8:(q + 1) * 8, t0:t0 + TG]), q, t0),
                        reads=["xdram"], writes=[f"XT{c}" for c in range(q * 8, q * 8 + 8)]))
                P.alias_group(grp)
                emit_norm(XT, HT, SQ, RSTD, TG, 4 + l, [7], "XT", "HT")
                for cg in range(INCOLS // 256):
                    slot = cnt["ws"] % 3
                    cnt["ws"] += 1
                    P.dma("pool", f"D_ws{slot}", B(lambda e, slot, cg: e.dma_start(
                        out=WS[slot][:, :, :], in_=win[:, :, cg * 256:(cg + 1) * 256]), slot, cg), writes=[f"WS{slot}"])
                    ch0 = 2 * cg
                    is_tm = (4 <= ch0 < 8) or (24 <= ch0 < 32)
                    if not is_tm:
                        for cc in range(2):
                            ch = ch0 + cc
                            bank = fm_banks[cnt["fm"] % 4]
                            cnt["fm"] += 1
                            for k in range(NCH):
                                P.op("pe", B(lambda e, slot, k, cc, bank: e.matmul(
                                    banks[bank][:, :], lhsT=WS[slot][:, k, cc * 128:(cc + 1) * 128], rhs=HT[:, k, :],
                                    start=(k == 0), stop=(k == NCH - 1)), slot, k, cc, bank),
                                    reads=[f"WS{slot}", f"HT{k}"], writes=[f"ps{bank}"], signal=(k == NCH - 1))
                            if ch < 4:
                                P.op("act", B(lambda e, ch, bank: e.activation(out=UT[:, ch, :], in_=banks[bank][:, :],
                                                                                func=AF.Copy), ch, bank),
                                     reads=[f"ps{bank}"], writes=[f"UT{ch}"])
                            elif ch < 16:
                                h = ch - 8
                                s = cnt["st"] % 3
                                cnt["st"] += 1
                                P.op("act", B(lambda e, s, bank, h: e.activation(out=QKST[:, s, :], in_=banks[bank][:, :],
                                                                                  func=AF.Copy, scale=float(2.0 ** (h - 2))),
                                              s, bank, h),
                                     reads=[f"ps{bank}"], writes=[f"QKST{s}"])
                                P.dma("sp", f"D_qkst{s}", B(lambda e, s, h, t0: e.dma_start(
                                    out=qs[h * 128:(h + 1) * 128, t0:t0 + TG], in_=QKST[:, s, :]), s, h, t0),
                                    reads=[f"QKST{s}"], writes=[f"qs{h}_{tg}"])
                            elif ch < 24:
                                h = ch - 16
                                s = cnt["st"] % 3
                                cnt["st"] += 1
                                P.op("dve", B(lambda e, s, bank: e.tensor_copy(out=QKST[:, s, :], in_=banks[bank][:, :]),
                                              s, bank),
                                     reads=[f"ps{bank}"], writes=[f"QKST{s}"])
                                P.dma("sp", f"D_qkst{s}", B(lambda e, s, h, t0: e.dma_start(
                                    out=ag_src[h * 128:(h + 1) * 128, t0:t0 + TG], in_=QKST[:, s, :]), s, h, t0),
                                    reads=[f"QKST{s}"], writes=[f"agk{h}_{tg}"])
                                agsrc_res.append(f"agk{h}_{tg}")
                            elif ch < 36:
                                i = ch - 32
                                P.op("act", B(lambda e, i, bank, t0: e.activation(out=CAT[:, 12 + i, t0:t0 + TG],
                                                                                   in_=banks[bank][:, :], func=AF.Copy),
                                              i, bank, t0),
                                     reads=[f"ps{bank}"], writes=[f"CAT{12 + i}_{tg}"])
                            elif ch < 40:
                                i = ch - 36
                                P.op("act", B(lambda e, i, bank: e.activation(out=CZ[:, i, :], in_=banks[bank][:, :],
                                                                               func=AF.Copy), i, bank),
                                     reads=[f"ps{bank}"], writes=[f"CZ{i}"])
                            else:
                                i = ch - 40
                                P.op("dve", B(lambda e, i, bank, tg: e.tensor_tensor(
                                    out=ZT[:, i, 4 * tg:4 * tg + 4, 2:130],
                                    in0=CZ[:, i, :].rearrange("p (a b) -> p a b", a=4),
                                    in1=banks[bank][:, :].rearrange("p (a b) -> p a b", a=4), op=ALU.mult), i, bank, tg),
                                    reads=[f"CZ{i}", f"ps{bank}"], writes=[f"ZT{i}_{tg}"])
                    else:
                        for tb in range(4):
                            bank = tm_banks[cnt["tm"] % 2]
                            cnt["tm"] += 1
                            for k in range(NCH):
                                P.op("pe", B(lambda e, slot, k, tb, bank: e.matmul(
                                    banks[bank][:, 0:256], lhsT=HT[:, k, tb * 128:(tb + 1) * 128], rhs=WS[slot][:, k, :],
                                    start=(k == 0), stop=(k == NCH - 1)), slot, k, tb, bank),
                                    reads=[f"WS{slot}", f"HT{k}"], writes=[f"ps{bank}"], signal=(k == NCH - 1))
                            if ch0 < 8:
                                c0 = (ch0 - 4) * 128
                                P.op("act", B(lambda e, tb, c0, bank: e.activation(out=V32[:, tb, c0:c0 + 256],
                                                                                    in_=banks[bank][:, 0:256],
                                                                                    func=AF.Copy), tb, c0, bank),
                                     reads=[f"ps{bank}"], writes=[f"V32_{tb}"])
                            else:
                                c0 = (ch0 - 24) * 128
                                vs = 0
                                P.op("dve", B(lambda e, vs, tb, c0, bank: e.tensor_copy(out=VST[:, vs, tb, c0:c0 + 256],
                                                                                         in_=banks[bank][:, 0:256]),
                                              vs, tb, c0, bank),
                                     reads=[f"ps{bank}"], writes=[f"VST{vs}_{tb}_{c0}"])
                        if ch0 == 6:
                            for tb in range(4):
                                blk = 4 * tg + tb
                                P.op("act", B(lambda e, tb: e.activation(out=JUNK, in_=V32[:, tb, :], func=AF.Square,
                                                                         accum_out=SSV[:, tb:tb + 1]), tb),
                                     reads=[f"V32_{tb}"], writes=["JUNK", f"SSV{tb}"])
                                P.op("act", B(lambda e, tb: e.activation(out=SSV[:, 4 + tb:5 + tb], in_=SSV[:, tb:tb + 1],
                                                                         func=AF.Sqrt, scale=1.0 / 512.0,
                                                                         bias=epscol[:, 0:1]), tb),
                                     reads=[f"SSV{tb}", "epscol"], writes=[f"SSR{tb}"])
                                P.op("dve", B(lambda e, tb: e.reciprocal(out=SSV[:, 8 + tb:9 + tb],
                                                                         in_=SSV[:, 4 + tb:5 + tb]), tb),
                                     reads=[f"SSR{tb}"], writes=[f"SSI{tb}"])
                                vn = cnt["vn"] % 2
                                cnt["vn"] += 1
                                P.op("dve", B(lambda e, tb, vn: e.scalar_tensor_tensor(
                                    out=VN[:, vn, :], in0=V32[:, tb, :], scalar=SSV[:, 8 + tb:9 + tb], in1=gvb,
                                    op0=ALU.mult, op1=ALU.mult), tb, vn),
                                    reads=[f"V32_{tb}", f"SSI{tb}", "gvb"], writes=[f"VN{vn}"])
                                for g in range(4):
                                    P.op("pe", B(lambda e, vn, g: e.matmul(
                                        banks[6][:, g * 128:(g + 1) * 128], lhsT=VN[:, vn, g * 128:(g + 1) * 128],
                                        rhs=wst[:, g, :], start=(g == 0), stop=False, skip_group_check=True), vn, g),
                                        reads=[f"VN{vn}", "wst"], writes=["ps6"], signal=False)
                                    P.op("pe", B(lambda e, g: e.matmul(
                                        banks[6][:, g * 128:(g + 1) * 128], lhsT=ones_bf[0:1, 0:128],
                                        rhs=bsrow[0:1, g * 128:(g + 1) * 128], start=False, stop=(g == 3),
                                        skip_group_check=True), g),
                                        reads=["bsrow", "ones"], writes=["ps6"], signal=(g == 3))
                                P.op("dve", B(lambda e, tb, blk: e.tensor_tensor(
                                    out=CAT[:, 0:4, blk * 128:(blk + 1) * 128], in0=UT[:, 0:4, tb * 128:(tb + 1) * 128],
                                    in1=banks[6][:, :].rearrange("p (a b) -> p a b", a=4), op=ALU.mult), tb, blk),
                                    reads=["ps6"] + [f"UT{i}" for i in range(4)], writes=[f"CATA_{blk}"])
                        if ch0 == 30:
                            vs = 0
                            P.dma("sp", f"D_vst{vs}", B(lambda e, vs, t0: e.dma_start(
                                out=vsrc[t0:t0 + TG, :].rearrange("(tb p) f -> p tb f", p=128), in_=VST[:, vs, :, :]), vs, t0),
                                reads=[f"VST{vs}_{tb}_{c0}" for tb in range(4) for c0 in (0, 256, 512, 768)], writes=[f"agv_{tg}"])
                            agsrc_res.append(f"agv_{tg}")
                for i in range(4):
                    P.op("dve", B(lambda e, tg, i: e.tensor_copy(out=TAILS[:, i, 4 * tg:4 * tg + 4, :],
                                                                 in_=ZT[:, i, 4 * tg:4 * tg + 4, 128:130]), tg, i),
                         reads=[f"ZT{i}_{tg}"], writes=[f"TAILS{tg}_{i}"])
            P.dma("sp", "D_tail", lambda e: e.dma_start(out=tsrc, in_=TAILS.rearrange("p c b j -> p c (b j)")),
                  reads=[f"TAILS{tg}_{i}" for tg in range(NTG) for i in range(4)], writes=["agtail"])
            agsrc_res.append("agtail")

            P.barrier()
            KMIX = int(os.environ.get("KMIX", "9"))
            for pi, (r0, r1) in enumerate(AG_PARTS if KMIX >= 2 else []):
                P.collective("CC_ag", B(lambda e, pi, r0, r1: e.collective_compute(
                    "AllGather", ALU.bypass, replica_groups=REPLICA_GROUPS,
                    ins=[ag_src[r0:r1, :].opt()], outs=[ag_dstp[pi].opt()]), pi, r0, r1), reads=[], writes=[f"agdst{pi}"])

            al.off = base
            KH = [[al([128, SEQ], BF16) for m in range(2)] for b in range(2)]
            QH = [[al([128, NTOK], BF16) for m in range(2)] for b in range(2)]
            VH = [al([128, 32, 130], BF16) for b in range(2)]
            PT = al([128, 6, 512], BF16)
            RL = al([128, 2, 4], F32)
            T1 = al([128, 2, 128], F32)
            OSB = al([128, 2, 128], F32)
            SQJ = al([128, 2, 128], F32)
            SSQ = al([128, 2, 4], F32)
            YB = al([128, 8, 128], BF16)
            TA = al([128, 4, 32], BF16)
            TB = al([128, 4, 32], BF16)
            TMPH = al([128, 4, NBLK, 2], F32)
            ACC = al([128, NBLK, 128], F32)
            assert al.off <= ARENA_BYTES

            augs = []
            for b in range(2 if KMIX >= 3 else 0):
                for m in range(2):
                    augs.append(P.dma("sp", "D_const", B(lambda e, b, m: e.dma_start(out=KH[b][m][64:68, :], in_=kaug_d), b, m),
                                      writes=[f"KHaug{b}"]))
                    augs.append(P.dma("sp", "D_const", B(lambda e, b, m: e.dma_start(out=QH[b][m][64:68, :], in_=qaug_d), b, m),
                                      writes=[f"QHaug{b}"]))
            P.alias_group(augs)
            for b in range(2 if KMIX >= 3 else 0):
                P.op("dve", B(lambda e, b: e.memset(VH[b][:, :, 128:129], 1.0), b), writes=[f"VHaug{b}"])

            def tview(r):
                return ag_dstp[4][r * 8:(r + 1) * 8, :].rearrange(
                    "r (a f) -> (r a) f", f=32).rearrange("(ch p) f -> p ch f", p=128)
            if KMIX < 3:
                def _skip(*a, **k):
                    return None
                P_op_saved, P_dma_saved = P.op, P.dma
                P.op, P.dma = _skip, _skip
            P.dma("sp", "D_ta", lambda e: e.dma_start(out=TA, in_=tview(0)), reads=["agdst4"], writes=["TA"])
            P.dma("sp", "D_tb", lambda e: e.dma_start(out=TB, in_=tview(1)), reads=["agdst4"], writes=["TB"])
            TA4 = TA.rearrange("p c (b j) -> p c b j", j=2)
            TB4 = TB.rearrange("p c (b j) -> p c b j", j=2)
            P.op("dve", lambda e: e.tensor_scalar(out=TMPH.rearrange("p c b j -> p (c b j)"),
                                                  in0=TA.rearrange("p c f -> p (c f)"), scalar1=sel[:, 0:1],
                                                  scalar2=None, op0=ALU.mult),
                 reads=["TA", "sel"], writes=["TMPH"])
            for i in range(4):
                P.op("dve", B(lambda e, i: e.scalar_tensor_tensor(out=ZT[:, i, 1:NBLK, 0:2], in0=TB4[:, i, 0:NBLK - 1, :],
                                                                  scalar=sel[:, 1:2], in1=TMPH[:, i, 1:NBLK, :],
                                                                  op0=ALU.mult, op1=ALU.add), i),
                     reads=["TB", "TMPH", "sel"], writes=[f"ZTh{i}"])
            P.op("dve", lambda e: e.tensor_copy(out=ZT[:, :, 0, 0:2], in_=TMPH[:, :, 0, :]),
                 reads=["TMPH"], writes=["ZTh0"])
            zres = ["ZTh0"] + [f"ZTh{i}" for i in range(4)]
            for i in range(4):
                P.op("dve", B(lambda e, i: e.tensor_scalar(out=ACC, in0=ZT[:, i, :, 2:130], scalar1=wconv[:, 2, i:i + 1],
                                                           scalar2=None, op0=ALU.mult), i),
                     reads=zres + ["wconv"], writes=["ACC"])
                P.op("dve", B(lambda e, i: e.scalar_tensor_tensor(out=ACC, in0=ZT[:, i, :, 1:129],
                                                                  scalar=wconv[:, 1, i:i + 1], in1=ACC,
                                                                  op0=ALU.mult, op1=ALU.add), i),
                     reads=zres + ["ACC", "wconv"], writes=["ACC"])
                P.op("dve", B(lambda e, i: e.scalar_tensor_tensor(out=ACC, in0=ZT[:, i, :, 0:128],
                                                                  scalar=wconv[:, 0, i:i + 1], in1=ACC,
                                                                  op0=ALU.mult, op1=ALU.add), i),
                     reads=zres + ["ACC", "wconv"], writes=["ACC"])
                P.op("dve", B(lambda e, i: e.tensor_tensor(
                    out=CAT[:, 12 + i, :].rearrange("p (b t) -> p b t", t=128), in0=ACC,
                    in1=CAT[:, 12 + i, :].rearrange("p (b t) -> p b t", t=128), op=ALU.mult), i),
                    reads=["ACC"], writes=[f"CATC{i}"])

            if KMIX < 3:
                P.op, P.dma = P_op_saved, P_dma_saved
            def vview(r, half):
                return ag_dstp[2 + half][r * 512:(r + 1) * 512, :].rearrange("r (two f) -> (r two) f", two=2)

            def load_head(h):
                b = h % 2
                grp = []
                for m in range(2):
                    for r in range(2):
                        row = h * 128 + m * 64
                        part, loc = row // 512, row % 512
                        row0 = r * 512 + loc
                        grp.append(P.dma("sp", f"D_kh{b}", B(lambda e, b, m, r, part, row0: e.dma_start(
                            out=KH[b][m][0:64, r * NTOK:(r + 1) * NTOK], in_=ag_dstp[part][row0:row0 + 64, :]),
                            b, m, r, part, row0),
                            reads=[f"agdst{part}"], writes=[f"KH{b}"]))
                P.alias_group(grp)
                grp = []
                for m in range(2):
                    row0 = h * 128 + m * 64
                    grp.append(P.dma("sp", f"D_qh{b}", B(lambda e, b, m, row0: e.dma_start(
                        out=QH[b][m][0:64, :], in_=qs[row0:row0 + 64, :]), b, m, row0),
                        reads=[f"qs{h}_{tg}" for tg in range(NTG)], writes=[f"QH{b}"]))
                P.alias_group(grp)
                grp = []
                for r in range(2):
                    for half in range(2):
                        grp.append(P.dma("sp", f"D_vh{b}", B(lambda e, b, r, h, half: e.dma_start(
                            out=VH[b][:, r * 16 + half * 8:r * 16 + half * 8 + 8, 0:128],
                            in_=vview(r, half)[:, h * 128:(h + 1) * 128].rearrange("(t p) f -> p t f", p=128)),
                            b, r, h, half),
                            reads=[f"agdst{2 + half}"], writes=[f"VH{b}"]))
                P.alias_group(grp)

            s_banks = [4, 5, 6, 7]
            cs = {"s": 0, "pt": 0, "sm": 0, "yb": 0, "unit": 0}
            tq = []
            TLAG = 10

            def flush_t(force=False):
                while tq and (force or cs["unit"] - tq[0][0] >= TLAG):
                    _, yb, h, t = tq.pop(0)
                    sb = s_banks[cs["s"] % 4]
                    cs["s"] += 1
                    P.op("pe", B(lambda e, yb, sb: e.transpose(out=banks[sb][:, 0:64].bitcast(BF16), in_=YB[:, yb, :],
                                                                identity=ident), yb, sb),
                         reads=[f"YB{yb}", "ident"], writes=[f"ps{sb}"], signal=True)
                    P.op("dve", B(lambda e, sb, h, t: e.tensor_copy(out=CAT[:, 4 + h, t * 128:(t + 1) * 128],
                                                                    in_=banks[sb][:, 0:64].bitcast(BF16)), sb, h, t),
                         reads=[f"ps{sb}"], writes=[f"CATB{h}_{t}"])

            def attn_head(h):
                b = h % 2
                slope = 2.0 ** (-(h + 1))
                for qg in range(4):
                    units = []
                    for tp in range(4 * qg + 4):
                        for r in range(2):
                            for m in range(2):
                                units.append((tp, r, m))
                    first_o = [True] * 4
                    pend = []

                    def qk(u):
                        tp, r, m = u
                        kt = r * 16 + tp
                        tmin = max(tp, 4 * qg)
                        n = (4 * qg + 4 - tmin) * 128
                        c0 = tmin * 128
                        sb = s_banks[cs["s"] % 4]
                        cs["s"] += 1
                        diag = tp >= 4 * qg
                        P.op("pe", B(lambda e, b, m, kt, c0, n, sb, diag: e.matmul(
                            banks[sb][:, 0:n], lhsT=KH[b][m][0:68, kt * 128:(kt + 1) * 128],
                            rhs=QH[b][m][0:68, c0:c0 + n], start=True, stop=(not diag)), b, m, kt, c0, n, sb, diag),
                            reads=[f"KH{b}", f"KHaug{b}", f"QH{b}", f"QHaug{b}"], writes=[f"ps{sb}"], signal=(not diag))
                        if diag:
                            mk = maskA if r == 0 else maskB
                            P.op("pe", B(lambda e, sb, mk: e.matmul(banks[sb][:, 0:128], lhsT=ident, rhs=mk,
                                                                     start=False, stop=True), sb, mk),
                                 reads=["ident", "maskA", "maskB"], writes=[f"ps{sb}"], signal=True)
                        pt = cs["pt"] % 6
                        cs["pt"] += 1
                        P.op("act", B(lambda e, pt, sb, n: e.activation(out=PT[:, pt, 0:n], in_=banks[sb][:, 0:n],
                                                                         func=AF.Exp, scale=float(slope)), pt, sb, n),
                             reads=[f"ps{sb}"], writes=[f"PT{pt}"])
                        return (u, pt, tmin, n)

                    def pv(info):
                        (tp, r, m), pt, tmin, n = info
                        kt = r * 16 + tp
                        cs["unit"] += 1
                        flush_t()
                        for t in range(tmin, 4 * qg + 4):
                            ob = t - 4 * qg
                            st = first_o[ob]
                            first_o[ob] = False
                            last = (tp == t and r == 1)
                            off = (t - tmin) * 128
                            P.op("pe", B(lambda e, ob, m, pt, off, b, kt, st, last: e.matmul(
                                banks[ob][:, m * 129:(m + 1) * 129], lhsT=PT[:, pt, off:off + 128],
                                rhs=VH[b][:, kt, 0:129], start=st, stop=last, skip_group_check=True),
                                ob, m, pt, off, b, kt, st, last),
                                reads=[f"PT{pt}", f"VH{b}", f"VHaug{b}"], writes=[f"ps{ob}"],
                                signal=(last or t == 4 * qg + 3))
                            if last and m == 1:
                                post(h, t, ob)

                    def post(h, t, ob):
                        sm = cs["sm"] % 2
                        cs["sm"] += 1
                        bk = banks[ob]
                        P.op("dve", B(lambda e, sm, bk: e.reciprocal(out=RL[:, sm, 0:2], in_=bk[:, 128:258:129]), sm, bk),
                             reads=[f"ps{ob}"], writes=[f"RL{sm}"])
                        P.op("dve", B(lambda e, sm: e.tensor_tensor(out=RL[:, sm, 2:3], in0=RL[:, sm, 1:2],
                                                                    in1=lamcol[:, 5:6], op=ALU.mult), sm),
                             reads=[f"RL{sm}", "lamcol"], writes=[f"RLb{sm}"])
                        P.op("dve", B(lambda e, sm, bk: e.tensor_scalar(out=T1[:, sm, :], in0=bk[:, 129:257],
                                                                        scalar1=RL[:, sm, 2:3], scalar2=None,
                                                                        op0=ALU.mult), sm, bk),
                             reads=[f"ps{ob}", f"RLb{sm}"], writes=[f"T1{sm}"])
                        P.op("dve", B(lambda e, sm, bk: e.scalar_tensor_tensor(out=OSB[:, sm, :], in0=bk[:, 0:128],
                                                                               scalar=RL[:, sm, 0:1], in1=T1[:, sm, :],
                                                                               op0=ALU.mult, op1=ALU.subtract), sm, bk),
                             reads=[f"ps{ob}", f"RL{sm}", f"T1{sm}"], writes=[f"OSB{sm}"])
                        P.op("dve", B(lambda e, sm: e.tensor_tensor(out=SQJ[:, sm, :], in0=OSB[:, sm, :],
                                                                    in1=OSB[:, sm, :], op=ALU.mult), sm),
                             reads=[f"OSB{sm}"], writes=[f"SQJ{sm}"])
                        P.op("dve", B(lambda e, sm: e.reduce_sum(out=SSQ[:, sm, 0:1], in_=SQJ[:, sm, :], axis=AX), sm),
                             reads=[f"SQJ{sm}"], writes=[f"SSQ{sm}"])
                        P.op("act", B(lambda e, sm: e.activation(out=SSQ[:, sm, 1:2], in_=SSQ[:, sm, 0:1], func=AF.Ln,
                                                                 scale=1.0 / 128.0, bias=epscol[:, 0:1]), sm),
                             reads=[f"SSQ{sm}", "epscol"], writes=[f"SSL{sm}"])
                        P.op("act", B(lambda e, sm: e.activation(out=SSQ[:, sm, 2:3], in_=SSQ[:, sm, 1:2], func=AF.Exp,
                                                                 scale=-0.5), sm),
                             reads=[f"SSL{sm}"], writes=[f"SSE{sm}"])
                        yb = cs["yb"] % 8
                        cs["yb"] += 1
                        P.op("dve", B(lambda e, sm, yb: e.scalar_tensor_tensor(out=YB[:, yb, :], in0=OSB[:, sm, :],
                                                                               scalar=SSQ[:, sm, 2:3], in1=gsub,
                                                                               op0=ALU.mult, op1=ALU.mult), sm, yb),
                             reads=[f"OSB{sm}", f"SSE{sm}", "gsub"], writes=[f"YB{yb}"])
                        tq.append((cs["unit"], yb, h, t))

                    SKEW = 3
                    for i, u in enumerate(units):
                        pend.append(qk(u))
                        if len(pend) > SKEW:
                            pv(pend.pop(0))
                    while pend:
                        pv(pend.pop(0))

            NHEADS = 8 if KMIX >= 9 else (KMIX - 3 if KMIX >= 4 else 0)
            if NHEADS:
                load_head(0)
            for h in range(NHEADS):
                if h + 1 < NHEADS:
                    load_head(h + 1)
                attn_head(h)
            flush_t(force=True)

            P.barrier()
            al.off = base
            XT6 = [al([128, NCH, TG], F32) for _ in range(2)]
            WS6 = [al([128, NCH, 256], BF16) for _ in range(3)]
            c6 = {"ws": 0, "bk": 0}
            for tg in range(NTG):
                t0 = tg * TG
                xb = tg % 2
                X6 = XT6[xb]
                grp = []
                for q in range(2):
                    grp.append(P.dma("sp", f"D_xt6{xb}", B(lambda e, q, t0, X6: e.dma_start(
                        out=X6[:, q * 8:(q + 1) * 8, :], in_=xsv[:, q * 8:(q + 1) * 8, t0:t0 + TG]), q, t0, X6),
                        reads=[f"xdram{tg}"], writes=[f"X6{xb}_{c}" for c in range(q * 8, q * 8 + 8)]))
                P.alias_group(grp)
                for cg in range(D // 256):
                    slot = c6["ws"] % 3
                    c6["ws"] += 1
                    P.dma("pool", f"D_ws{slot}", B(lambda e, slot, cg: e.dma_start(
                        out=WS6[slot][:, :, :], in_=wout[:, :, cg * 256:(cg + 1) * 256]), slot, cg), writes=[f"WS{slot}"])
                    for cc in range(2):
                        j = 2 * cg + cc
                        bank = c6["bk"] % 4
                        c6["bk"] += 1
                        for k in range(NCH):
                            P.op("pe", B(lambda e, slot, k, cc, bank, t0: e.matmul(
                                banks[bank][:, :], lhsT=WS6[slot][:, k, cc * 128:(cc + 1) * 128],
                                rhs=CAT[:, k, t0:t0 + TG], start=(k == 0), stop=(k == NCH - 1)), slot, k, cc, bank, t0),
                                reads=[f"WS{slot}"], writes=[f"ps{bank}"], signal=(k == NCH - 1))
                        P.op("dve", B(lambda e, j, bank, X6: e.tensor_tensor(out=X6[:, j, :], in0=banks[bank][:, :],
                                                                             in1=X6[:, j, :], op=ALU.add), j, bank, X6),
                             reads=[f"ps{bank}", f"X6{xb}_{j}"], writes=[f"X6{xb}_{j}"])
                grp = []
                for q in range(2):
                    grp.append(P.dma("sp", f"D_xst6{xb}", B(lambda e, q, t0, X6: e.dma_start(
                        out=xdv[:, q * 8:(q + 1) * 8, t0:t0 + TG], in_=X6[:, q * 8:(q + 1) * 8, :]), q, t0, X6),
                        reads=[f"X6{xb}_{c}" for c in range(q * 8, q * 8 + 8)], writes=[f"xdram{tg}"]))
                P.alias_group(grp)

        MIX["emit"] = emit_mix

        setup()
        P.barrier()
        cur_src = xT_in
        n_sub = len(sublayers)
        for i, (kind, l) in enumerate(sublayers):
            last = (i == n_sub - 1)
            if kind in ("ffn1", "ffn2"):
                emit_ffn(l, 1 if kind == "ffn1" else 2, cur_src, xs, final=(last and final_norm))
            else:
                MIX["emit"](l, cur_src, xs)
            cur_src = xs
            P.barrier()
        if not final_norm:
            al = Alloc(arena, 0, ARENA_BYTES)
            XT = al([128, NCH, 1024], F32)
            xsv = xs.rearrange("(c p) t -> p c t", p=128)
            outv = outT.rearrange("(c p) t -> p c t", p=128)
            for tg in range(2):
                t0 = tg * 1024
                P.dma("sp", "D_xt", (lambda t0: lambda e: e.dma_start(out=XT, in_=xsv[:, :, t0:t0 + 1024]))(t0),
                      reads=["xdram"], writes=["XTall"])
                P.dma("sp", "D_xst", (lambda t0: lambda e: e.dma_start(out=outv[:, :, t0:t0 + 1024], in_=XT))(t0),
                      reads=["XTall"], writes=["outdram"])
            P.barrier()

        P.barrier()
        P.finalize()

        sems = {}
        for k in P.sem_names:
            sems[k] = ctx.enter_context(nc.semaphore(k))
        block = ctx.enter_context(nc.Block())

        @block.tensor
        def _(e):
            _emit_engine(P, "pe", e, sems)

        @block.scalar
        def _(e):
            _emit_engine(P, "act", e, sems)

        @block.vector
        def _(e):
            _emit_engine(P, "dve", e, sems)

        @block.gpsimd
        def _(e):
            _emit_engine(P, "pool", e, sems)

        @block.sync
        def _(e):
            _emit_engine(P, "sp", e, sems)

    nc._planner_stats = {e: len(P.ops[e]) for e in P.ENGS}
    nc._sem_counts = dict(P.sem_count)
    return nc


def core_token_index(r):
    t = np.arange(NBLK)
    return ((2 * t[:, None] + r) * 128 + np.arange(128)[None, :]).reshape(-1)


def make_common_inputs(inputs, sublayers):
    f32 = np.float32
    com = {}
    g = np.zeros((13, D), f32)
    g[0:4] = np.asarray(inputs["g_ffn1"], f32)
    g[4:8] = np.asarray(inputs["g_mix"], f32)
    g[8:12] = np.asarray(inputs["g_ffn2"], f32)
    g[12] = np.asarray(inputs["g_final"], f32)
    com["gcols"] = np.ascontiguousarray(g.reshape(13, NCH, 128).transpose(2, 0, 1).reshape(128, 13 * NCH))
    if any(k == "mix" for k, _ in sublayers):
        bf = ml_dtypes.bfloat16
        com["ident"] = np.eye(128, dtype=f32).astype(bf)
        com["identf"] = np.eye(128, dtype=f32)
        ii = np.arange(128)
        com["tril01"] = (ii[:, None] <= ii[None, :]).astype(f32)
        n = np.arange(SEQ)
        rr, tt, i_ = n // NTOK, (n % NTOK) // 128, n % 128
        com["kaug"] = np.stack([np.ones(SEQ), np.ones(SEQ), 2 * tt + rr, i_]).astype(f32).astype(bf)
    for kind, l in sublayers:
        if kind == "mix":
            com[f"win_{l}"] = np.asarray(inputs["w_in"][l], f32)
            com[f"wout_{l}"] = np.asarray(inputs["w_out"][l], f32)
            com[f"gv_{l}"] = np.asarray(inputs["g_sga_v"][l], f32).reshape(1, 512)
            com[f"ws_{l}"] = np.asarray(inputs["w_sga_s"][l], f32)
            com[f"bs_{l}"] = np.asarray(inputs["b_sga_s"][l], f32).reshape(1, 512)
            com[f"lq_{l}"] = np.asarray(inputs["lambda_qk"][l], f32).reshape(1, 256)
            com[f"gsub_{l}"] = np.asarray(inputs["g_diff_sub"][l], f32).reshape(1, 128)
            com[f"wc_{l}"] = np.ascontiguousarray(
                np.asarray(inputs["w_conv"][l], f32).reshape(3, 4, 128).transpose(2, 0, 1).reshape(128, 12))
    for kind, l in sublayers:
        if kind == "ffn1":
            com[f"w1g_{l}"] = np.asarray(inputs["w_ffn1_gate"][l], f32)
            com[f"w1u_{l}"] = np.asarray(inputs["w_ffn1_up"][l], f32)
            com[f"w1d_{l}"] = np.asarray(inputs["w_ffn1_down"][l], f32)
        elif kind == "ffn2":
            com[f"w2g_{l}"] = np.asarray(inputs["w_ffn2_gate"][l], f32)
            com[f"w2u_{l}"] = np.asarray(inputs["w_ffn2_up"][l], f32)
            com[f"w2d_{l}"] = np.asarray(inputs["w_ffn2_down"][l], f32)
    return com


def core_constants(r):
    f32 = np.float32
    bf = ml_dtypes.bfloat16
    n = np.arange(NTOK)
    t, i_ = n // 128, n % 128
    qaug = np.stack([-128.0 * (2 * t + r), -1.0 * i_, np.full(NTOK, 128.0), np.ones(NTOK)]).astype(f32).astype(bf)
    ii = np.arange(128)
    tri = np.where(ii[:, None] > ii[None, :], -MASK_BIG, 0.0).astype(f32)
    allm = np.full((128, 128), -MASK_BIG, f32)
    zero = np.zeros((128, 128), f32)
    if r == 0:
        mA, mB = tri, allm
        sel = np.tile(np.array([[0.0, 1.0]], f32), (128, 1))
    else:
        mA, mB = zero, tri
        sel = np.tile(np.array([[1.0, 0.0]], f32), (128, 1))
    return {"qaug": qaug, "maskA": mA.astype(bf), "maskB": mB.astype(bf), "sel": np.ascontiguousarray(sel)}


_NC_CACHE = {}


def run_sublayers(inputs, sublayers, x_cores, final_norm):
    key = (tuple(sublayers), final_norm)
    if key not in _NC_CACHE:
        _NC_CACHE[key] = build_program(list(sublayers), final_norm=final_norm)
    nc = _NC_CACHE[key]
    com = make_common_inputs(inputs, sublayers)
    in_maps = []
    has_mix = any(k == "mix" for k, _ in sublayers)
    for c in range(8):
        m = dict(com)
        m["xT"] = x_cores[c]
        if has_mix:
            m.update(core_constants(c % 2))
        in_maps.append(m)
    if os.environ.get("KTRACE"):
        res = run_bass_kernel_spmd(nc, in_maps, core_ids=list(range(8)), trace=True)
        print("KTRACE exec_time_ns", res.exec_time_ns, flush=True)
    else:
        res = run_bass_kernel_spmd(nc, in_maps, core_ids=list(range(8)))
    return [np.asarray(res.results[c]["outT"]) for c in range(8)]


def shard_x(x):
    x = np.asarray(x, np.float32)
    outs = []
    for c in range(8):
        b, r = c // 2, c % 2
        idx = core_token_index(r)
        outs.append(np.ascontiguousarray(x[b][idx].T))
    return outs


def unshard_x(outs):
    y = np.zeros((4, SEQ, D), np.float32)
    for c in range(8):
        b, r = c // 2, c % 2
        idx = core_token_index(r)
        y[b][idx] = outs[c].T
    return y


def kernel(**inputs):
    subl = []
    for l in range(DEPTH):
        subl += [("ffn1", l), ("mix", l), ("ffn2", l)]
    xc = shard_x(inputs["x"])
    outs = run_sublayers(inputs, subl, xc, final_norm=True)
    return unshard_x(outs)
```

```python
import math
import os
import numpy as np
import ml_dtypes
import concourse.bass as bass
import concourse.mybir as mybir
from concourse.bass_utils import run_bass_kernel_spmd

F32 = mybir.dt.float32
BF16 = mybir.dt.bfloat16
AF = mybir.ActivationFunctionType
ALU = mybir.AluOpType

D = 2048
DFF = 5632
NCH = D // 128
NFF = DFF // 128
SEQ = 4096
NTOK = 2048
NBLK = 16
DEPTH = 4
INCOLS = 5632
EPS = 1e-6
SQRT_D = math.sqrt(float(D))
AG_ROWS = 2056
MASK_BIG = 131072.0
KIB = 1024
REPLICA_GROUPS = [[0, 1]] if os.environ.get('KSIM2') else [[0, 1], [2, 3], [4, 5], [6, 7]]


class Op:
    __slots__ = ("eng", "fn", "deps", "sem", "val", "signal", "idx", "inc")

    def __init__(self, eng, fn):
        self.eng = eng
        self.fn = fn
        self.deps = []
        self.sem = None
        self.val = None
        self.signal = True
        self.inc = 1


class Planner:
    ENGS = ("pe", "act", "dve", "pool", "sp")

    def __init__(self):
        self.ops = {e: [] for e in self.ENGS}
        self.last_writer = {}
        self.readers = {}
        self.sem_names = {}
        self.sem_count = {}
        self.last_on_sem = {}
        for e in ("pe", "act", "dve"):
            self.sem_names["E_" + e] = None

    def _collect(self, reads, writes):
        deps = {}

        def add(op):
            if op is None:
                return
            deps[id(op)] = op

        for r in reads:
            add(self.last_writer.get(r))
        for w in writes:
            add(self.last_writer.get(w))
            for op in self.readers.get(w, {}).values():
                add(op)
        return list(deps.values())

    def _register(self, op, reads, writes):
        for r in reads:
            self.readers.setdefault(r, {})[op.sem if op.sem else ("E", op.eng)] = op
        for w in writes:
            self.last_writer[w] = op
            self.readers[w] = {}

    def op(self, eng, fn, reads=(), writes=(), signal=True):
        o = Op(eng, fn)
        o.signal = signal
        o.sem = "E_" + eng
        deps = self._collect(reads, writes)
        if eng == "pe":
            deps = [d for d in deps if not (d.eng == "pe" and d.sem == "E_pe")]
        o.deps = deps
        self.ops[eng].append(o)
        self._register(o, reads, writes)
        self.last_on_sem[o.sem] = o
        return o

    def dma(self, queue, semkey, fn, reads=(), writes=(), extra_deps=()):
        o = Op(queue, fn)
        o.sem = semkey
        o.inc = 16
        self.sem_names.setdefault(semkey, None)
        self.sem_count[semkey] = self.sem_count.get(semkey, 0) + 16
        o.val = self.sem_count[semkey]
        o.deps = self._collect(reads, writes) + list(extra_deps)
        self.ops[queue].append(o)
        self._register(o, reads, writes)
        self.last_on_sem[semkey] = o
        return o

    def alias_group(self, ops):
        if not ops:
            return
        v = max(o.val for o in ops)
        ids = set(id(o) for o in ops)
        for o in ops:
            o.val = v
            o.deps = [d for d in o.deps if id(d) not in ids]

    def collective(self, semkey, fn, reads=(), writes=()):
        o = Op("pool", fn)
        o.sem = semkey
        o.inc = 1
        self.sem_names.setdefault(semkey, None)
        self.sem_count[semkey] = self.sem_count.get(semkey, 0) + 1
        o.val = self.sem_count[semkey]
        o.deps = self._collect(reads, writes)
        self.ops["pool"].append(o)
        self._register(o, reads, writes)
        self.last_on_sem[semkey] = o
        return o

    def barrier(self):
        lasts = []
        for k, o in self.last_on_sem.items():
            if o is not None:
                if o.sem.startswith("E_"):
                    o.signal = True
                lasts.append(o)
        for e in self.ENGS:
            b = Op(e, None)
            b.signal = False
            b.sem = None
            b.deps = [o for o in lasts if not (o.eng == e and e == "pe" and o.sem == "E_pe")]
            self.ops[e].append(b)
        self.last_writer = {}
        self.readers = {}

    def finalize(self):
        for e in ("pe", "act", "dve"):
            ops = [o for o in self.ops[e] if o.sem == "E_" + e]
            cnt = 0
            for o in ops:
                if o.signal:
                    cnt += 1
                    o.val = cnt
            nxt = None
            for o in reversed(ops):
                if o.signal:
                    nxt = o.val
                else:
                    assert nxt is not None, "trailing non-signalling op with dependants?"
                    o.val = nxt
            self.sem_count["E_" + e] = cnt

    def replay(self, nc, sems, engines):
        pass


def _emit_engine(planner, eng_name, e, sems):
    waited = {}
    for o in planner.ops[eng_name]:
        need = {}
        for d in o.deps:
            if d.val is None:
                continue
            if need.get(d.sem, 0) < d.val:
                need[d.sem] = d.val
        for s, v in need.items():
            if waited.get(s, 0) < v:
                e.wait_ge(sems[s], v)
                waited[s] = v
        if o.fn is None:
            continue
        ins = o.fn(e)
        if o.sem is not None and (o.signal or not o.sem.startswith("E_")):
            if o.sem.startswith("CC"):
                ins.then_inc(sems[o.sem])
            else:
                ins.then_inc(sems[o.sem], o.inc)


def lambda_init(l):
    return 0.8 - 0.6 * math.exp(-0.3 * l)


def build_program(sublayers, final_norm=True, x_in_name="xT", debug=False):
    nc = bass.Bass("TRN2", target_bir_lowering=False)
    P = Planner()

    def din(name, shape, dt=F32):
        return nc.dram_tensor(name, list(shape), dt, kind="ExternalInput").ap()

    layers = sorted(set(l for _, l in sublayers))
    kinds = set(k for k, _ in sublayers)
    xT_in = din(x_in_name, [D, NTOK])
    outT = nc.dram_tensor("outT", [D, NTOK], F32, kind="ExternalOutput").ap()
    gcols_d = din("gcols", [128, 13 * NCH])
    W = {}
    for l in layers:
        if ("ffn1", l) in sublayers:
            W[("g1", l)] = din(f"w1g_{l}", [D, DFF])
            W[("u1", l)] = din(f"w1u_{l}", [D, DFF])
            W[("d1", l)] = din(f"w1d_{l}", [DFF, D])
        if ("ffn2", l) in sublayers:
            W[("g2", l)] = din(f"w2g_{l}", [D, DFF])
            W[("u2", l)] = din(f"w2u_{l}", [D, DFF])
            W[("d2", l)] = din(f"w2d_{l}", [DFF, D])
        if ("mix", l) in sublayers:
            W[("in", l)] = din(f"win_{l}", [D, INCOLS])
            W[("out", l)] = din(f"wout_{l}", [D, D])
            W[("gv", l)] = din(f"gv_{l}", [1, 512])
            W[("ws", l)] = din(f"ws_{l}", [4, 128, 128])
            W[("bs", l)] = din(f"bs_{l}", [1, 512])
            W[("lq", l)] = din(f"lq_{l}", [1, 256])
            W[("gsub", l)] = din(f"gsub_{l}", [1, 128])
            W[("wc", l)] = din(f"wc_{l}", [128, 12])
    has_mix = "mix" in kinds
    if has_mix:
        ident_d = din("ident", [128, 128], BF16)
        identf_d = din("identf", [128, 128], F32)
        tril_d = din("tril01", [128, 128], F32)
        maskA_d = din("maskA", [128, 128], BF16)
        maskB_d = din("maskB", [128, 128], BF16)
        kaug_d = din("kaug", [4, SEQ], BF16)
        qaug_d = din("qaug", [4, NTOK], BF16)
        sel_d = din("sel", [128, 2])
    xs = nc.dram_tensor("xs", [D, NTOK], F32).ap()
    if has_mix:
        qs = nc.dram_tensor("qs", [1024, NTOK], BF16).ap()
        ag_src = nc.dram_tensor("ag_src", [AG_ROWS, NTOK], BF16).ap()
        AG_PARTS = [(0, 512), (512, 1024), (1024, 1536), (1536, 2048), (2048, 2056)]
        ag_dstp = [nc.dram_tensor(f"ag_dst{i}", [2 * (r1 - r0), NTOK], BF16).ap() for i, (r0, r1) in enumerate(AG_PARTS)]
    dbg = {}

    ARENA_BYTES = 190 * KIB
    PERS_BYTES = 15 * KIB

    import contextlib
    with contextlib.ExitStack() as ctx:
        arena = ctx.enter_context(nc.sbuf_tensor("arena", [128, ARENA_BYTES // 2], BF16))
        pers = ctx.enter_context(nc.sbuf_tensor("pers", [128, PERS_BYTES // 2], BF16))
        banks = [ctx.enter_context(nc.psum_tensor(f"bank{i}", [128, 512], F32)) for i in range(8)]

        def carve(base, off, shape, dt):
            n = int(np.prod(shape[1:]))
            esz = 4 if dt == F32 else 2
            assert off % 4 == 0
            a = base[:, off // 2: off // 2 + n * esz // 2]
            if dt == F32:
                a = a.bitcast(F32)
            if len(shape) == 3:
                a = a.rearrange("p (a b) -> p a b", a=shape[1])
            elif len(shape) == 4:
                a = a.rearrange("p (a b c) -> p a b c", a=shape[1], b=shape[2])
            return a

        class Alloc:
            def __init__(self, base, start, limit):
                self.base, self.off, self.limit = base, start, limit

            def __call__(self, shape, dt):
                n = int(np.prod(shape[1:])) * (4 if dt == F32 else 2)
                n = (n + 31) // 32 * 32
                a = carve(self.base, self.off, shape, dt)
                self.off += n
                assert self.off <= self.limit, (self.off, self.limit)
                return a

        pal = Alloc(pers, 0, PERS_BYTES)
        ones_bf = pal([128, 128], BF16)
        gcols = pal([128, 13, NCH], F32)
        epscol = pal([128, 8], F32)
        if has_mix:
            ident = pal([128, 128], BF16)
            identf = pal([128, 128], F32)
            tril = pal([128, 128], F32)
            maskA = pal([128, 128], BF16)
            maskB = pal([128, 128], BF16)
            sel = pal([128, 2], F32)
            gvb = pal([128, 512], F32)
            gsub = pal([128, 128], F32)
            wst = pal([128, 4, 128], BF16)
            bsrow = pal([128, 512], BF16)
            bsrow32 = pal([128, 512], F32)
            lamb = pal([128, 4, 64], F32)
            lamprod = pal([128, 4, 64], F32)
            lamcol = pal([128, 8], F32)
            wconv = pal([128, 3, 4], F32)
            wsnat = pal([128, 4, 128], F32)

        def setup():
            P.op("dve", lambda e: e.memset(ones_bf, 1.0), writes=["ones"])
            P.dma("sp", "D_const", lambda e: e.dma_start(out=gcols.rearrange("p a c -> p (a c)"), in_=gcols_d),
                  writes=["gcols"])
            P.op("dve", lambda e: e.memset(epscol, EPS), writes=["epscol"])
            if has_mix:
                for nm, t, d in (("ident", ident, ident_d), ("identf", identf, identf_d), ("tril", tril, tril_d),
                                 ("maskA", maskA, maskA_d), ("maskB", maskB, maskB_d), ("sel", sel, sel_d)):
                    P.dma("sp", "D_const", (lambda t, d: (lambda e: e.dma_start(out=t, in_=d)))(t, d), writes=[nm])

        def emit_norm(XT, HT, SQ, RSTD, T, gidx, ss_banks, xres, hres, inplace=False):
            nh = T // 512
            for c in range(NCH):
                sl = c % 2
                P.op("act", (lambda c, sl: lambda e: e.activation(out=SQ[:, sl, :], in_=XT[:, c, :], func=AF.Square))(c, sl),
                     reads=[f"{xres}{c}"], writes=[f"SQ{sl}"])
                for h in range(nh):
                    P.op("pe", (lambda c, sl, h: lambda e: e.matmul(banks[ss_banks[h]][:, :], lhsT=ones_bf,
                                                                    rhs=SQ[:, sl, h * 512:(h + 1) * 512],
                                                                    start=(c == 0), stop=(c == NCH - 1)))(c, sl, h),
                         reads=[f"SQ{sl}", "ones"], writes=[f"ps{ss_banks[h]}"], signal=(c == NCH - 1 or True))
            for h in range(nh):
                P.op("act", (lambda h: lambda e: e.activation(out=RSTD[:, h * 512:(h + 1) * 512],
                                                              in_=banks[ss_banks[h]][:, :], func=AF.Sqrt,
                                                              scale=1.0 / float(D), bias=epscol[:, 0:1]))(h),
                     reads=[f"ps{ss_banks[h]}", "epscol"], writes=[f"RSTD{h}"])
                P.op("dve", (lambda h: lambda e: e.reciprocal(out=RSTD[:, h * 512:(h + 1) * 512],
                                                              in_=RSTD[:, h * 512:(h + 1) * 512]))(h),
                     reads=[f"RSTD{h}"], writes=[f"RSTD{h}"])
            for c in range(NCH):
                dst = XT if inplace else HT
                P.op("dve", (lambda c, dst: lambda e: e.scalar_tensor_tensor(out=dst[:, c, :], in0=XT[:, c, :],
                                                                             scalar=gcols[:, gidx, c:c + 1],
                                                                             in1=RSTD[:, 0:T],
                                                                             op0=ALU.mult, op1=ALU.mult))(c, dst),
                     reads=[f"{xres}{c}", "gcols"] + [f"RSTD{h}" for h in range(nh)],
                     writes=[f"{xres if inplace else hres}{c}"])

        def emit_ffn(l, which, x_src, x_dst, final):
            TG = 1024
            NH = 2
            al = Alloc(arena, 0, ARENA_BYTES)
            XT = al([128, NCH, TG], F32)
            HT = al([128, NCH, TG], BF16)
            AT = al([128, 8, TG], BF16)
            SG = al([128, 2, 512], BF16)
            SQ = al([128, 2, TG], BF16)
            RSTD = al([128, TG], F32)
            WGU = [al([128, 2, NCH, 256], BF16) for _ in range(2)]
            WD = [al([128, 4, D], BF16) for _ in range(2)]
            wg, wu, wd = W[("g%d" % which, l)], W[("u%d" % which, l)], W[("d%d" % which, l)]
            wgv = wg.rearrange("(k p) n -> p k n", p=128)
            wuv = wu.rearrange("(k p) n -> p k n", p=128)
            wdv = wd.rearrange("(c p) n -> p c n", p=128)
            gidx = (0 if which == 1 else 2) * 4 + l
            xsv = x_src.rearrange("(c p) t -> p c t", p=128)
            xdv = x_dst.rearrange("(c p) t -> p c t", p=128)
            outv = outT.rearrange("(c p) t -> p c t", p=128)
            ps_gu = [(0, 1), (2, 3), (4, 5)]
            ps_y = [6, 7]
            gu_i = 0
            y_i = 0
            wgu_i = 0
            wd_i = 0
            sg_i = 0
            DBG = int(os.environ.get('KDBG', '9'))
            NTG_F = NTOK // TG

            def load_chunk(tg, c):
                t0 = tg * TG
                P.dma("sp", f"D_xt{c}", B(lambda e, c, t0: e.dma_start(out=XT[:, c, :], in_=xsv[:, c, t0:t0 + TG]), c, t0),
                      reads=[f"xd{tg}_{c}"], writes=[f"XT{c}"])

            def store_chunk(tg, c, dstv, dres):
                t0 = tg * TG
                P.dma("sp", f"D_xst{c}", B(lambda e, c, t0, dstv: e.dma_start(out=dstv[:, c, t0:t0 + TG], in_=XT[:, c, :]),
                                           c, t0, dstv),
                      reads=[f"XT{c}"], writes=[f"{dres}{tg}_{c}"])

            for tg in range(NTG_F):
                t0 = tg * TG
                if tg == 0:
                    for c in range(NCH):
                        load_chunk(0, c)
                if DBG >= 1:
                    emit_norm(XT, HT, SQ, RSTD, TG, gidx, [6, 7], "XT", "HT")
                if DBG >= 9:
                    supers = [[g] for g in range(NFF // 4 - 2)] + [[NFF // 4 - 2, NFF // 4 - 1]]
                elif DBG >= 2:
                    supers = [[0]]
                else:
                    supers = []
                for sgrp in supers:
                    for gi_, g in enumerate(sgrp):
                        for pair in range(2):
                            slot = wgu_i % 2
                            wgu_i += 1
                            ff0 = (g * 4 + pair * 2) * 128
                            grp = []
                            for gi, wv in enumerate((wgv, wuv)):
                                grp.append(P.dma("pool", f"D_wgu{slot}", B(lambda e, gi, wv, slot, ff0: e.dma_start(
                                    out=WGU[slot][:, gi, :, :], in_=wv[:, :, ff0:ff0 + 256]), gi, wv, slot, ff0),
                                    writes=[f"WGU{slot}"]))
                            P.alias_group(grp)
                            for cc in range(2):
                                c = gi_ * 4 + pair * 2 + cc
                                for h in range(NH):
                                    bg, bu = ps_gu[gu_i % 3]
                                    gu_i += 1
                                    for gi, bank in ((0, bg), (1, bu)):
                                        for k in range(NCH):
                                            P.op("pe", B(lambda e, slot, gi, k, cc, h, bank: e.matmul(
                                                banks[bank][:, :], lhsT=WGU[slot][:, gi, k, cc * 128:(cc + 1) * 128],
                                                rhs=HT[:, k, h * 512:(h + 1) * 512], start=(k == 0), stop=(k == NCH - 1)),
                                                slot, gi, k, cc, h, bank),
                                                reads=[f"WGU{slot}", f"HT{k}"], writes=[f"ps{bank}"], signal=(k == NCH - 1))
                                    s_ = sg_i % 2
                                    sg_i += 1
                                    P.op("act", B(lambda e, s_, bg: e.activation(out=SG[:, s_, :], in_=banks[bg][:, :],
                                                                                 func=AF.Silu), s_, bg),
                                         reads=[f"ps{bg}"], writes=[f"SG{s_}"])
                                    P.op("dve", B(lambda e, s_, bu, c, h: e.tensor_tensor(
                                        out=AT[:, c, h * 512:(h + 1) * 512], in0=SG[:, s_, :], in1=banks[bu][:, :],
                                        op=ALU.mult), s_, bu, c, h),
                                        reads=[f"SG{s_}", f"ps{bu}"], writes=[f"AT{c}_{h}"])
                    if DBG == 2:
                        continue
                    slots = []
                    for g in sgrp:
                        slot = wd_i % 2
                        wd_i += 1
                        slots.append(slot)
                        P.dma("pool", f"D_wd{slot}", B(lambda e, slot, g: e.dma_start(
                            out=WD[slot][:, :, :], in_=wdv[:, g * 4:(g + 1) * 4, :]), slot, g), writes=[f"WD{slot}"])
                    nmm = 4 * len(sgrp)
                    for j in range(NCH):
                        for h in range(NH):
                            bank = ps_y[y_i % 2]
                            y_i += 1
                            for idx in range(nmm):
                                gi_, c4 = idx // 4, idx % 4
                                slot = slots[gi_]
                                c = gi_ * 4 + c4
                                P.op("pe", B(lambda e, slot, c4, c, j, h, bank, idx: e.matmul(
                                    banks[bank][:, :], lhsT=WD[slot][:, c4, j * 128:(j + 1) * 128],
                                    rhs=AT[:, c, h * 512:(h + 1) * 512], start=(idx == 0), stop=(idx == nmm - 1)),
                                    slot, c4, c, j, h, bank, idx),
                                    reads=[f"WD{slot}", f"AT{c}_{h}"], writes=[f"ps{bank}"], signal=(idx == nmm - 1))
                            P.op("dve", B(lambda e, j, h, bank: e.scalar_tensor_tensor(
                                out=XT[:, j, h * 512:(h + 1) * 512], in0=banks[bank][:, :], scalar=0.5,
                                in1=XT[:, j, h * 512:(h + 1) * 512], op0=ALU.mult, op1=ALU.add), j, h, bank),
                                reads=[f"ps{bank}", f"XT{j}"], writes=[f"XT{j}"])
                if final:
                    emit_norm(XT, None, SQ, RSTD, TG, 12, [6, 7], "XT", None, inplace=True)
                    dstv, dres = outv, "od"
                else:
                    dstv, dres = xdv, "xd"
                LAGC = 3
                for c in range(NCH + LAGC):
                    if c < NCH:
                        store_chunk(tg, c, dstv, dres)
                    if tg + 1 < NTG_F and c >= LAGC:
                        load_chunk(tg + 1, c - LAGC)

        MIX = {}

        def B(f, *a):
            return lambda e: f(e, *a)

        AX = mybir.AxisListType.X

        def emit_mix(l, x_src, x_dst):
            TG = 512
            NTG = NTOK // TG
            al = Alloc(arena, 0, ARENA_BYTES)
            CAT = al([128, NCH, NTOK], BF16)
            ZT = al([128, 4, NBLK, 130], BF16)
            base = al.off
            win = W[("in", l)].rearrange("(k p) n -> p k n", p=128)
            wout = W[("out", l)].rearrange("(k p) n -> p k n", p=128)
            xsv = x_src.rearrange("(c p) t -> p c t", p=128)
            xdv = x_dst.rearrange("(c p) t -> p c t", p=128)
            linit = lambda_init(l)
            vsrc = ag_src[1024:2048, :].rearrange("r (two f) -> (r two) f", two=2)
            tsrc = ag_src[2048:2056, :].rearrange("r (a f) -> (r a) f", f=32).rearrange("(ch p) f -> p ch f", p=128)

            smalls = []

            def small(nm, t, d):
                smalls.append(P.dma("sp", "D_const", B(lambda e, t, d: e.dma_start(out=t, in_=d), t, d), writes=[nm]))
            small("gvb", gvb, W[("gv", l)].partition_broadcast(128))
            small("gsub", gsub, W[("gsub", l)].partition_broadcast(128))
            small("lamb", lamb.rearrange("p a b -> p (a b)"), W[("lq", l)].partition_broadcast(128))
            small("bsrow32", bsrow32[0:1, :], W[("bs", l)])
            small("wconv", wconv.rearrange("p a b -> p (a b)"), W[("wc", l)])
            small("wsnat", wsnat, W[("ws", l)].rearrange("g t s -> t g s"))
            P.alias_group(smalls)
            P.op("dve", lambda e: e.tensor_scalar(out=gsub, in0=gsub, scalar1=float(1.0 - linit), scalar2=None,
                                                  op0=ALU.mult), reads=["gsub"], writes=["gsub"])
            P.op("act", lambda e: e.activation(out=bsrow[0:1, :], in_=bsrow32[0:1, :], func=AF.Copy),
                 reads=["bsrow32"], writes=["bsrow"])
            P.op("dve", lambda e: e.tensor_tensor(out=lamprod[:, 0:2, :], in0=lamb[:, 0:4:2, :], in1=lamb[:, 1:4:2, :],
                                                  op=ALU.mult), reads=["lamb"], writes=["lamprod"])
            P.op("dve", lambda e: e.reduce_sum(out=lamcol[:, 0:2], in_=lamprod[:, 0:2, :], axis=AX),
                 reads=["lamprod"], writes=["lamcol"])
            P.op("act", lambda e: e.activation(out=lamcol[:, 2:4], in_=lamcol[:, 0:2], func=AF.Exp),
                 reads=["lamcol"], writes=["lamcol"])
            P.op("dve", lambda e: e.tensor_tensor(out=lamcol[:, 4:5], in0=lamcol[:, 2:3], in1=lamcol[:, 3:4],
                                                  op=ALU.subtract), reads=["lamcol"], writes=["lamcol"])
            P.op("dve", lambda e: e.tensor_scalar(out=lamcol[:, 5:6], in0=lamcol[:, 4:5], scalar1=float(linit),
                                                  scalar2=None, op0=ALU.add), reads=["lamcol"], writes=["lamcol"])
            for g in range(4):
                P.op("pe", B(lambda e, g: e.transpose(out=banks[6][:, g * 128:(g + 1) * 128], in_=wsnat[:, g, :],
                                                      identity=identf), g),
                     reads=["wsnat", "identf"], writes=["ps6"])
            for g in range(4):
                P.op("dve", B(lambda e, g: e.tensor_tensor(out=wst[:, g, :], in0=banks[6][:, g * 128:(g + 1) * 128],
                                                           in1=tril, op=ALU.mult), g),
                     reads=["ps6", "tril"], writes=["wst"])

            XT = al([128, NCH, TG], F32)
            HT = al([128, NCH, TG], BF16)
            WS = [al([128, NCH, 256], BF16) for _ in range(3)]
            UT = al([128, 4, TG], BF16)
            SQ = al([128, 2, TG], BF16)
            RSTD = al([128, TG], F32)
            QKST = al([128, 3, TG], BF16)
            CZ = al([128, 4, TG], BF16)
            V32 = al([128, 4, 512], F32)
            VN = al([128, 2, 512], BF16)
            VST = al([128, 1, 4, 1024], BF16)
            TAILS = al([128, 4, NBLK, 2], BF16)
            SSV = al([128, 16], F32)
            JUNK = al([128, 512], BF16)
            m1_end = al.off
            fm_banks = [0, 1, 2, 3]
            tm_banks = [4, 5]
            CG_ORDER = list(range(INCOLS // 256))
            KMIX = int(os.environ.get("KMIX", "9"))

            def issue_ag(parts, res):
                for pi in parts:
                    r0, r1 = AG_PARTS[pi]
                    P.collective("CC_ag", B(lambda e, pi, r0, r1: e.collective_compute(
                        "AllGather", ALU.bypass, replica_groups=REPLICA_GROUPS,
                        ins=[ag_src[r0:r1, :].opt()], outs=[ag_dstp[pi].opt()]), pi, r0, r1),
                        reads=list(res), writes=[f"agdst{pi}"])
            cnt = {"fm": 0, "tm": 0, "ws": 0, "st": 0, "vn": 0}
            agsrc_res = []
            for tg in range(NTG):
                t0 = tg * TG
                grp = []
                for q in range(2):
                    grp.append(P.dma("sp", "D_xt", B(lambda e, q, t0: e.dma_start(
                        out=XT[:, q * 8:(q + 1) * 8, :], in_=xsv[:, q * 8:(q + 1) * 8, t0:t0 + TG]), q, t0),
                        reads=["xdram"], writes=[f"XT{c}" for c in range(q * 8, q * 8 + 8)]))
                P.alias_group(grp)
                emit_norm(XT, HT, SQ, RSTD, TG, 4 + l, [7], "XT", "HT")
                for cg in CG_ORDER:
                    slot = cnt["ws"] % 3
                    cnt["ws"] += 1
                    P.dma("pool", f"D_ws{slot}", B(lambda e, slot, cg: e.dma_start(
                        out=WS[slot][:, :, :], in_=win[:, :, cg * 256:(cg + 1) * 256]), slot, cg), writes=[f"WS{slot}"])
                    ch0 = 2 * cg
                    is_tm = (4 <= ch0 < 8) or (24 <= ch0 < 32)
                    if not is_tm:
                        for cc in range(2):
                            ch = ch0 + cc
                            bank = fm_banks[cnt["fm"] % 4]
                            cnt["fm"] += 1
                            for k in range(NCH):
                                P.op("pe", B(lambda e, slot, k, cc, bank: e.matmul(
                                    banks[bank][:, :], lhsT=WS[slot][:, k, cc * 128:(cc + 1) * 128], rhs=HT[:, k, :],
                                    start=(k == 0), stop=(k == NCH - 1)), slot, k, cc, bank),
                                    reads=[f"WS{slot}", f"HT{k}"], writes=[f"ps{bank}"], signal=(k == NCH - 1))
                            if ch < 4:
                                P.op("act", B(lambda e, ch, bank: e.activation(out=UT[:, ch, :], in_=banks[bank][:, :],
                                                                                func=AF.Copy), ch, bank),
                                     reads=[f"ps{bank}"], writes=[f"UT{ch}"])
                            elif ch < 16:
                                h = ch - 8
                                s = cnt["st"] % 3
                                cnt["st"] += 1
                                P.op("act", B(lambda e, s, bank, h: e.activation(out=QKST[:, s, :], in_=banks[bank][:, :],
                                                                                  func=AF.Copy, scale=float(2.0 ** (h - 2))),
                                              s, bank, h),
                                     reads=[f"ps{bank}"], writes=[f"QKST{s}"])
                                P.dma("sp", f"D_qkst{s}", B(lambda e, s, h, t0: e.dma_start(
                                    out=qs[h * 128:(h + 1) * 128, t0:t0 + TG], in_=QKST[:, s, :]), s, h, t0),
                                    reads=[f"QKST{s}"], writes=[f"qs{h}_{tg}"])
                            elif ch < 24:
                                h = ch - 16
                                s = cnt["st"] % 3
                                cnt["st"] += 1
                                P.op("dve", B(lambda e, s, bank: e.tensor_copy(out=QKST[:, s, :], in_=banks[bank][:, :]),
                                              s, bank),
                                     reads=[f"ps{bank}"], writes=[f"QKST{s}"])
                                P.dma("sp", f"D_qkst{s}", B(lambda e, s, h, t0: e.dma_start(
                                    out=ag_src[512 * (h // 2) + 128 * (h % 2):512 * (h // 2) + 128 * (h % 2) + 128, t0:t0 + TG],
                                    in_=QKST[:, s, :]), s, h, t0),
                                    reads=[f"QKST{s}"], writes=[f"agk{h}_{tg}"])
                                agsrc_res.append(f"agk{h}_{tg}")
                            elif ch < 36:
                                i = ch - 32
                                P.op("act", B(lambda e, i, bank, t0: e.activation(out=CAT[:, 12 + i, t0:t0 + TG],
                                                                                   in_=banks[bank][:, :], func=AF.Copy),
                                              i, bank, t0),
                                     reads=[f"ps{bank}"], writes=[f"CAT{12 + i}_{tg}"])
                            elif ch < 40:
                                i = ch - 36
                                P.op("act", B(lambda e, i, bank: e.activation(out=CZ[:, i, :], in_=banks[bank][:, :],
                                                                               func=AF.Copy), i, bank),
                                     reads=[f"ps{bank}"], writes=[f"CZ{i}"])
                            else:
                                i = ch - 40
                                P.op("dve", B(lambda e, i, bank, tg: e.tensor_tensor(
                                    out=ZT[:, i, 4 * tg:4 * tg + 4, 2:130],
                                    in0=CZ[:, i, :].rearrange("p (a b) -> p a b", a=4),
                                    in1=banks[bank][:, :].rearrange("p (a b) -> p a b", a=4), op=ALU.mult), i, bank, tg),
                                    reads=[f"CZ{i}", f"ps{bank}"], writes=[f"ZT{i}_{tg}"])
                    else:
                        for tb in range(4):
                            bank = tm_banks[cnt["tm"] % 2]
                            cnt["tm"] += 1
                            for k in range(NCH):
                                P.op("pe", B(lambda e, slot, k, tb, bank: e.matmul(
                                    banks[bank][:, 0:256], lhsT=HT[:, k, tb * 128:(tb + 1) * 128], rhs=WS[slot][:, k, :],
                                    start=(k == 0), stop=(k == NCH - 1)), slot, k, tb, bank),
                                    reads=[f"WS{slot}", f"HT{k}"], writes=[f"ps{bank}"], signal=(k == NCH - 1))
                            if ch0 < 8:
                                c0 = (ch0 - 4) * 128
                                P.op("act", B(lambda e, tb, c0, bank: e.activation(out=V32[:, tb, c0:c0 + 256],
                                                                                    in_=banks[bank][:, 0:256],
                                                                                    func=AF.Copy), tb, c0, bank),
                                     reads=[f"ps{bank}"], writes=[f"V32_{tb}"])
                            else:
                                c0 = (ch0 - 24) * 128
                                vs = 0
                                P.op("dve", B(lambda e, vs, tb, c0, bank: e.tensor_copy(out=VST[:, vs, tb, c0:c0 + 256],
                                                                                         in_=banks[bank][:, 0:256]),
                                              vs, tb, c0, bank),
                                     reads=[f"ps{bank}"], writes=[f"VST{vs}_{tb}_{c0}"])
                        if ch0 == 6:
                            for tb in range(4):
                                blk = 4 * tg + tb
                                P.op("act", B(lambda e, tb: e.activation(out=JUNK, in_=V32[:, tb, :], func=AF.Square,
                                                                         accum_out=SSV[:, tb:tb + 1]), tb),
                                     reads=[f"V32_{tb}"], writes=["JUNK", f"SSV{tb}"])
                                P.op("act", B(lambda e, tb: e.activation(out=SSV[:, 4 + tb:5 + tb], in_=SSV[:, tb:tb + 1],
                                                                         func=AF.Sqrt, scale=1.0 / 512.0,
                                                                         bias=epscol[:, 0:1]), tb),
                                     reads=[f"SSV{tb}", "epscol"], writes=[f"SSR{tb}"])
                                P.op("dve", B(lambda e, tb: e.reciprocal(out=SSV[:, 8 + tb:9 + tb],
                                                                         in_=SSV[:, 4 + tb:5 + tb]), tb),
                                     reads=[f"SSR{tb}"], writes=[f"SSI{tb}"])
                                vn = cnt["vn"] % 2
                                cnt["vn"] += 1
                                P.op("dve", B(lambda e, tb, vn: e.scalar_tensor_tensor(
                                    out=VN[:, vn, :], in0=V32[:, tb, :], scalar=SSV[:, 8 + tb:9 + tb], in1=gvb,
                                    op0=ALU.mult, op1=ALU.mult), tb, vn),
                                    reads=[f"V32_{tb}", f"SSI{tb}", "gvb"], writes=[f"VN{vn}"])
                                for g in range(4):
                                    P.op("pe", B(lambda e, vn, g: e.matmul(
                                        banks[6][:, g * 128:(g + 1) * 128], lhsT=VN[:, vn, g * 128:(g + 1) * 128],
                                        rhs=wst[:, g, :], start=(g == 0), stop=False, skip_group_check=True), vn, g),
                                        reads=[f"VN{vn}", "wst"], writes=["ps6"], signal=False)
                                    P.op("pe", B(lambda e, g: e.matmul(
                                        banks[6][:, g * 128:(g + 1) * 128], lhsT=ones_bf[0:1, 0:128],
                                        rhs=bsrow[0:1, g * 128:(g + 1) * 128], start=False, stop=(g == 3),
                                        skip_group_check=True), g),
                                        reads=["bsrow", "ones"], writes=["ps6"], signal=(g == 3))
                                P.op("dve", B(lambda e, tb, blk: e.tensor_tensor(
                                    out=CAT[:, 0:4, blk * 128:(blk + 1) * 128], in0=UT[:, 0:4, tb * 128:(tb + 1) * 128],
                                    in1=banks[6][:, :].rearrange("p (a b) -> p a b", a=4), op=ALU.mult), tb, blk),
                                    reads=["ps6"] + [f"UT{i}" for i in range(4)], writes=[f"CATA_{blk}"])
                        if ch0 == 30:
                            vs = 0
                            grp = []
                            for pr in range(4):
                                grp.append(P.dma("sp", f"D_vst{vs}", B(lambda e, vs, t0, pr: e.dma_start(
                                    out=ag_src[512 * pr + 256:512 * pr + 512, :].rearrange("r (n f) -> (r n) f", f=256)[
                                        t0:t0 + TG, :].rearrange("(tb p) f -> p tb f", p=128),
                                    in_=VST[:, vs, :, 256 * pr:256 * pr + 256]), vs, t0, pr),
                                    reads=[f"VST{vs}_{tb}_{c0}" for tb in range(4) for c0 in (0, 256, 512, 768)],
                                    writes=[f"agv_{tg}_{pr}"]))
                            P.alias_group(grp)
                            agsrc_res.append(f"agv_{tg}")

                for i in range(4):
                    P.op("dve", B(lambda e, tg, i: e.tensor_copy(out=TAILS[:, i, 4 * tg:4 * tg + 4, :],
                                                                 in_=ZT[:, i, 4 * tg:4 * tg + 4, 128:130]), tg, i),
                         reads=[f"ZT{i}_{tg}"], writes=[f"TAILS{tg}_{i}"])
            P.dma("sp", "D_tail", lambda e: e.dma_start(out=tsrc, in_=TAILS.rearrange("p c b j -> p c (b j)")),
                  reads=[f"TAILS{tg}_{i}" for tg in range(NTG) for i in range(4)], writes=["agtail"])
            agsrc_res.append("agtail")

            P.barrier()
            issue_ag([0, 1, 2, 3, 4], [])
            al.off = base
            KH = [[al([128, SEQ], BF16) for m in range(2)] for b in range(2)]
            QH = [[al([128, NTOK], BF16) for m in range(2)] for b in range(2)]
            VH = [al([128, 32, 130], BF16) for b in range(2)]
            PT = al([128, 6, 512], BF16)
            RL = al([128, 2, 4], F32)
            T1 = al([128, 2, 128], F32)
            OSB = al([128, 2, 128], F32)
            SQJ = al([128, 2, 128], F32)
            SSQ = al([128, 2, 4], F32)
            YB = al([128, 8, 128], BF16)
            TA = al([128, 4, 32], BF16)
            TB = al([128, 4, 32], BF16)
            TMPH = al([128, 4, NBLK, 2], F32)
            ACC = al([128, NBLK, 128], F32)
            assert al.off <= ARENA_BYTES

            augs = []
            for b in range(2 if KMIX >= 3 else 0):
                for m in range(2):
                    augs.append(P.dma("sp", "D_const", B(lambda e, b, m: e.dma_start(out=KH[b][m][64:68, :], in_=kaug_d), b, m),
                                      writes=[f"KHaug{b}"]))
                    augs.append(P.dma("sp", "D_const", B(lambda e, b, m: e.dma_start(out=QH[b][m][64:68, :], in_=qaug_d), b, m),
                                      writes=[f"QHaug{b}"]))
            P.alias_group(augs)
            for b in range(2 if KMIX >= 3 else 0):
                P.op("dve", B(lambda e, b: e.memset(VH[b][:, :, 128:129], 1.0), b), writes=[f"VHaug{b}"])

            def tview(r):
                return ag_dstp[4][r * 8:(r + 1) * 8, :].rearrange(
                    "r (a f) -> (r a) f", f=32).rearrange("(ch p) f -> p ch f", p=128)
            if KMIX < 3:
                def _skip(*a, **k):
                    return None
                P_op_saved, P_dma_saved = P.op, P.dma
                P.op, P.dma = _skip, _skip
            P.dma("sp", "D_ta", lambda e: e.dma_start(out=TA, in_=tview(0)), reads=["agdst4"], writes=["TA"])
            P.dma("sp", "D_tb", lambda e: e.dma_start(out=TB, in_=tview(1)), reads=["agdst4"], writes=["TB"])
            TA4 = TA.rearrange("p c (b j) -> p c b j", j=2)
            TB4 = TB.rearrange("p c (b j) -> p c b j", j=2)
            P.op("dve", lambda e: e.tensor_scalar(out=TMPH.rearrange("p c b j -> p (c b j)"),
                                                  in0=TA.rearrange("p c f -> p (c f)"), scalar1=sel[:, 0:1],
                                                  scalar2=None, op0=ALU.mult),
                 reads=["TA", "sel"], writes=["TMPH"])
            for i in range(4):
                P.op("dve", B(lambda e, i: e.scalar_tensor_tensor(out=ZT[:, i, 1:NBLK, 0:2], in0=TB4[:, i, 0:NBLK - 1, :],
                                                                  scalar=sel[:, 1:2], in1=TMPH[:, i, 1:NBLK, :],
                                                                  op0=ALU.mult, op1=ALU.add), i),
                     reads=["TB", "TMPH", "sel"], writes=[f"ZTh{i}"])
            P.op("dve", lambda e: e.tensor_copy(out=ZT[:, :, 0, 0:2], in_=TMPH[:, :, 0, :]),
                 reads=["TMPH"], writes=["ZTh0"])
            zres = ["ZTh0"] + [f"ZTh{i}" for i in range(4)]
            for i in range(4):
                P.op("dve", B(lambda e, i: e.tensor_scalar(out=ACC, in0=ZT[:, i, :, 2:130], scalar1=wconv[:, 2, i:i + 1],
                                                           scalar2=None, op0=ALU.mult), i),
                     reads=zres + ["wconv"], writes=["ACC"])
                P.op("dve", B(lambda e, i: e.scalar_tensor_tensor(out=ACC, in0=ZT[:, i, :, 1:129],
                                                                  scalar=wconv[:, 1, i:i + 1], in1=ACC,
                                                                  op0=ALU.mult, op1=ALU.add), i),
                     reads=zres + ["ACC", "wconv"], writes=["ACC"])
                P.op("dve", B(lambda e, i: e.scalar_tensor_tensor(out=ACC, in0=ZT[:, i, :, 0:128],
                                                                  scalar=wconv[:, 0, i:i + 1], in1=ACC,
                                                                  op0=ALU.mult, op1=ALU.add), i),
                     reads=zres + ["ACC", "wconv"], writes=["ACC"])
                P.op("dve", B(lambda e, i: e.tensor_tensor(
                    out=CAT[:, 12 + i, :].rearrange("p (b t) -> p b t", t=128), in0=ACC,
                    in1=CAT[:, 12 + i, :].rearrange("p (b t) -> p b t", t=128), op=ALU.mult), i),
                    reads=["ACC"], writes=[f"CATC{i}"])

            if KMIX < 3:
                P.op, P.dma = P_op_saved, P_dma_saved
            def vview(r, pr):
                return ag_dstp[pr][r * 512 + 256:r * 512 + 512, :].rearrange("r (n f) -> (r n) f", f=256)

            def load_head(h):
                b = h % 2
                pr = h // 2
                grp = []
                for m in range(2):
                    for r in range(2):
                        row0 = r * 512 + 128 * (h % 2) + 64 * m
                        grp.append(P.dma("sp", f"D_kh{b}", B(lambda e, b, m, r, pr, row0: e.dma_start(
                            out=KH[b][m][0:64, r * NTOK:(r + 1) * NTOK], in_=ag_dstp[pr][row0:row0 + 64, :]),
                            b, m, r, pr, row0),
                            reads=[f"agdst{pr}"], writes=[f"KH{b}"]))
                P.alias_group(grp)
                grp = []
                for m in range(2):
                    row0 = h * 128 + m * 64
                    grp.append(P.dma("sp", f"D_qh{b}", B(lambda e, b, m, row0: e.dma_start(
                        out=QH[b][m][0:64, :], in_=qs[row0:row0 + 64, :]), b, m, row0),
                        reads=[f"qs{h}_{tg}" for tg in range(NTG)], writes=[f"QH{b}"]))
                P.alias_group(grp)
                grp = []
                for r in range(2):
                    for half in range(2):
                        grp.append(P.dma("sp", f"D_vh{b}", B(lambda e, b, r, h, half, pr: e.dma_start(
                            out=VH[b][:, r * 16 + half * 8:r * 16 + half * 8 + 8, 0:128],
                            in_=vview(r, pr)[half * 1024:(half + 1) * 1024, (h % 2) * 128:(h % 2) * 128 + 128].rearrange(
                                "(t p) f -> p t f", p=128)), b, r, h, half, pr),
                            reads=[f"agdst{pr}"], writes=[f"VH{b}"]))
                P.alias_group(grp)

            s_banks = [4, 5, 6, 7]
            cs = {"s": 0, "pt": 0, "sm": 0, "yb": 0, "unit": 0}
            tq = []
            TLAG = 10

            def flush_t(force=False):
                while tq and (force or cs["unit"] - tq[0][0] >= TLAG):
                    _, yb, h, t = tq.pop(0)
                    sb = s_banks[cs["s"] % 4]
                    cs["s"] += 1
                    P.op("pe", B(lambda e, yb, sb: e.transpose(out=banks[sb][:, 0:64].bitcast(BF16), in_=YB[:, yb, :],
                                                                identity=ident), yb, sb),
                         reads=[f"YB{yb}", "ident"], writes=[f"ps{sb}"], signal=True)
                    P.op("dve", B(lambda e, sb, h, t: e.tensor_copy(out=CAT[:, 4 + h, t * 128:(t + 1) * 128],
                                                                    in_=banks[sb][:, 0:64].bitcast(BF16)), sb, h, t),
                         reads=[f"ps{sb}"], writes=[f"CATB{h}_{t}"])

            def attn_head(h):
                b = h % 2
                slope = 2.0 ** (-(h + 1))
                for qg in range(4):
                    units = []
                    for tp in range(4 * qg + 4):
                        for r in range(2):
                            for m in range(2):
                                units.append((tp, r, m))
                    first_o = [True] * 4
                    pend = []

                    def qk(u):
                        tp, r, m = u
                        kt = r * 16 + tp
                        tmin = max(tp, 4 * qg)
                        n = (4 * qg + 4 - tmin) * 128
                        c0 = tmin * 128
                        sb = s_banks[cs["s"] % 4]
                        cs["s"] += 1
                        diag = tp >= 4 * qg
                        P.op("pe", B(lambda e, b, m, kt, c0, n, sb, diag: e.matmul(
                            banks[sb][:, 0:n], lhsT=KH[b][m][0:68, kt * 128:(kt + 1) * 128],
                            rhs=QH[b][m][0:68, c0:c0 + n], start=True, stop=(not diag)), b, m, kt, c0, n, sb, diag),
                            reads=[f"KH{b}", f"KHaug{b}", f"QH{b}", f"QHaug{b}"], writes=[f"ps{sb}"], signal=(not diag))
                        if diag:
                            mk = maskA if r == 0 else maskB
                            P.op("pe", B(lambda e, sb, mk: e.matmul(banks[sb][:, 0:128], lhsT=ident, rhs=mk,
                                                                     start=False, stop=True), sb, mk),
                                 reads=["ident", "maskA", "maskB"], writes=[f"ps{sb}"], signal=True)
                        pt = cs["pt"] % 6
                        cs["pt"] += 1
                        P.op("act", B(lambda e, pt, sb, n: e.activation(out=PT[:, pt, 0:n], in_=banks[sb][:, 0:n],
                                                                         func=AF.Exp, scale=float(slope)), pt, sb, n),
                             reads=[f"ps{sb}"], writes=[f"PT{pt}"])
                        return (u, pt, tmin, n)

                    def pv(info):
                        (tp, r, m), pt, tmin, n = info
                        kt = r * 16 + tp
                        cs["unit"] += 1
                        flush_t()
                        for t in range(tmin, 4 * qg + 4):
                            ob = t - 4 * qg
                            st = first_o[ob]
                            first_o[ob] = False
                            last = (tp == t and r == 1)
                            off = (t - tmin) * 128
                            P.op("pe", B(lambda e, ob, m, pt, off, b, kt, st, last: e.matmul(
                                banks[ob][:, m * 129:(m + 1) * 129], lhsT=PT[:, pt, off:off + 128],
                                rhs=VH[b][:, kt, 0:129], start=st, stop=last, skip_group_check=True),
                                ob, m, pt, off, b, kt, st, last),
                                reads=[f"PT{pt}", f"VH{b}", f"VHaug{b}"], writes=[f"ps{ob}"],
                                signal=(last or t == 4 * qg + 3))
                            if last and m == 1:
                                post(h, t, ob)

                    def post(h, t, ob):
                        sm = cs["sm"] % 2
                        cs["sm"] += 1
                        bk = banks[ob]
                        P.op("dve", B(lambda e, sm, bk: e.reciprocal(out=RL[:, sm, 0:2], in_=bk[:, 128:258:129]), sm, bk),
                             reads=[f"ps{ob}"], writes=[f"RL{sm}"])
                        P.op("dve", B(lambda e, sm: e.tensor_tensor(out=RL[:, sm, 2:3], in0=RL[:, sm, 1:2],
                                                                    in1=lamcol[:, 5:6], op=ALU.mult), sm),
                             reads=[f"RL{sm}", "lamcol"], writes=[f"RLb{sm}"])
                        P.op("dve", B(lambda e, sm, bk: e.tensor_scalar(out=T1[:, sm, :], in0=bk[:, 129:257],
                                                                        scalar1=RL[:, sm, 2:3], scalar2=None,
                                                                        op0=ALU.mult), sm, bk),
                             reads=[f"ps{ob}", f"RLb{sm}"], writes=[f"T1{sm}"])
                        P.op("dve", B(lambda e, sm, bk: e.scalar_tensor_tensor(out=OSB[:, sm, :], in0=bk[:, 0:128],
                                                                               scalar=RL[:, sm, 0:1], in1=T1[:, sm, :],
                                                                               op0=ALU.mult, op1=ALU.subtract), sm, bk),
                             reads=[f"ps{ob}", f"RL{sm}", f"T1{sm}"], writes=[f"OSB{sm}"])
                        P.op("dve", B(lambda e, sm: e.tensor_tensor(out=SQJ[:, sm, :], in0=OSB[:, sm, :],
                                                                    in1=OSB[:, sm, :], op=ALU.mult), sm),
                             reads=[f"OSB{sm}"], writes=[f"SQJ{sm}"])
                        P.op("dve", B(lambda e, sm: e.reduce_sum(out=SSQ[:, sm, 0:1], in_=SQJ[:, sm, :], axis=AX), sm),
                             reads=[f"SQJ{sm}"], writes=[f"SSQ{sm}"])
                        P.op("act", B(lambda e, sm: e.activation(out=SSQ[:, sm, 1:2], in_=SSQ[:, sm, 0:1], func=AF.Ln,
                                                                 scale=1.0 / 128.0, bias=epscol[:, 0:1]), sm),
                             reads=[f"SSQ{sm}", "epscol"], writes=[f"SSL{sm}"])
                        P.op("act", B(lambda e, sm: e.activation(out=SSQ[:, sm, 2:3], in_=SSQ[:, sm, 1:2], func=AF.Exp,
                                                                 scale=-0.5), sm),
                             reads=[f"SSL{sm}"], writes=[f"SSE{sm}"])
                        yb = cs["yb"] % 8
                        cs["yb"] += 1
                        P.op("dve", B(lambda e, sm, yb: e.scalar_tensor_tensor(out=YB[:, yb, :], in0=OSB[:, sm, :],
                                                                               scalar=SSQ[:, sm, 2:3], in1=gsub,
                                                                               op0=ALU.mult, op1=ALU.mult), sm, yb),
                             reads=[f"OSB{sm}", f"SSE{sm}", "gsub"], writes=[f"YB{yb}"])
                        tq.append((cs["unit"], yb, h, t))

                    SKEW = 3
                    for i, u in enumerate(units):
                        pend.append(qk(u))
                        if len(pend) > SKEW:
                            pv(pend.pop(0))
                    while pend:
                        pv(pend.pop(0))

            NHEADS = 8 if KMIX >= 9 else (KMIX - 3 if KMIX >= 4 else 0)
            if NHEADS:
                load_head(0)
            for h in range(NHEADS):
                if h + 1 < NHEADS:
                    load_head(h + 1)
                attn_head(h)
            flush_t(force=True)

            P.barrier()
            al.off = base
            XT6 = [al([128, NCH, TG], F32) for _ in range(2)]
            WS6 = [al([128, NCH, 256], BF16) for _ in range(3)]
            c6 = {"ws": 0, "bk": 0}
            for tg in range(NTG):
                t0 = tg * TG
                xb = tg % 2
                X6 = XT6[xb]
                grp = []
                for q in range(2):
                    grp.append(P.dma("sp", f"D_xt6{xb}", B(lambda e, q, t0, X6: e.dma_start(
                        out=X6[:, q * 8:(q + 1) * 8, :], in_=xsv[:, q * 8:(q + 1) * 8, t0:t0 + TG]), q, t0, X6),
                        reads=[f"xdram{tg}"], writes=[f"X6{xb}_{c}" for c in range(q * 8, q * 8 + 8)]))
                P.alias_group(grp)
                for cg in range(D // 256):
                    slot = c6["ws"] % 3
                    c6["ws"] += 1
                    P.dma("pool", f"D_ws{slot}", B(lambda e, slot, cg: e.dma_start(
                        out=WS6[slot][:, :, :], in_=wout[:, :, cg * 256:(cg + 1) * 256]), slot, cg), writes=[f"WS{slot}"])
                    for cc in range(2):
                        j = 2 * cg + cc
                        bank = c6["bk"] % 4
                        c6["bk"] += 1
                        for k in range(NCH):
                            P.op("pe", B(lambda e, slot, k, cc, bank, t0: e.matmul(
                                banks[bank][:, :], lhsT=WS6[slot][:, k, cc * 128:(cc + 1) * 128],
                                rhs=CAT[:, k, t0:t0 + TG], start=(k == 0), stop=(k == NCH - 1)), slot, k, cc, bank, t0),
                                reads=[f"WS{slot}"], writes=[f"ps{bank}"], signal=(k == NCH - 1))
                        P.op("dve", B(lambda e, j, bank, X6: e.tensor_tensor(out=X6[:, j, :], in0=banks[bank][:, :],
                                                                             in1=X6[:, j, :], op=ALU.add), j, bank, X6),
                             reads=[f"ps{bank}", f"X6{xb}_{j}"], writes=[f"X6{xb}_{j}"])
                grp = []
                for q in range(2):
                    grp.append(P.dma("sp", f"D_xst6{xb}", B(lambda e, q, t0, X6: e.dma_start(
                        out=xdv[:, q * 8:(q + 1) * 8, t0:t0 + TG], in_=X6[:, q * 8:(q + 1) * 8, :]), q, t0, X6),
                        reads=[f"X6{xb}_{c}" for c in range(q * 8, q * 8 + 8)], writes=[f"xdram{tg}"]))
                P.alias_group(grp)

        MIX["emit"] = emit_mix

        setup()
        P.barrier()
        cur_src = xT_in
        n_sub = len(sublayers)
        for i, (kind, l) in enumerate(sublayers):
            last = (i == n_sub - 1)
            if kind in ("ffn1", "ffn2"):
                emit_ffn(l, 1 if kind == "ffn1" else 2, cur_src, xs, final=(last and final_norm))
            else:
                MIX["emit"](l, cur_src, xs)
            cur_src = xs
            P.barrier()
        if not final_norm:
            al = Alloc(arena, 0, ARENA_BYTES)
            XT = al([128, NCH, 1024], F32)
            xsv = xs.rearrange("(c p) t -> p c t", p=128)
            outv = outT.rearrange("(c p) t -> p c t", p=128)
            for tg in range(2):
                t0 = tg * 1024
                P.dma("sp", "D_xt", (lambda t0: lambda e: e.dma_start(out=XT, in_=xsv[:, :, t0:t0 + 1024]))(t0),
                      reads=["xdram"], writes=["XTall"])
                P.dma("sp", "D_xst", (lambda t0: lambda e: e.dma_start(out=outv[:, :, t0:t0 + 1024], in_=XT))(t0),
                      reads=["XTall"], writes=["outdram"])
            P.barrier()

        P.barrier()
        P.finalize()

        sems = {}
        for k in P.sem_names:
            sems[k] = ctx.enter_context(nc.semaphore(k))
        block = ctx.enter_context(nc.Block())

        @block.tensor
        def _(e):
            _emit_engine(P, "pe", e, sems)

        @block.scalar
        def _(e):
            _emit_engine(P, "act", e, sems)

        @block.vector
        def _(e):
            _emit_engine(P, "dve", e, sems)

        @block.gpsimd
        def _(e):
            _emit_engine(P, "pool", e, sems)

        @block.sync
        def _(e):
            _emit_engine(P, "sp", e, sems)

    nc._planner_stats = {e: len(P.ops[e]) for e in P.ENGS}
    nc._sem_counts = dict(P.sem_count)
    return nc


def core_token_index(r):
    t = np.arange(NBLK)
    return ((2 * t[:, None] + r) * 128 + np.arange(128)[None, :]).reshape(-1)


def make_common_inputs(inputs, sublayers):
    f32 = np.float32
    com = {}
    g = np.zeros((13, D), f32)
    g[0:4] = np.asarray(inputs["g_ffn1"], f32)
    g[4:8] = np.asarray(inputs["g_mix"], f32)
    g[8:12] = np.asarray(inputs["g_ffn2"], f32)
    g[12] = np.asarray(inputs["g_final"], f32)
    com["gcols"] = np.ascontiguousarray(g.reshape(13, NCH, 128).transpose(2, 0, 1).reshape(128, 13 * NCH))
    if any(k == "mix" for k, _ in sublayers):
        bf = ml_dtypes.bfloat16
        com["ident"] = np.eye(128, dtype=f32).astype(bf)
        com["identf"] = np.eye(128, dtype=f32)
        ii = np.arange(128)
        com["tril01"] = (ii[:, None] <= ii[None, :]).astype(f32)
        n = np.arange(SEQ)
        rr, tt, i_ = n // NTOK, (n % NTOK) // 128, n % 128
        com["kaug"] = np.stack([np.ones(SEQ), np.ones(SEQ), 2 * tt + rr, i_]).astype(f32).astype(bf)
    for kind, l in sublayers:
        if kind == "mix":
            com[f"win_{l}"] = np.asarray(inputs["w_in"][l], f32)
            com[f"wout_{l}"] = np.asarray(inputs["w_out"][l], f32)
            com[f"gv_{l}"] = np.asarray(inputs["g_sga_v"][l], f32).reshape(1, 512)
            com[f"ws_{l}"] = np.asarray(inputs["w_sga_s"][l], f32)
            com[f"bs_{l}"] = np.asarray(inputs["b_sga_s"][l], f32).reshape(1, 512)
            com[f"lq_{l}"] = np.asarray(inputs["lambda_qk"][l], f32).reshape(1, 256)
            com[f"gsub_{l}"] = np.asarray(inputs["g_diff_sub"][l], f32).reshape(1, 128)
            com[f"wc_{l}"] = np.ascontiguousarray(
                np.asarray(inputs["w_conv"][l], f32).reshape(3, 4, 128).transpose(2, 0, 1).reshape(128, 12))
    for kind, l in sublayers:
        if kind == "ffn1":
            com[f"w1g_{l}"] = np.asarray(inputs["w_ffn1_gate"][l], f32)
            com[f"w1u_{l}"] = np.asarray(inputs["w_ffn1_up"][l], f32)
            com[f"w1d_{l}"] = np.asarray(inputs["w_ffn1_down"][l], f32)
        elif kind == "ffn2":
            com[f"w2g_{l}"] = np.asarray(inputs["w_ffn2_gate"][l], f32)
            com[f"w2u_{l}"] = np.asarray(inputs["w_ffn2_up"][l], f32)
            com[f"w2d_{l}"] = np.asarray(inputs["w_ffn2_down"][l], f32)
    return com


def core_constants(r):
    f32 = np.float32
    bf = ml_dtypes.bfloat16
    n = np.arange(NTOK)
    t, i_ = n // 128, n % 128
    qaug = np.stack([-128.0 * (2 * t + r), -1.0 * i_, np.full(NTOK, 128.0), np.ones(NTOK)]).astype(f32).astype(bf)
    ii = np.arange(128)
    tri = np.where(ii[:, None] > ii[None, :], -MASK_BIG, 0.0).astype(f32)
    allm = np.full((128, 128), -MASK_BIG, f32)
    zero = np.zeros((128, 128), f32)
    if r == 0:
        mA, mB = tri, allm
        sel = np.tile(np.array([[0.0, 1.0]], f32), (128, 1))
    else:
        mA, mB = zero, tri
        sel = np.tile(np.array([[1.0, 0.0]], f32), (128, 1))
    return {"qaug": qaug, "maskA": mA.astype(bf), "maskB": mB.astype(bf), "sel": np.ascontiguousarray(sel)}


_NC_CACHE = {}


def run_sublayers(inputs, sublayers, x_cores, final_norm):
    key = (tuple(sublayers), final_norm)
    if key not in _NC_CACHE:
        _NC_CACHE[key] = build_program(list(sublayers), final_norm=final_norm)
    nc = _NC_CACHE[key]
    com = make_common_inputs(inputs, sublayers)
    in_maps = []
    has_mix = any(k == "mix" for k, _ in sublayers)
    for c in range(8):
        m = dict(com)
        m["xT"] = x_cores[c]
        if has_mix:
            m.update(core_constants(c % 2))
        in_maps.append(m)
    if os.environ.get("KTRACE"):
        res = run_bass_kernel_spmd(nc, in_maps, core_ids=list(range(8)), trace=True)
        print("KTRACE exec_time_ns", res.exec_time_ns, flush=True)
    else:
        res = run_bass_kernel_spmd(nc, in_maps, core_ids=list(range(8)))
    return [np.asarray(res.results[c]["outT"]) for c in range(8)]


def shard_x(x):
    x = np.asarray(x, np.float32)
    outs = []
    for c in range(8):
        b, r = c // 2, c % 2
        idx = core_token_index(r)
        outs.append(np.ascontiguousarray(x[b][idx].T))
    return outs


def unshard_x(outs):
    y = np.zeros((4, SEQ, D), np.float32)
    for c in range(8):
        b, r = c // 2, c % 2
        idx = core_token_index(r)
        y[b][idx] = outs[c].T
    return y


def kernel(**inputs):
    subl = []
    for l in range(DEPTH):
        subl += [("ffn1", l), ("mix", l), ("ffn2", l)]
    xc = shard_x(inputs["x"])
    outs = run_sublayers(inputs, subl, xc, final_norm=True)
    return unshard_x(outs)
```

```python
import math
import os
import numpy as np
import ml_dtypes
import concourse.bass as bass
import concourse.mybir as mybir
from concourse.bass_utils import run_bass_kernel_spmd

F32 = mybir.dt.float32
BF16 = mybir.dt.bfloat16
AF = mybir.ActivationFunctionType
ALU = mybir.AluOpType

D = 2048
DFF = 5632
NCH = D // 128
NFF = DFF // 128
SEQ = 4096
NTOK = 2048
NBLK = 16
DEPTH = 4
INCOLS = 5632
EPS = 1e-6
SQRT_D = math.sqrt(float(D))
AG_ROWS = 2056
MASK_BIG = 131072.0
KIB = 1024
REPLICA_GROUPS = [[0, 1]] if os.environ.get('KSIM2') else [[0, 1], [2, 3], [4, 5], [6, 7]]


class Op:
    __slots__ = ("eng", "fn", "deps", "sem", "val", "signal", "idx", "inc")

    def __init__(self, eng, fn):
        self.eng = eng
        self.fn = fn
        self.deps = []
        self.sem = None
        self.val = None
        self.signal = True
        self.inc = 1


class Planner:
    ENGS = ("pe", "act", "dve", "pool", "sp")

    def __init__(self):
        self.ops = {e: [] for e in self.ENGS}
        self.last_writer = {}
        self.readers = {}
        self.sem_names = {}
        self.sem_count = {}
        self.last_on_sem = {}
        for e in ("pe", "act", "dve", "pool"):
            self.sem_names["E_" + e] = None

    def _collect(self, reads, writes):
        deps = {}

        def add(op):
            if op is None:
                return
            deps[id(op)] = op

        for r in reads:
            add(self.last_writer.get(r))
        for w in writes:
            add(self.last_writer.get(w))
            for op in self.readers.get(w, {}).values():
                add(op)
        return list(deps.values())

    def _register(self, op, reads, writes):
        for r in reads:
            self.readers.setdefault(r, {})[op.sem if op.sem else ("E", op.eng)] = op
        for w in writes:
            self.last_writer[w] = op
            self.readers[w] = {}

    def op(self, eng, fn, reads=(), writes=(), signal=True):
        o = Op(eng, fn)
        o.signal = signal
        o.sem = "E_" + eng
        deps = self._collect(reads, writes)
        if eng == "pe":
            deps = [d for d in deps if not (d.eng == "pe" and d.sem == "E_pe")]
        o.deps = deps
        self.ops[eng].append(o)
        self._register(o, reads, writes)
        self.last_on_sem[o.sem] = o
        return o

    def dma(self, queue, semkey, fn, reads=(), writes=(), extra_deps=()):
        o = Op(queue, fn)
        o.sem = semkey
        o.inc = 16
        self.sem_names.setdefault(semkey, None)
        self.sem_count[semkey] = self.sem_count.get(semkey, 0) + 16
        o.val = self.sem_count[semkey]
        o.deps = self._collect(reads, writes) + list(extra_deps)
        self.ops[queue].append(o)
        self._register(o, reads, writes)
        self.last_on_sem[semkey] = o
        return o

    def alias_group(self, ops):
        if not ops:
            return
        v = max(o.val for o in ops)
        ids = set(id(o) for o in ops)
        for o in ops:
            o.val = v
            o.deps = [d for d in o.deps if id(d) not in ids]

    def collective(self, semkey, fn, reads=(), writes=()):
        o = Op("pool", fn)
        o.sem = semkey
        o.inc = 1
        self.sem_names.setdefault(semkey, None)
        self.sem_count[semkey] = self.sem_count.get(semkey, 0) + 1
        o.val = self.sem_count[semkey]
        o.deps = self._collect(reads, writes)
        self.ops["pool"].append(o)
        self._register(o, reads, writes)
        self.last_on_sem[semkey] = o
        return o

    def barrier(self):
        lasts = []
        for k, o in self.last_on_sem.items():
            if o is not None:
                if o.sem.startswith("E_"):
                    o.signal = True
                lasts.append(o)
        for e in self.ENGS:
            b = Op(e, None)
            b.signal = False
            b.sem = None
            b.deps = [o for o in lasts if not (o.eng == e and e == "pe" and o.sem == "E_pe")]
            self.ops[e].append(b)
        self.last_writer = {}
        self.readers = {}

    def finalize(self):
        for e in ("pe", "act", "dve", "pool"):
            ops = [o for o in self.ops[e] if o.sem == "E_" + e]
            cnt = 0
            for o in ops:
                if o.signal:
                    cnt += 1
                    o.val = cnt
            nxt = None
            for o in reversed(ops):
                if o.signal:
                    nxt = o.val
                else:
                    assert nxt is not None, "trailing non-signalling op with dependants?"
                    o.val = nxt
            self.sem_count["E_" + e] = cnt

    def replay(self, nc, sems, engines):
        pass


def _emit_engine(planner, eng_name, e, sems):
    waited = {}
    for o in planner.ops[eng_name]:
        need = {}
        for d in o.deps:
            if d.val is None:
                continue
            if need.get(d.sem, 0) < d.val:
                need[d.sem] = d.val
        for s, v in need.items():
            if waited.get(s, 0) < v:
                e.wait_ge(sems[s], v)
                waited[s] = v
        if o.fn is None:
            continue
        ins = o.fn(e)
        if o.sem is not None and (o.signal or not o.sem.startswith("E_")):
            if o.sem.startswith("CC"):
                ins.then_inc(sems[o.sem])
            else:
                ins.then_inc(sems[o.sem], o.inc)


def lambda_init(l):
    return 0.8 - 0.6 * math.exp(-0.3 * l)


def build_program(sublayers, final_norm=True, x_in_name="xT", debug=False):
    nc = bass.Bass("TRN2", target_bir_lowering=False)
    P = Planner()

    def din(name, shape, dt=F32):
        return nc.dram_tensor(name, list(shape), dt, kind="ExternalInput").ap()

    layers = sorted(set(l for _, l in sublayers))
    kinds = set(k for k, _ in sublayers)
    xT_in = din(x_in_name, [D, NTOK])
    outT = nc.dram_tensor("outT", [D, NTOK], F32, kind="ExternalOutput").ap()
    gcols_d = din("gcols", [128, 13 * NCH])
    W = {}
    for l in layers:
        if ("ffn1", l) in sublayers:
            W[("g1", l)] = din(f"w1g_{l}", [D, DFF])
            W[("u1", l)] = din(f"w1u_{l}", [D, DFF])
            W[("d1", l)] = din(f"w1d_{l}", [DFF, D])
        if ("ffn2", l) in sublayers:
            W[("g2", l)] = din(f"w2g_{l}", [D, DFF])
            W[("u2", l)] = din(f"w2u_{l}", [D, DFF])
            W[("d2", l)] = din(f"w2d_{l}", [DFF, D])
        if ("mix", l) in sublayers:
            W[("in", l)] = din(f"win_{l}", [D, INCOLS])
            W[("out", l)] = din(f"wout_{l}", [D, D])
            W[("gv", l)] = din(f"gv_{l}", [1, 512])
            W[("ws", l)] = din(f"ws_{l}", [4, 128, 128])
            W[("bs", l)] = din(f"bs_{l}", [1, 512])
            W[("lq", l)] = din(f"lq_{l}", [1, 256])
            W[("gsub", l)] = din(f"gsub_{l}", [1, 128])
            W[("wc", l)] = din(f"wc_{l}", [128, 12])
    has_mix = "mix" in kinds
    if has_mix:
        ident_d = din("ident", [128, 128], BF16)
        identf_d = din("identf", [128, 128], F32)
        tril_d = din("tril01", [128, 128], F32)
        maskA_d = din("maskA", [128, 128], BF16)
        maskB_d = din("maskB", [128, 128], BF16)
        kaug_d = din("kaug", [4, SEQ], BF16)
        qaug_d = din("qaug", [4, NTOK], BF16)
        sel_d = din("sel", [128, 2])
    xs = nc.dram_tensor("xs", [D, NTOK], F32).ap()
    if has_mix:
        qs = nc.dram_tensor("qs", [1024, NTOK], BF16).ap()
        winb = nc.dram_tensor("winb", [D, INCOLS], BF16).ap()
        woutb = nc.dram_tensor("woutb", [D, D], BF16).ap()
        ag_src = nc.dram_tensor("ag_src", [AG_ROWS, NTOK], BF16).ap()
        AG_PARTS = [(0, 512), (512, 1024), (1024, 1536), (1536, 2048), (2048, 2056)]
        ag_dstp = [nc.dram_tensor(f"ag_dst{i}", [2 * (r1 - r0), NTOK], BF16).ap() for i, (r0, r1) in enumerate(AG_PARTS)]
    dbg = {}

    ARENA_BYTES = 190 * KIB
    PERS_BYTES = 15 * KIB

    import contextlib
    with contextlib.ExitStack() as ctx:
        arena = ctx.enter_context(nc.sbuf_tensor("arena", [128, ARENA_BYTES // 2], BF16))
        pers = ctx.enter_context(nc.sbuf_tensor("pers", [128, PERS_BYTES // 2], BF16))
        banks = [ctx.enter_context(nc.psum_tensor(f"bank{i}", [128, 512], F32)) for i in range(8)]

        def carve(base, off, shape, dt):
            n = int(np.prod(shape[1:]))
            esz = 4 if dt == F32 else 2
            assert off % 4 == 0
            a = base[:, off // 2: off // 2 + n * esz // 2]
            if dt == F32:
                a = a.bitcast(F32)
            if len(shape) == 3:
                a = a.rearrange("p (a b) -> p a b", a=shape[1])
            elif len(shape) == 4:
                a = a.rearrange("p (a b c) -> p a b c", a=shape[1], b=shape[2])
            return a

        class Alloc:
            def __init__(self, base, start, limit):
                self.base, self.off, self.limit = base, start, limit

            def __call__(self, shape, dt):
                n = int(np.prod(shape[1:])) * (4 if dt == F32 else 2)
                n = (n + 31) // 32 * 32
                a = carve(self.base, self.off, shape, dt)
                self.off += n
                assert self.off <= self.limit, (self.off, self.limit)
                return a

        pal = Alloc(pers, 0, PERS_BYTES)
        ones_bf = pal([128, 128], BF16)
        gcols = pal([128, 13, NCH], F32)
        epscol = pal([128, 8], F32)
        if has_mix:
            ident = pal([128, 128], BF16)
            identf = pal([128, 128], F32)
            tril = pal([128, 128], F32)
            maskA = pal([128, 128], BF16)
            maskB = pal([128, 128], BF16)
            sel = pal([128, 2], F32)
            gvb = pal([128, 512], F32)
            gsub = pal([128, 128], F32)
            wst = pal([128, 4, 128], BF16)
            bsrow = pal([128, 512], BF16)
            bsrow32 = pal([128, 512], F32)
            lamb = pal([128, 4, 64], F32)
            lamprod = pal([128, 4, 64], F32)
            lamcol = pal([128, 8], F32)
            wconv = pal([128, 3, 4], F32)
            wsnat = pal([128, 4, 128], F32)

        def setup():
            P.op("dve", lambda e: e.memset(ones_bf, 1.0), writes=["ones"])
            P.dma("sp", "D_const", lambda e: e.dma_start(out=gcols.rearrange("p a c -> p (a c)"), in_=gcols_d),
                  writes=["gcols"])
            P.op("dve", lambda e: e.memset(epscol, EPS), writes=["epscol"])
            if has_mix:
                for nm, t, d in (("ident", ident, ident_d), ("identf", identf, identf_d), ("tril", tril, tril_d),
                                 ("maskA", maskA, maskA_d), ("maskB", maskB, maskB_d), ("sel", sel, sel_d)):
                    P.dma("sp", "D_const", (lambda t, d: (lambda e: e.dma_start(out=t, in_=d)))(t, d), writes=[nm])

        def emit_norm(XT, HT, SQ, RSTD, T, gidx, ss_banks, xres, hres, inplace=False):
            nh = T // 512
            for c in range(NCH):
                sl = c % 2
                P.op("act", (lambda c, sl: lambda e: e.activation(out=SQ[:, sl, :], in_=XT[:, c, :], func=AF.Square))(c, sl),
                     reads=[f"{xres}{c}"], writes=[f"SQ{sl}"])
                for h in range(nh):
                    P.op("pe", (lambda c, sl, h: lambda e: e.matmul(banks[ss_banks[h]][:, :], lhsT=ones_bf,
                                                                    rhs=SQ[:, sl, h * 512:(h + 1) * 512],
                                                                    start=(c == 0), stop=(c == NCH - 1)))(c, sl, h),
                         reads=[f"SQ{sl}", "ones"], writes=[f"ps{ss_banks[h]}"], signal=(c == NCH - 1 or True))
            for h in range(nh):
                P.op("act", (lambda h: lambda e: e.activation(out=RSTD[:, h * 512:(h + 1) * 512],
                                                              in_=banks[ss_banks[h]][:, :], func=AF.Sqrt,
                                                              scale=1.0 / float(D), bias=epscol[:, 0:1]))(h),
                     reads=[f"ps{ss_banks[h]}", "epscol"], writes=[f"RSTD{h}"])
                P.op("dve", (lambda h: lambda e: e.reciprocal(out=RSTD[:, h * 512:(h + 1) * 512],
                                                              in_=RSTD[:, h * 512:(h + 1) * 512]))(h),
                     reads=[f"RSTD{h}"], writes=[f"RSTD{h}"])
            for c in range(NCH):
                dst = XT if inplace else HT
                P.op("dve", (lambda c, dst: lambda e: e.scalar_tensor_tensor(out=dst[:, c, :], in0=XT[:, c, :],
                                                                             scalar=gcols[:, gidx, c:c + 1],
                                                                             in1=RSTD[:, 0:T],
                                                                             op0=ALU.mult, op1=ALU.mult))(c, dst),
                     reads=[f"{xres}{c}", "gcols"] + [f"RSTD{h}" for h in range(nh)],
                     writes=[f"{xres if inplace else hres}{c}"])

        def emit_ffn(l, which, x_src, x_dst, final):
            TG = 1024
            NH = 2
            al = Alloc(arena, 0, ARENA_BYTES)
            XT = al([128, NCH, TG], F32)
            HT = al([128, NCH, TG], BF16)
            AT = al([128, 8, TG], BF16)
            SG = al([128, 2, 512], BF16)
            SQ = al([128, 2, TG], BF16)
            RSTD = al([128, TG], F32)
            WGU = [al([128, 2, NCH, 256], BF16) for _ in range(2)]
            WD = [al([128, 4, D], BF16) for _ in range(2)]
            wg, wu, wd = W[("g%d" % which, l)], W[("u%d" % which, l)], W[("d%d" % which, l)]
            wgv = wg.rearrange("(k p) n -> p k n", p=128)
            wuv = wu.rearrange("(k p) n -> p k n", p=128)
            wdv = wd.rearrange("(c p) n -> p c n", p=128)
            gidx = (0 if which == 1 else 2) * 4 + l
            xsv = x_src.rearrange("(c p) t -> p c t", p=128)
            xdv = x_dst.rearrange("(c p) t -> p c t", p=128)
            outv = outT.rearrange("(c p) t -> p c t", p=128)
            ps_gu = [(0, 1), (2, 3), (4, 5)]
            ps_y = [6, 7]
            gu_i = 0
            y_i = 0
            wgu_i = 0
            wd_i = 0
            sg_i = 0
            DBG = int(os.environ.get('KDBG', '9'))
            NTG_F = NTOK // TG
            cvt = []
            if which == 1 and ("mix", l) in sublayers and DBG >= 9:
                for cg in range(INCOLS // 256):
                    cvt.append((W[("in", l)][:, cg * 256:(cg + 1) * 256], winb[:, cg * 256:(cg + 1) * 256], "winb"))
                for cg in range(D // 256):
                    cvt.append((W[("out", l)][:, cg * 256:(cg + 1) * 256], woutb[:, cg * 256:(cg + 1) * 256], "woutb"))
            cvt_ops = []

            def issue_cvt(n):
                for _ in range(n):
                    if cvt:
                        src, dst, res = cvt.pop(0)
                        cvt_ops.append(P.dma("pool", "D_cvt", B(lambda e, src, dst: e.dma_start(out=dst, in_=src), src, dst),
                                             writes=[f"{res}_{len(cvt)}"]))

            def load_chunk(tg, c):
                t0 = tg * TG
                P.dma("sp", f"D_xt{c}", B(lambda e, c, t0: e.dma_start(out=XT[:, c, :], in_=xsv[:, c, t0:t0 + TG]), c, t0),
                      reads=[f"xd{tg}_{c}"], writes=[f"XT{c}"])

            def store_chunk(tg, c, dstv, dres):
                t0 = tg * TG
                P.dma("sp", f"D_xst{c}", B(lambda e, c, t0, dstv: e.dma_start(out=dstv[:, c, t0:t0 + TG], in_=XT[:, c, :]),
                                           c, t0, dstv),
                      reads=[f"XT{c}"], writes=[f"{dres}{tg}_{c}"])

            for tg in range(NTG_F):
                t0 = tg * TG
                if tg == 0:
                    for c in range(NCH):
                        load_chunk(0, c)
                if DBG >= 1:
                    emit_norm(XT, HT, SQ, RSTD, TG, gidx, [6, 7], "XT", "HT")
                if DBG >= 9:
                    supers = [[g] for g in range(NFF // 4 - 2)] + [[NFF // 4 - 2, NFF // 4 - 1]]
                elif DBG >= 2:
                    supers = [[0]]
                else:
                    supers = []
                for sgrp in supers:
                    issue_cvt(2)
                    for gi_, g in enumerate(sgrp):
                        for pair in range(2):
                            slot = wgu_i % 2
                            wgu_i += 1
                            ff0 = (g * 4 + pair * 2) * 128
                            grp = []
                            for gi, wv in enumerate((wgv, wuv)):
                                grp.append(P.dma("pool", f"D_wgu{slot}", B(lambda e, gi, wv, slot, ff0: e.dma_start(
                                    out=WGU[slot][:, gi, :, :], in_=wv[:, :, ff0:ff0 + 256]), gi, wv, slot, ff0),
                                    writes=[f"WGU{slot}"]))
                            P.alias_group(grp)
                            for cc in range(2):
                                c = gi_ * 4 + pair * 2 + cc
                                for h in range(NH):
                                    bg, bu = ps_gu[gu_i % 3]
                                    gu_i += 1
                                    for gi, bank in ((0, bg), (1, bu)):
                                        for k in range(NCH):
                                            P.op("pe", B(lambda e, slot, gi, k, cc, h, bank: e.matmul(
                                                banks[bank][:, :], lhsT=WGU[slot][:, gi, k, cc * 128:(cc + 1) * 128],
                                                rhs=HT[:, k, h * 512:(h + 1) * 512], start=(k == 0), stop=(k == NCH - 1)),
                                                slot, gi, k, cc, h, bank),
                                                reads=[f"WGU{slot}", f"HT{k}"], writes=[f"ps{bank}"], signal=(k == NCH - 1))
                                    s_ = sg_i % 2
                                    sg_i += 1
                                    P.op("act", B(lambda e, s_, bg: e.activation(out=SG[:, s_, :], in_=banks[bg][:, :],
                                                                                 func=AF.Silu), s_, bg),
                                         reads=[f"ps{bg}"], writes=[f"SG{s_}"])
                                    P.op("dve", B(lambda e, s_, bu, c, h: e.tensor_tensor(
                                        out=AT[:, c, h * 512:(h + 1) * 512], in0=SG[:, s_, :], in1=banks[bu][:, :],
                                        op=ALU.mult), s_, bu, c, h),
                                        reads=[f"SG{s_}", f"ps{bu}"], writes=[f"AT{c}_{h}"])
                    if DBG == 2:
                        continue
                    slots = []
                    for g in sgrp:
                        slot = wd_i % 2
                        wd_i += 1
                        slots.append(slot)
                        P.dma("pool", f"D_wd{slot}", B(lambda e, slot, g: e.dma_start(
                            out=WD[slot][:, :, :], in_=wdv[:, g * 4:(g + 1) * 4, :]), slot, g), writes=[f"WD{slot}"])
                    nmm = 4 * len(sgrp)
                    for j in range(NCH):
                        for h in range(NH):
                            bank = ps_y[y_i % 2]
                            y_i += 1
                            for idx in range(nmm):
                                gi_, c4 = idx // 4, idx % 4
                                slot = slots[gi_]
                                c = gi_ * 4 + c4
                                P.op("pe", B(lambda e, slot, c4, c, j, h, bank, idx: e.matmul(
                                    banks[bank][:, :], lhsT=WD[slot][:, c4, j * 128:(j + 1) * 128],
                                    rhs=AT[:, c, h * 512:(h + 1) * 512], start=(idx == 0), stop=(idx == nmm - 1)),
                                    slot, c4, c, j, h, bank, idx),
                                    reads=[f"WD{slot}", f"AT{c}_{h}"], writes=[f"ps{bank}"], signal=(idx == nmm - 1))
                            P.op("dve", B(lambda e, j, h, bank: e.scalar_tensor_tensor(
                                out=XT[:, j, h * 512:(h + 1) * 512], in0=banks[bank][:, :], scalar=0.5,
                                in1=XT[:, j, h * 512:(h + 1) * 512], op0=ALU.mult, op1=ALU.add), j, h, bank),
                                reads=[f"ps{bank}", f"XT{j}"], writes=[f"XT{j}"])
                if final:
                    emit_norm(XT, None, SQ, RSTD, TG, 12, [6, 7], "XT", None, inplace=True)
                    dstv, dres = outv, "od"
                else:
                    dstv, dres = xdv, "xd"
                LAGC = 3
                for c in range(NCH + LAGC):
                    if c < NCH:
                        store_chunk(tg, c, dstv, dres)
                    if tg + 1 < NTG_F and c >= LAGC:
                        load_chunk(tg + 1, c - LAGC)
            issue_cvt(len(cvt))
            P.alias_group(cvt_ops)
            CVT_DONE[l] = cvt_ops[-1] if cvt_ops else None

        MIX = {}
        CVT_DONE = {}

        def B(f, *a):
            return lambda e: f(e, *a)

        AX = mybir.AxisListType.X

        def emit_mix(l, x_src, x_dst):
            TG = 512
            NTG = NTOK // TG
            al = Alloc(arena, 0, ARENA_BYTES)
            CAT = al([128, NCH, NTOK], BF16)
            ZT = al([128, 4, NBLK, 130], BF16)
            base = al.off
            pre = ("ffn1", l) in sublayers and int(os.environ.get('KDBG', '9')) >= 9
            win = (winb if pre else W[("in", l)]).rearrange("(k p) n -> p k n", p=128)
            wout = (woutb if pre else W[("out", l)]).rearrange("(k p) n -> p k n", p=128)
            xsv = x_src.rearrange("(c p) t -> p c t", p=128)
            xdv = x_dst.rearrange("(c p) t -> p c t", p=128)
            linit = lambda_init(l)
            vsrc = ag_src[1024:2048, :].rearrange("r (two f) -> (r two) f", two=2)
            tsrc = ag_src[2048:2056, :].rearrange("r (a f) -> (r a) f", f=32).rearrange("(ch p) f -> p ch f", p=128)

            smalls = []

            def small(nm, t, d):
                smalls.append(P.dma("sp", "D_const", B(lambda e, t, d: e.dma_start(out=t, in_=d), t, d), writes=[nm]))
            small("gvb", gvb, W[("gv", l)].partition_broadcast(128))
            small("gsub", gsub, W[("gsub", l)].partition_broadcast(128))
            small("lamb", lamb.rearrange("p a b -> p (a b)"), W[("lq", l)].partition_broadcast(128))
            small("bsrow32", bsrow32[0:1, :], W[("bs", l)])
            small("wconv", wconv.rearrange("p a b -> p (a b)"), W[("wc", l)])
            small("wsnat", wsnat, W[("ws", l)].rearrange("g t s -> t g s"))
            P.alias_group(smalls)
            P.op("dve", lambda e: e.tensor_scalar(out=gsub, in0=gsub, scalar1=float(1.0 - linit), scalar2=None,
                                                  op0=ALU.mult), reads=["gsub"], writes=["gsub"])
            P.op("act", lambda e: e.activation(out=bsrow[0:1, :], in_=bsrow32[0:1, :], func=AF.Copy),
                 reads=["bsrow32"], writes=["bsrow"])
            P.op("dve", lambda e: e.tensor_tensor(out=lamprod[:, 0:2, :], in0=lamb[:, 0:4:2, :], in1=lamb[:, 1:4:2, :],
                                                  op=ALU.mult), reads=["lamb"], writes=["lamprod"])
            P.op("dve", lambda e: e.reduce_sum(out=lamcol[:, 0:2], in_=lamprod[:, 0:2, :], axis=AX),
                 reads=["lamprod"], writes=["lamcol"])
            P.op("act", lambda e: e.activation(out=lamcol[:, 2:4], in_=lamcol[:, 0:2], func=AF.Exp),
                 reads=["lamcol"], writes=["lamcol"])
            P.op("dve", lambda e: e.tensor_tensor(out=lamcol[:, 4:5], in0=lamcol[:, 2:3], in1=lamcol[:, 3:4],
                                                  op=ALU.subtract), reads=["lamcol"], writes=["lamcol"])
            P.op("dve", lambda e: e.tensor_scalar(out=lamcol[:, 5:6], in0=lamcol[:, 4:5], scalar1=float(linit),
                                                  scalar2=None, op0=ALU.add), reads=["lamcol"], writes=["lamcol"])
            for g in range(4):
                P.op("pe", B(lambda e, g: e.transpose(out=banks[6][:, g * 128:(g + 1) * 128], in_=wsnat[:, g, :],
                                                      identity=identf), g),
                     reads=["wsnat", "identf"], writes=["ps6"])
            for g in range(4):
                P.op("dve", B(lambda e, g: e.tensor_tensor(out=wst[:, g, :], in0=banks[6][:, g * 128:(g + 1) * 128],
                                                           in1=tril, op=ALU.mult), g),
                     reads=["ps6", "tril"], writes=["wst"])

            XT = al([128, NCH, TG], F32)
            HT = al([128, NCH, TG], BF16)
            WS = [al([128, NCH, 256], BF16) for _ in range(3)]
            UT = al([128, 4, TG], BF16)
            SQ = al([128, 2, TG], BF16)
            RSTD = al([128, TG], F32)
            QKST = al([128, 3, TG], BF16)
            CZ = al([128, 4, TG], BF16)
            V32 = al([128, 4, 512], F32)
            VN = al([128, 2, 512], BF16)
            VST = al([128, 1, 4, 1024], BF16)
            TAILS = al([128, 4, NBLK, 2], BF16)
            SSV = al([128, 16], F32)
            JUNK = al([128, 512], BF16)
            m1_end = al.off
            fm_banks = [0, 1, 2, 3]
            tm_banks = [4, 5]
            CG_ORDER = list(range(INCOLS // 256))
            KMIX = int(os.environ.get("KMIX", "9"))

            def issue_ag(parts, res):
                for pi in parts:
                    r0, r1 = AG_PARTS[pi]
                    P.collective("CC_ag", B(lambda e, pi, r0, r1: e.collective_compute(
                        "AllGather", ALU.bypass, replica_groups=REPLICA_GROUPS,
                        ins=[ag_src[r0:r1, :].opt()], outs=[ag_dstp[pi].opt()]), pi, r0, r1),
                        reads=list(res), writes=[f"agdst{pi}"])
            cnt = {"fm": 0, "tm": 0, "ws": 0, "st": 0, "vn": 0}
            agsrc_res = []
            for tg in range(NTG):
                t0 = tg * TG
                grp = []
                for q in range(2):
                    grp.append(P.dma("sp", "D_xt", B(lambda e, q, t0: e.dma_start(
                        out=XT[:, q * 8:(q + 1) * 8, :], in_=xsv[:, q * 8:(q + 1) * 8, t0:t0 + TG]), q, t0),
                        reads=["xdram"], writes=[f"XT{c}" for c in range(q * 8, q * 8 + 8)]))
                P.alias_group(grp)
                emit_norm(XT, HT, SQ, RSTD, TG, 4 + l, [7], "XT", "HT")
                for cg in CG_ORDER:
                    slot = cnt["ws"] % 3
                    cnt["ws"] += 1
                    P.dma("pool", f"D_ws{slot}", B(lambda e, slot, cg: e.dma_start(
                        out=WS[slot][:, :, :], in_=win[:, :, cg * 256:(cg + 1) * 256]), slot, cg), writes=[f"WS{slot}"])
                    ch0 = 2 * cg
                    is_tm = (4 <= ch0 < 8) or (24 <= ch0 < 32)
                    if not is_tm:
                        for cc in range(2):
                            ch = ch0 + cc
                            bank = fm_banks[cnt["fm"] % 4]
                            cnt["fm"] += 1
                            for k in range(NCH):
                                P.op("pe", B(lambda e, slot, k, cc, bank: e.matmul(
                                    banks[bank][:, :], lhsT=WS[slot][:, k, cc * 128:(cc + 1) * 128], rhs=HT[:, k, :],
                                    start=(k == 0), stop=(k == NCH - 1)), slot, k, cc, bank),
                                    reads=[f"WS{slot}", f"HT{k}"], writes=[f"ps{bank}"], signal=(k == NCH - 1))
                            if ch < 4:
                                P.op("act", B(lambda e, ch, bank: e.activation(out=UT[:, ch, :], in_=banks[bank][:, :],
                                                                                func=AF.Copy), ch, bank),
                                     reads=[f"ps{bank}"], writes=[f"UT{ch}"])
                            elif ch < 16:
                                h = ch - 8
                                s = cnt["st"] % 3
                                cnt["st"] += 1
                                P.op("act", B(lambda e, s, bank, h: e.activation(out=QKST[:, s, :], in_=banks[bank][:, :],
                                                                                  func=AF.Copy, scale=float(2.0 ** (h - 2))),
                                              s, bank, h),
                                     reads=[f"ps{bank}"], writes=[f"QKST{s}"])
                                P.dma("sp", f"D_qkst{s}", B(lambda e, s, h, t0: e.dma_start(
                                    out=qs[h * 128:(h + 1) * 128, t0:t0 + TG], in_=QKST[:, s, :]), s, h, t0),
                                    reads=[f"QKST{s}"], writes=[f"qs{h}_{tg}"])
                            elif ch < 24:
                                h = ch - 16
                                s = cnt["st"] % 3
                                cnt["st"] += 1
                                P.op("dve", B(lambda e, s, bank: e.tensor_copy(out=QKST[:, s, :], in_=banks[bank][:, :]),
                                              s, bank),
                                     reads=[f"ps{bank}"], writes=[f"QKST{s}"])
                                P.dma("sp", f"D_qkst{s}", B(lambda e, s, h, t0: e.dma_start(
                                    out=ag_src[512 * (h // 2) + 128 * (h % 2):512 * (h // 2) + 128 * (h % 2) + 128, t0:t0 + TG],
                                    in_=QKST[:, s, :]), s, h, t0),
                                    reads=[f"QKST{s}"], writes=[f"agk{h}_{tg}"])
                                agsrc_res.append(f"agk{h}_{tg}")
                            elif ch < 36:
                                i = ch - 32
                                P.op("act", B(lambda e, i, bank, t0: e.activation(out=CAT[:, 12 + i, t0:t0 + TG],
                                                                                   in_=banks[bank][:, :], func=AF.Copy),
                                              i, bank, t0),
                                     reads=[f"ps{bank}"], writes=[f"CAT{12 + i}_{tg}"])
                            elif ch < 40:
                                i = ch - 36
                                P.op("act", B(lambda e, i, bank: e.activation(out=CZ[:, i, :], in_=banks[bank][:, :],
                                                                               func=AF.Copy), i, bank),
                                     reads=[f"ps{bank}"], writes=[f"CZ{i}"])
                            else:
                                i = ch - 40
                                P.op("dve", B(lambda e, i, bank, tg: e.tensor_tensor(
                                    out=ZT[:, i, 4 * tg:4 * tg + 4, 2:130],
                                    in0=CZ[:, i, :].rearrange("p (a b) -> p a b", a=4),
                                    in1=banks[bank][:, :].rearrange("p (a b) -> p a b", a=4), op=ALU.mult), i, bank, tg),
                                    reads=[f"CZ{i}", f"ps{bank}"], writes=[f"ZT{i}_{tg}"])
                    else:
                        for tb in range(4):
                            bank = tm_banks[cnt["tm"] % 2]
                            cnt["tm"] += 1
                            for k in range(NCH):
                                P.op("pe", B(lambda e, slot, k, tb, bank: e.matmul(
                                    banks[bank][:, 0:256], lhsT=HT[:, k, tb * 128:(tb + 1) * 128], rhs=WS[slot][:, k, :],
                                    start=(k == 0), stop=(k == NCH - 1)), slot, k, tb, bank),
                                    reads=[f"WS{slot}", f"HT{k}"], writes=[f"ps{bank}"], signal=(k == NCH - 1))
                            if ch0 < 8:
                                c0 = (ch0 - 4) * 128
                                P.op("act", B(lambda e, tb, c0, bank: e.activation(out=V32[:, tb, c0:c0 + 256],
                                                                                    in_=banks[bank][:, 0:256],
                                                                                    func=AF.Copy), tb, c0, bank),
                                     reads=[f"ps{bank}"], writes=[f"V32_{tb}"])
                            else:
                                c0 = (ch0 - 24) * 128
                                vs = 0
                                P.op("dve", B(lambda e, vs, tb, c0, bank: e.tensor_copy(out=VST[:, vs, tb, c0:c0 + 256],
                                                                                         in_=banks[bank][:, 0:256]),
                                              vs, tb, c0, bank),
                                     reads=[f"ps{bank}"], writes=[f"VST{vs}_{tb}_{c0}"])
                        if ch0 == 6:
                            for tb in range(4):
                                blk = 4 * tg + tb
                                P.op("act", B(lambda e, tb: e.activation(out=JUNK, in_=V32[:, tb, :], func=AF.Square,
                                                                         accum_out=SSV[:, tb:tb + 1]), tb),
                                     reads=[f"V32_{tb}"], writes=["JUNK", f"SSV{tb}"])
                                P.op("act", B(lambda e, tb: e.activation(out=SSV[:, 4 + tb:5 + tb], in_=SSV[:, tb:tb + 1],
                                                                         func=AF.Sqrt, scale=1.0 / 512.0,
                                                                         bias=epscol[:, 0:1]), tb),
                                     reads=[f"SSV{tb}", "epscol"], writes=[f"SSR{tb}"])
                                P.op("dve", B(lambda e, tb: e.reciprocal(out=SSV[:, 8 + tb:9 + tb],
                                                                         in_=SSV[:, 4 + tb:5 + tb]), tb),
                                     reads=[f"SSR{tb}"], writes=[f"SSI{tb}"])
                                vn = cnt["vn"] % 2
                                cnt["vn"] += 1
                                P.op("dve", B(lambda e, tb, vn: e.scalar_tensor_tensor(
                                    out=VN[:, vn, :], in0=V32[:, tb, :], scalar=SSV[:, 8 + tb:9 + tb], in1=gvb,
                                    op0=ALU.mult, op1=ALU.mult), tb, vn),
                                    reads=[f"V32_{tb}", f"SSI{tb}", "gvb"], writes=[f"VN{vn}"])
                                for g in range(4):
                                    P.op("pe", B(lambda e, vn, g: e.matmul(
                                        banks[6][:, g * 128:(g + 1) * 128], lhsT=VN[:, vn, g * 128:(g + 1) * 128],
                                        rhs=wst[:, g, :], start=(g == 0), stop=False, skip_group_check=True), vn, g),
                                        reads=[f"VN{vn}", "wst"], writes=["ps6"], signal=False)
                                    P.op("pe", B(lambda e, g: e.matmul(
                                        banks[6][:, g * 128:(g + 1) * 128], lhsT=ones_bf[0:1, 0:128],
                                        rhs=bsrow[0:1, g * 128:(g + 1) * 128], start=False, stop=(g == 3),
                                        skip_group_check=True), g),
                                        reads=["bsrow", "ones"], writes=["ps6"], signal=(g == 3))
                                P.op("dve", B(lambda e, tb, blk: e.tensor_tensor(
                                    out=CAT[:, 0:4, blk * 128:(blk + 1) * 128], in0=UT[:, 0:4, tb * 128:(tb + 1) * 128],
                                    in1=banks[6][:, :].rearrange("p (a b) -> p a b", a=4), op=ALU.mult), tb, blk),
                                    reads=["ps6"] + [f"UT{i}" for i in range(4)], writes=[f"CATA_{blk}"])
                        if ch0 == 30:
                            vs = 0
                            grp = []
                            for pr in range(4):
                                grp.append(P.dma("sp", f"D_vst{vs}", B(lambda e, vs, t0, pr: e.dma_start(
                                    out=ag_src[512 * pr + 256:512 * pr + 512, :].rearrange("r (n f) -> (r n) f", f=256)[
                                        t0:t0 + TG, :].rearrange("(tb p) f -> p tb f", p=128),
                                    in_=VST[:, vs, :, 256 * pr:256 * pr + 256]), vs, t0, pr),
                                    reads=[f"VST{vs}_{tb}_{c0}" for tb in range(4) for c0 in (0, 256, 512, 768)],
                                    writes=[f"agv_{tg}_{pr}"]))
                            P.alias_group(grp)
                            agsrc_res.append(f"agv_{tg}")

                for i in range(4):
                    P.op("dve", B(lambda e, tg, i: e.tensor_copy(out=TAILS[:, i, 4 * tg:4 * tg + 4, :],
                                                                 in_=ZT[:, i, 4 * tg:4 * tg + 4, 128:130]), tg, i),
                         reads=[f"ZT{i}_{tg}"], writes=[f"TAILS{tg}_{i}"])
            P.dma("sp", "D_tail", lambda e: e.dma_start(out=tsrc, in_=TAILS.rearrange("p c b j -> p c (b j)")),
                  reads=[f"TAILS{tg}_{i}" for tg in range(NTG) for i in range(4)], writes=["agtail"])
            agsrc_res.append("agtail")

            P.barrier()
            issue_ag([0, 1, 2, 3, 4], [])
            al.off = base
            KH = [[al([128, SEQ], BF16) for m in range(2)] for b in range(2)]
            QH = [[al([128, NTOK], BF16) for m in range(2)] for b in range(2)]
            VH = [al([128, 32, 130], BF16) for b in range(2)]
            PT = al([128, 6, 512], BF16)
            RL = al([128, 2, 4], F32)
            T1 = al([128, 2, 128], F32)
            OSB = al([128, 2, 128], F32)
            SQJ = al([128, 2, 128], F32)
            SSQ = al([128, 2, 4], F32)
            YB = al([128, 8, 128], BF16)
            TA = al([128, 4, 32], BF16)
            TB = al([128, 4, 32], BF16)
            TMPH = al([128, 4, NBLK, 2], F32)
            ACC = al([128, NBLK, 128], F32)
            ACC2 = al([128, NBLK, 128], F32)
            TMPH2 = al([128, 4, NBLK, 2], F32)
            assert al.off <= ARENA_BYTES

            augs = []
            for b in range(2 if KMIX >= 3 else 0):
                for m in range(2):
                    augs.append(P.dma("sp", "D_const", B(lambda e, b, m: e.dma_start(out=KH[b][m][64:68, :], in_=kaug_d), b, m),
                                      writes=[f"KHaug{b}"]))
                    augs.append(P.dma("sp", "D_const", B(lambda e, b, m: e.dma_start(out=QH[b][m][64:68, :], in_=qaug_d), b, m),
                                      writes=[f"QHaug{b}"]))
            P.alias_group(augs)
            for b in range(2 if KMIX >= 3 else 0):
                P.op("dve", B(lambda e, b: e.memset(VH[b][:, :, 128:129], 1.0), b), writes=[f"VHaug{b}"])

            def emit_conv_prep():
                def tview(r):
                    return ag_dstp[4][r * 8:(r + 1) * 8, :].rearrange(
                        "r (a f) -> (r a) f", f=32).rearrange("(ch p) f -> p ch f", p=128)
                P.dma("sp", "D_ta", lambda e: e.dma_start(out=TA, in_=tview(0)), reads=["agdst4"], writes=["TA"])
                P.dma("sp", "D_tb", lambda e: e.dma_start(out=TB, in_=tview(1)), reads=["agdst4"], writes=["TB"])
                TB4 = TB.rearrange("p c (b j) -> p c b j", j=2)
                P.op("dve", lambda e: e.tensor_scalar(out=TMPH.rearrange("p c b j -> p (c b j)"),
                                                      in0=TA.rearrange("p c f -> p (c f)"), scalar1=sel[:, 0:1],
                                                      scalar2=None, op0=ALU.mult),
                     reads=["TA", "sel"], writes=["TMPH"])
                for i in range(4):
                    P.op("dve", B(lambda e, i: e.scalar_tensor_tensor(out=ZT[:, i, 1:NBLK, 0:2], in0=TB4[:, i, 0:NBLK - 1, :],
                                                                      scalar=sel[:, 1:2], in1=TMPH[:, i, 1:NBLK, :],
                                                                      op0=ALU.mult, op1=ALU.add), i),
                         reads=["TB", "TMPH", "sel"], writes=[f"ZTh{i}"])
                P.op("dve", lambda e: e.tensor_copy(out=ZT[:, :, 0, 0:2], in_=TMPH[:, :, 0, :]),
                     reads=["TMPH"], writes=["ZTh0"])

            def emit_conv_chunk(i):
                zres = ["ZTh0"] + [f"ZTh{i}" for i in range(4)]
                P.op("dve", B(lambda e, i: e.tensor_scalar(out=ACC, in0=ZT[:, i, :, 2:130], scalar1=wconv[:, 2, i:i + 1],
                                                           scalar2=None, op0=ALU.mult), i),
                     reads=zres + ["wconv"], writes=["ACC"])
                P.op("dve", B(lambda e, i: e.scalar_tensor_tensor(out=ACC, in0=ZT[:, i, :, 1:129],
                                                                  scalar=wconv[:, 1, i:i + 1], in1=ACC,
                                                                  op0=ALU.mult, op1=ALU.add), i),
                     reads=zres + ["ACC", "wconv"], writes=["ACC"])
                P.op("dve", B(lambda e, i: e.scalar_tensor_tensor(out=ACC, in0=ZT[:, i, :, 0:128],
                                                                  scalar=wconv[:, 0, i:i + 1], in1=ACC,
                                                                  op0=ALU.mult, op1=ALU.add), i),
                     reads=zres + ["ACC", "wconv"], writes=["ACC"])
                P.op("dve", B(lambda e, i: e.tensor_tensor(
                    out=CAT[:, 12 + i, :].rearrange("p (b t) -> p b t", t=128), in0=ACC,
                    in1=CAT[:, 12 + i, :].rearrange("p (b t) -> p b t", t=128), op=ALU.mult), i),
                    reads=["ACC"], writes=[f"CATC{i}"])

            def emit_conv():
                emit_conv_prep()
                for i in range(4):
                    emit_conv_chunk(i)

            def vview(r, pr):
                return ag_dstp[pr][r * 512 + 256:r * 512 + 512, :].rearrange("r (n f) -> (r n) f", f=256)

            def load_head(h):
                b = h % 2
                pr = h // 2
                grp = []
                for m in range(2):
                    for r in range(2):
                        row0 = r * 512 + 128 * (h % 2) + 64 * m
                        grp.append(P.dma("sp", f"D_kh{b}", B(lambda e, b, m, r, pr, row0: e.dma_start(
                            out=KH[b][m][0:64, r * NTOK:(r + 1) * NTOK], in_=ag_dstp[pr][row0:row0 + 64, :]),
                            b, m, r, pr, row0),
                            reads=[f"agdst{pr}"], writes=[f"KH{b}"]))
                P.alias_group(grp)
                grp = []
                for m in range(2):
                    row0 = h * 128 + m * 64
                    grp.append(P.dma("sp", f"D_qh{b}", B(lambda e, b, m, row0: e.dma_start(
                        out=QH[b][m][0:64, :], in_=qs[row0:row0 + 64, :]), b, m, row0),
                        reads=[f"qs{h}_{tg}" for tg in range(NTG)], writes=[f"QH{b}"]))
                P.alias_group(grp)
                grp = []
                for r in range(2):
                    for half in range(2):
                        grp.append(P.dma("sp", f"D_vh{b}", B(lambda e, b, r, h, half, pr: e.dma_start(
                            out=VH[b][:, r * 16 + half * 8:r * 16 + half * 8 + 8, 0:128],
                            in_=vview(r, pr)[half * 1024:(half + 1) * 1024, (h % 2) * 128:(h % 2) * 128 + 128].rearrange(
                                "(t p) f -> p t f", p=128)), b, r, h, half, pr),
                            reads=[f"agdst{pr}"], writes=[f"VH{b}"]))
                P.alias_group(grp)

            s_banks = [4, 5, 6, 7]
            cs = {"s": 0, "pt": 0, "sm": 0, "yb": 0, "unit": 0}
            tq = []
            TLAG = 10

            def flush_t(force=False):
                while tq and (force or cs["unit"] - tq[0][0] >= TLAG):
                    _, yb, h, t = tq.pop(0)
                    sb = s_banks[cs["s"] % 4]
                    cs["s"] += 1
                    P.op("pe", B(lambda e, yb, sb: e.transpose(out=banks[sb][:, 0:64].bitcast(BF16), in_=YB[:, yb, :],
                                                                identity=ident), yb, sb),
                         reads=[f"YB{yb}", "ident"], writes=[f"ps{sb}"], signal=True)
                    P.op("dve", B(lambda e, sb, h, t: e.tensor_copy(out=CAT[:, 4 + h, t * 128:(t + 1) * 128],
                                                                    in_=banks[sb][:, 0:64].bitcast(BF16)), sb, h, t),
                         reads=[f"ps{sb}"], writes=[f"CATB{h}_{t}"])

            def attn_head(h):
                b = h % 2
                slope = 2.0 ** (-(h + 1))
                for qg in range(4):
                    units = []
                    for tp in range(4 * qg + 4):
                        for r in range(2):
                            for m in range(2):
                                units.append((tp, r, m))
                    first_o = [True] * 4
                    pend = []

                    def qk(u):
                        tp, r, m = u
                        kt = r * 16 + tp
                        tmin = max(tp, 4 * qg)
                        n = (4 * qg + 4 - tmin) * 128
                        c0 = tmin * 128
                        sb = s_banks[cs["s"] % 4]
                        cs["s"] += 1
                        diag = tp >= 4 * qg
                        P.op("pe", B(lambda e, b, m, kt, c0, n, sb, diag: e.matmul(
                            banks[sb][:, 0:n], lhsT=KH[b][m][0:68, kt * 128:(kt + 1) * 128],
                            rhs=QH[b][m][0:68, c0:c0 + n], start=True, stop=(not diag)), b, m, kt, c0, n, sb, diag),
                            reads=[f"KH{b}", f"KHaug{b}", f"QH{b}", f"QHaug{b}"], writes=[f"ps{sb}"], signal=(not diag))
                        if diag:
                            mk = maskA if r == 0 else maskB
                            P.op("pe", B(lambda e, sb, mk: e.matmul(banks[sb][:, 0:128], lhsT=ident, rhs=mk,
                                                                     start=False, stop=True), sb, mk),
                                 reads=["ident", "maskA", "maskB"], writes=[f"ps{sb}"], signal=True)
                        pt = cs["pt"] % 6
                        cs["pt"] += 1
                        P.op("act", B(lambda e, pt, sb, n: e.activation(out=PT[:, pt, 0:n], in_=banks[sb][:, 0:n],
                                                                         func=AF.Exp, scale=float(slope)), pt, sb, n),
                             reads=[f"ps{sb}"], writes=[f"PT{pt}"])
                        return (u, pt, tmin, n)

                    def pv(info):
                        (tp, r, m), pt, tmin, n = info
                        kt = r * 16 + tp
                        cs["unit"] += 1
                        flush_t()
                        for t in range(tmin, 4 * qg + 4):
                            ob = t - 4 * qg
                            st = first_o[ob]
                            first_o[ob] = False
                            last = (tp == t and r == 1)
                            off = (t - tmin) * 128
                            P.op("pe", B(lambda e, ob, m, pt, off, b, kt, st, last: e.matmul(
                                banks[ob][:, m * 129:(m + 1) * 129], lhsT=PT[:, pt, off:off + 128],
                                rhs=VH[b][:, kt, 0:129], start=st, stop=last, skip_group_check=True),
                                ob, m, pt, off, b, kt, st, last),
                                reads=[f"PT{pt}", f"VH{b}", f"VHaug{b}"], writes=[f"ps{ob}"],
                                signal=(last or t == 4 * qg + 3))
                            if last and m == 1:
                                post(h, t, ob)

                    def post(h, t, ob):
                        sm = cs["sm"] % 2
                        cs["sm"] += 1
                        bk = banks[ob]
                        P.op("dve", B(lambda e, sm, bk: e.reciprocal(out=RL[:, sm, 0:2], in_=bk[:, 128:258:129]), sm, bk),
                             reads=[f"ps{ob}"], writes=[f"RL{sm}"])
                        P.op("dve", B(lambda e, sm: e.tensor_tensor(out=RL[:, sm, 2:3], in0=RL[:, sm, 1:2],
                                                                    in1=lamcol[:, 5:6], op=ALU.mult), sm),
                             reads=[f"RL{sm}", "lamcol"], writes=[f"RLb{sm}"])
                        P.op("dve", B(lambda e, sm, bk: e.tensor_scalar(out=T1[:, sm, :], in0=bk[:, 129:257],
                                                                        scalar1=RL[:, sm, 2:3], scalar2=None,
                                                                        op0=ALU.mult), sm, bk),
                             reads=[f"ps{ob}", f"RLb{sm}"], writes=[f"T1{sm}"])
                        P.op("dve", B(lambda e, sm, bk: e.scalar_tensor_tensor(out=OSB[:, sm, :], in0=bk[:, 0:128],
                                                                               scalar=RL[:, sm, 0:1], in1=T1[:, sm, :],
                                                                               op0=ALU.mult, op1=ALU.subtract), sm, bk),
                             reads=[f"ps{ob}", f"RL{sm}", f"T1{sm}"], writes=[f"OSB{sm}"])
                        P.op("dve", B(lambda e, sm: e.tensor_tensor(out=SQJ[:, sm, :], in0=OSB[:, sm, :],
                                                                    in1=OSB[:, sm, :], op=ALU.mult), sm),
                             reads=[f"OSB{sm}"], writes=[f"SQJ{sm}"])
                        P.op("dve", B(lambda e, sm: e.reduce_sum(out=SSQ[:, sm, 0:1], in_=SQJ[:, sm, :], axis=AX), sm),
                             reads=[f"SQJ{sm}"], writes=[f"SSQ{sm}"])
                        P.op("act", B(lambda e, sm: e.activation(out=SSQ[:, sm, 1:2], in_=SSQ[:, sm, 0:1], func=AF.Ln,
                                                                 scale=1.0 / 128.0, bias=epscol[:, 0:1]), sm),
                             reads=[f"SSQ{sm}", "epscol"], writes=[f"SSL{sm}"])
                        P.op("act", B(lambda e, sm: e.activation(out=SSQ[:, sm, 2:3], in_=SSQ[:, sm, 1:2], func=AF.Exp,
                                                                 scale=-0.5), sm),
                             reads=[f"SSL{sm}"], writes=[f"SSE{sm}"])
                        yb = cs["yb"] % 8
                        cs["yb"] += 1
                        P.op("dve", B(lambda e, sm, yb: e.scalar_tensor_tensor(out=YB[:, yb, :], in0=OSB[:, sm, :],
                                                                               scalar=SSQ[:, sm, 2:3], in1=gsub,
                                                                               op0=ALU.mult, op1=ALU.mult), sm, yb),
                             reads=[f"OSB{sm}", f"SSE{sm}", "gsub"], writes=[f"YB{yb}"])
                        tq.append((cs["unit"], yb, h, t))

                    SKEW = 3
                    for i, u in enumerate(units):
                        pend.append(qk(u))
                        if len(pend) > SKEW:
                            pv(pend.pop(0))
                    while pend:
                        pv(pend.pop(0))

            NHEADS = 8 if KMIX >= 9 else (KMIX - 3 if KMIX >= 4 else 0)
            if NHEADS:
                load_head(0)
            for h in range(NHEADS):
                if h + 1 < NHEADS:
                    load_head(h + 1)
                attn_head(h)
                if NHEADS >= 8:
                    if h == 1:
                        emit_conv_prep()
                    elif 2 <= h <= 5:
                        emit_conv_chunk(h - 2)
            if NHEADS < 8:
                emit_conv()
            flush_t(force=True)

            P.barrier()
            al.off = base
            XT6 = [al([128, NCH, TG], F32) for _ in range(2)]
            WS6 = [al([128, NCH, 256], BF16) for _ in range(3)]
            c6 = {"ws": 0, "bk": 0}
            for tg in range(NTG):
                t0 = tg * TG
                xb = tg % 2
                X6 = XT6[xb]
                grp = []
                for q in range(2):
                    grp.append(P.dma("sp", f"D_xt6{xb}", B(lambda e, q, t0, X6: e.dma_start(
                        out=X6[:, q * 8:(q + 1) * 8, :], in_=xsv[:, q * 8:(q + 1) * 8, t0:t0 + TG]), q, t0, X6),
                        reads=[f"xdram{tg}"], writes=[f"X6{xb}_{c}" for c in range(q * 8, q * 8 + 8)]))
                P.alias_group(grp)
                for cg in range(D // 256):
                    slot = c6["ws"] % 3
                    c6["ws"] += 1
                    P.dma("pool", f"D_ws{slot}", B(lambda e, slot, cg: e.dma_start(
                        out=WS6[slot][:, :, :], in_=wout[:, :, cg * 256:(cg + 1) * 256]), slot, cg), writes=[f"WS{slot}"])
                    for cc in range(2):
                        j = 2 * cg + cc
                        bank = c6["bk"] % 4
                        c6["bk"] += 1
                        for k in range(NCH):
                            P.op("pe", B(lambda e, slot, k, cc, bank, t0: e.matmul(
                                banks[bank][:, :], lhsT=WS6[slot][:, k, cc * 128:(cc + 1) * 128],
                                rhs=CAT[:, k, t0:t0 + TG], start=(k == 0), stop=(k == NCH - 1)), slot, k, cc, bank, t0),
                                reads=[f"WS{slot}"], writes=[f"ps{bank}"], signal=(k == NCH - 1))
                        P.op("dve", B(lambda e, j, bank, X6: e.tensor_tensor(out=X6[:, j, :], in0=banks[bank][:, :],
                                                                             in1=X6[:, j, :], op=ALU.add), j, bank, X6),
                             reads=[f"ps{bank}", f"X6{xb}_{j}"], writes=[f"X6{xb}_{j}"])
                grp = []
                for q in range(2):
                    grp.append(P.dma("sp", f"D_xst6{xb}", B(lambda e, q, t0, X6: e.dma_start(
                        out=xdv[:, q * 8:(q + 1) * 8, t0:t0 + TG], in_=X6[:, q * 8:(q + 1) * 8, :]), q, t0, X6),
                        reads=[f"X6{xb}_{c}" for c in range(q * 8, q * 8 + 8)], writes=[f"xdram{tg}"]))
                P.alias_group(grp)

        MIX["emit"] = emit_mix

        setup()
        P.barrier()
        cur_src = xT_in
        n_sub = len(sublayers)
        for i, (kind, l) in enumerate(sublayers):
            last = (i == n_sub - 1)
            if kind in ("ffn1", "ffn2"):
                emit_ffn(l, 1 if kind == "ffn1" else 2, cur_src, xs, final=(last and final_norm))
            else:
                MIX["emit"](l, cur_src, xs)
            cur_src = xs
            P.barrier()
        if not final_norm:
            al = Alloc(arena, 0, ARENA_BYTES)
            XT = al([128, NCH, 1024], F32)
            xsv = xs.rearrange("(c p) t -> p c t", p=128)
            outv = outT.rearrange("(c p) t -> p c t", p=128)
            for tg in range(2):
                t0 = tg * 1024
                P.dma("sp", "D_xt", (lambda t0: lambda e: e.dma_start(out=XT, in_=xsv[:, :, t0:t0 + 1024]))(t0),
                      reads=["xdram"], writes=["XTall"])
                P.dma("sp", "D_xst", (lambda t0: lambda e: e.dma_start(out=outv[:, :, t0:t0 + 1024], in_=XT))(t0),
                      reads=["XTall"], writes=["outdram"])
            P.barrier()

        P.barrier()
        P.finalize()

        sems = {}
        for k in P.sem_names:
            sems[k] = ctx.enter_context(nc.semaphore(k))
        block = ctx.enter_context(nc.Block())

        @block.tensor
        def _(e):
            _emit_engine(P, "pe", e, sems)

        @block.scalar
        def _(e):
            _emit_engine(P, "act", e, sems)

        @block.vector
        def _(e):
            _emit_engine(P, "dve", e, sems)

        @block.gpsimd
        def _(e):
            _emit_engine(P, "pool", e, sems)

        @block.sync
        def _(e):
            _emit_engine(P, "sp", e, sems)

    nc._planner_stats = {e: len(P.ops[e]) for e in P.ENGS}
    nc._sem_counts = dict(P.sem_count)
    return nc


def core_token_index(r):
    t = np.arange(NBLK)
    return ((2 * t[:, None] + r) * 128 + np.arange(128)[None, :]).reshape(-1)


def make_common_inputs(inputs, sublayers):
    f32 = np.float32
    com = {}
    g = np.zeros((13, D), f32)
    g[0:4] = np.asarray(inputs["g_ffn1"], f32)
    g[4:8] = np.asarray(inputs["g_mix"], f32)
    g[8:12] = np.asarray(inputs["g_ffn2"], f32)
    g[12] = np.asarray(inputs["g_final"], f32)
    com["gcols"] = np.ascontiguousarray(g.reshape(13, NCH, 128).transpose(2, 0, 1).reshape(128, 13 * NCH))
    if any(k == "mix" for k, _ in sublayers):
        bf = ml_dtypes.bfloat16
        com["ident"] = np.eye(128, dtype=f32).astype(bf)
        com["identf"] = np.eye(128, dtype=f32)
        ii = np.arange(128)
        com["tril01"] = (ii[:, None] <= ii[None, :]).astype(f32)
        n = np.arange(SEQ)
        rr, tt, i_ = n // NTOK, (n % NTOK) // 128, n % 128
        com["kaug"] = np.stack([np.ones(SEQ), np.ones(SEQ), 2 * tt + rr, i_]).astype(f32).astype(bf)
    for kind, l in sublayers:
        if kind == "mix":
            com[f"win_{l}"] = np.asarray(inputs["w_in"][l], f32)
            com[f"wout_{l}"] = np.asarray(inputs["w_out"][l], f32)
            com[f"gv_{l}"] = np.asarray(inputs["g_sga_v"][l], f32).reshape(1, 512)
            com[f"ws_{l}"] = np.asarray(inputs["w_sga_s"][l], f32)
            com[f"bs_{l}"] = np.asarray(inputs["b_sga_s"][l], f32).reshape(1, 512)
            com[f"lq_{l}"] = np.asarray(inputs["lambda_qk"][l], f32).reshape(1, 256)
            com[f"gsub_{l}"] = np.asarray(inputs["g_diff_sub"][l], f32).reshape(1, 128)
            com[f"wc_{l}"] = np.ascontiguousarray(
                np.asarray(inputs["w_conv"][l], f32).reshape(3, 4, 128).transpose(2, 0, 1).reshape(128, 12))
    for kind, l in sublayers:
        if kind == "ffn1":
            com[f"w1g_{l}"] = np.asarray(inputs["w_ffn1_gate"][l], f32)
            com[f"w1u_{l}"] = np.asarray(inputs["w_ffn1_up"][l], f32)
            com[f"w1d_{l}"] = np.asarray(inputs["w_ffn1_down"][l], f32)
        elif kind == "ffn2":
            com[f"w2g_{l}"] = np.asarray(inputs["w_ffn2_gate"][l], f32)
            com[f"w2u_{l}"] = np.asarray(inputs["w_ffn2_up"][l], f32)
            com[f"w2d_{l}"] = np.asarray(inputs["w_ffn2_down"][l], f32)
    return com


def core_constants(r):
    f32 = np.float32
    bf = ml_dtypes.bfloat16
    n = np.arange(NTOK)
    t, i_ = n // 128, n % 128
    qaug = np.stack([-128.0 * (2 * t + r), -1.0 * i_, np.full(NTOK, 128.0), np.ones(NTOK)]).astype(f32).astype(bf)
    ii = np.arange(128)
    tri = np.where(ii[:, None] > ii[None, :], -MASK_BIG, 0.0).astype(f32)
    allm = np.full((128, 128), -MASK_BIG, f32)
    zero = np.zeros((128, 128), f32)
    if r == 0:
        mA, mB = tri, allm
        sel = np.tile(np.array([[0.0, 1.0]], f32), (128, 1))
    else:
        mA, mB = zero, tri
        sel = np.tile(np.array([[1.0, 0.0]], f32), (128, 1))
    return {"qaug": qaug, "maskA": mA.astype(bf), "maskB": mB.astype(bf), "sel": np.ascontiguousarray(sel)}


_NC_CACHE = {}


def run_sublayers(inputs, sublayers, x_cores, final_norm):
    key = (tuple(sublayers), final_norm)
    if key not in _NC_CACHE:
        _NC_CACHE[key] = build_program(list(sublayers), final_norm=final_norm)
    nc = _NC_CACHE[key]
    com = make_common_inputs(inputs, sublayers)
    in_maps = []
    has_mix = any(k == "mix" for k, _ in sublayers)
    for c in range(8):
        m = dict(com)
        m["xT"] = x_cores[c]
        if has_mix:
            m.update(core_constants(c % 2))
        in_maps.append(m)
    if os.environ.get("KTRACE"):
        res = run_bass_kernel_spmd(nc, in_maps, core_ids=list(range(8)), trace=True)
        print("KTRACE exec_time_ns", res.exec_time_ns, flush=True)
    else:
        res = run_bass_kernel_spmd(nc, in_maps, core_ids=list(range(8)))
    return [np.asarray(res.results[c]["outT"]) for c in range(8)]


def shard_x(x):
    x = np.asarray(x, np.float32)
    outs = []
    for c in range(8):
        b, r = c // 2, c % 2
        idx = core_token_index(r)
        outs.append(np.ascontiguousarray(x[b][idx].T))
    return outs


def unshard_x(outs):
    y = np.zeros((4, SEQ, D), np.float32)
    for c in range(8):
        b, r = c // 2, c % 2
        idx = core_token_index(r)
        y[b][idx] = outs[c].T
    return y


def kernel(**inputs):
    subl = []
    for l in range(DEPTH):
        subl += [("ffn1", l), ("mix", l), ("ffn2", l)]
    xc = shard_x(inputs["x"])
    outs = run_sublayers(inputs, subl, xc, final_norm=True)
    return unshard_x(outs)
```

```python
import math
import os
import numpy as np
import ml_dtypes
import concourse.bass as bass
import concourse.mybir as mybir
from concourse.bass_utils import run_bass_kernel_spmd

F32 = mybir.dt.float32
BF16 = mybir.dt.bfloat16
AF = mybir.ActivationFunctionType
ALU = mybir.AluOpType

D = 2048
DFF = 5632
NCH = D // 128
NFF = DFF // 128
SEQ = 4096
NTOK = 2048
NBLK = 16
DEPTH = 4
INCOLS = 5632
EPS = 1e-6
SQRT_D = math.sqrt(float(D))
AG_ROWS = 2056
MASK_BIG = 131072.0
KIB = 1024
REPLICA_GROUPS = [[0, 1]] if os.environ.get('KSIM2') else [[0, 1], [2, 3], [4, 5], [6, 7]]


class Op:
    __slots__ = ("eng", "fn", "deps", "sem", "val", "signal", "idx", "inc")

    def __init__(self, eng, fn):
        self.eng = eng
        self.fn = fn
        self.deps = []
        self.sem = None
        self.val = None
        self.signal = True
        self.inc = 1


class Planner:
    ENGS = ("pe", "act", "dve", "pool", "sp")

    def __init__(self):
        self.ops = {e: [] for e in self.ENGS}
        self.last_writer = {}
        self.readers = {}
        self.sem_names = {}
        self.sem_count = {}
        self.last_on_sem = {}
        for e in ("pe", "act", "dve", "pool"):
            self.sem_names["E_" + e] = None

    def _collect(self, reads, writes):
        deps = {}

        def add(op):
            if op is None:
                return
            deps[id(op)] = op

        for r in reads:
            add(self.last_writer.get(r))
        for w in writes:
            add(self.last_writer.get(w))
            for op in self.readers.get(w, {}).values():
                add(op)
        return list(deps.values())

    def _register(self, op, reads, writes):
        for r in reads:
            self.readers.setdefault(r, {})[op.sem if op.sem else ("E", op.eng)] = op
        for w in writes:
            self.last_writer[w] = op
            self.readers[w] = {}

    def op(self, eng, fn, reads=(), writes=(), signal=True):
        o = Op(eng, fn)
        o.signal = signal
        o.sem = "E_" + eng
        deps = self._collect(reads, writes)
        if eng == "pe":
            deps = [d for d in deps if not (d.eng == "pe" and d.sem == "E_pe")]
        o.deps = deps
        self.ops[eng].append(o)
        self._register(o, reads, writes)
        self.last_on_sem[o.sem] = o
        return o

    def dma(self, queue, semkey, fn, reads=(), writes=(), extra_deps=()):
        o = Op(queue, fn)
        o.sem = semkey
        o.inc = 16
        self.sem_names.setdefault(semkey, None)
        self.sem_count[semkey] = self.sem_count.get(semkey, 0) + 16
        o.val = self.sem_count[semkey]
        o.deps = self._collect(reads, writes) + list(extra_deps)
        self.ops[queue].append(o)
        self._register(o, reads, writes)
        self.last_on_sem[semkey] = o
        return o

    def alias_group(self, ops):
        if not ops:
            return
        v = max(o.val for o in ops)
        ids = set(id(o) for o in ops)
        for o in ops:
            o.val = v
            o.deps = [d for d in o.deps if id(d) not in ids]

    def collective(self, semkey, fn, reads=(), writes=()):
        o = Op("pool", fn)
        o.sem = semkey
        o.inc = 1
        self.sem_names.setdefault(semkey, None)
        self.sem_count[semkey] = self.sem_count.get(semkey, 0) + 1
        o.val = self.sem_count[semkey]
        o.deps = self._collect(reads, writes)
        self.ops["pool"].append(o)
        self._register(o, reads, writes)
        self.last_on_sem[semkey] = o
        return o

    def barrier(self):
        lasts = []
        for k, o in self.last_on_sem.items():
            if o is not None:
                if o.sem.startswith("E_"):
                    o.signal = True
                lasts.append(o)
        for e in self.ENGS:
            b = Op(e, None)
            b.signal = False
            b.sem = None
            b.deps = [o for o in lasts if not (o.eng == e and e == "pe" and o.sem == "E_pe")]
            self.ops[e].append(b)
        self.last_writer = {}
        self.readers = {}

    def finalize(self):
        for e in ("pe", "act", "dve", "pool"):
            ops = [o for o in self.ops[e] if o.sem == "E_" + e]
            cnt = 0
            for o in ops:
                if o.signal:
                    cnt += 1
                    o.val = cnt
            nxt = None
            for o in reversed(ops):
                if o.signal:
                    nxt = o.val
                else:
                    assert nxt is not None, "trailing non-signalling op with dependants?"
                    o.val = nxt
            self.sem_count["E_" + e] = cnt

    def replay(self, nc, sems, engines):
        pass


def _emit_engine(planner, eng_name, e, sems):
    waited = {}
    for o in planner.ops[eng_name]:
        need = {}
        for d in o.deps:
            if d.val is None:
                continue
            if need.get(d.sem, 0) < d.val:
                need[d.sem] = d.val
        for s, v in need.items():
            if waited.get(s, 0) < v:
                e.wait_ge(sems[s], v)
                waited[s] = v
        if o.fn is None:
            continue
        ins = o.fn(e)
        if o.sem is not None and (o.signal or not o.sem.startswith("E_")):
            if o.sem.startswith("CC"):
                ins.then_inc(sems[o.sem])
            else:
                ins.then_inc(sems[o.sem], o.inc)


def lambda_init(l):
    return 0.8 - 0.6 * math.exp(-0.3 * l)


def build_program(sublayers, final_norm=True, x_in_name="xT", debug=False):
    nc = bass.Bass("TRN2", target_bir_lowering=False)
    P = Planner()

    def din(name, shape, dt=F32):
        return nc.dram_tensor(name, list(shape), dt, kind="ExternalInput").ap()

    layers = sorted(set(l for _, l in sublayers))
    kinds = set(k for k, _ in sublayers)
    xT_in = din(x_in_name, [D, NTOK])
    outT = nc.dram_tensor("outT", [D, NTOK], F32, kind="ExternalOutput").ap()
    gcols_d = din("gcols", [128, 13 * NCH])
    W = {}
    for l in layers:
        if ("ffn1", l) in sublayers:
            W[("g1", l)] = din(f"w1g_{l}", [D, DFF])
            W[("u1", l)] = din(f"w1u_{l}", [D, DFF])
            W[("d1", l)] = din(f"w1d_{l}", [DFF, D])
        if ("ffn2", l) in sublayers:
            W[("g2", l)] = din(f"w2g_{l}", [D, DFF])
            W[("u2", l)] = din(f"w2u_{l}", [D, DFF])
            W[("d2", l)] = din(f"w2d_{l}", [DFF, D])
        if ("mix", l) in sublayers:
            W[("in", l)] = din(f"win_{l}", [D, INCOLS])
            W[("out", l)] = din(f"wout_{l}", [D, D])
            W[("gv", l)] = din(f"gv_{l}", [1, 512])
            W[("ws", l)] = din(f"ws_{l}", [4, 128, 128])
            W[("bs", l)] = din(f"bs_{l}", [1, 512])
            W[("lq", l)] = din(f"lq_{l}", [1, 256])
            W[("gsub", l)] = din(f"gsub_{l}", [1, 128])
            W[("wc", l)] = din(f"wc_{l}", [128, 12])
    has_mix = "mix" in kinds
    if has_mix:
        ident_d = din("ident", [128, 128], BF16)
        identf_d = din("identf", [128, 128], F32)
        tril_d = din("tril01", [128, 128], F32)
        maskA_d = din("maskA", [128, 128], BF16)
        maskB_d = din("maskB", [128, 128], BF16)
        kaug_d = din("kaug", [4, SEQ], BF16)
        qaug_d = din("qaug", [4, NTOK], BF16)
        sel_d = din("sel", [128, 2])
    xs = nc.dram_tensor("xs", [D, NTOK], F32).ap()
    if has_mix:
        qs = nc.dram_tensor("qs", [1024, NTOK], BF16).ap()
        winb = nc.dram_tensor("winb", [D, INCOLS], BF16).ap()
        woutb = nc.dram_tensor("woutb", [D, D], BF16).ap()
        ag_src = nc.dram_tensor("ag_src", [AG_ROWS, NTOK], BF16).ap()
        AG_PARTS = [(0, 512), (512, 1024), (1024, 1536), (1536, 2048), (2048, 2056)]
        ag_dstp = [nc.dram_tensor(f"ag_dst{i}", [2 * (r1 - r0), NTOK], BF16).ap() for i, (r0, r1) in enumerate(AG_PARTS)]
    dbg = {}

    ARENA_BYTES = 190 * KIB
    PERS_BYTES = 15 * KIB

    import contextlib
    with contextlib.ExitStack() as ctx:
        arena = ctx.enter_context(nc.sbuf_tensor("arena", [128, ARENA_BYTES // 2], BF16))
        pers = ctx.enter_context(nc.sbuf_tensor("pers", [128, PERS_BYTES // 2], BF16))
        banks = [ctx.enter_context(nc.psum_tensor(f"bank{i}", [128, 512], F32)) for i in range(8)]

        def carve(base, off, shape, dt):
            n = int(np.prod(shape[1:]))
            esz = 4 if dt == F32 else 2
            assert off % 4 == 0
            a = base[:, off // 2: off // 2 + n * esz // 2]
            if dt == F32:
                a = a.bitcast(F32)
            if len(shape) == 3:
                a = a.rearrange("p (a b) -> p a b", a=shape[1])
            elif len(shape) == 4:
                a = a.rearrange("p (a b c) -> p a b c", a=shape[1], b=shape[2])
            return a

        class Alloc:
            def __init__(self, base, start, limit):
                self.base, self.off, self.limit = base, start, limit

            def __call__(self, shape, dt):
                n = int(np.prod(shape[1:])) * (4 if dt == F32 else 2)
                n = (n + 31) // 32 * 32
                a = carve(self.base, self.off, shape, dt)
                self.off += n
                assert self.off <= self.limit, (self.off, self.limit)
                return a

        pal = Alloc(pers, 0, PERS_BYTES)
        ones_bf = pal([128, 128], BF16)
        gcols = pal([128, 13, NCH], F32)
        epscol = pal([128, 8], F32)
        if has_mix:
            ident = pal([128, 128], BF16)
            identf = pal([128, 128], F32)
            tril = pal([128, 128], F32)
            maskA = pal([128, 128], BF16)
            maskB = pal([128, 128], BF16)
            sel = pal([128, 2], F32)
            gvb = pal([128, 512], F32)
            gsub = pal([128, 128], F32)
            wst = pal([128, 4, 128], BF16)
            bsrow = pal([128, 512], BF16)
            bsrow32 = pal([128, 512], F32)
            lamb = pal([128, 4, 64], F32)
            lamprod = pal([128, 4, 64], F32)
            lamcol = pal([128, 8], F32)
            wconv = pal([128, 3, 4], F32)
            wsnat = pal([128, 4, 128], F32)

        def setup():
            P.op("dve", lambda e: e.memset(ones_bf, 1.0), writes=["ones"])
            P.dma("sp", "D_const", lambda e: e.dma_start(out=gcols.rearrange("p a c -> p (a c)"), in_=gcols_d),
                  writes=["gcols"])
            P.op("dve", lambda e: e.memset(epscol, EPS), writes=["epscol"])
            if has_mix:
                for nm, t, d in (("ident", ident, ident_d), ("identf", identf, identf_d), ("tril", tril, tril_d),
                                 ("maskA", maskA, maskA_d), ("maskB", maskB, maskB_d), ("sel", sel, sel_d)):
                    P.dma("sp", "D_const", (lambda t, d: (lambda e: e.dma_start(out=t, in_=d)))(t, d), writes=[nm])

        def emit_norm(XT, HT, SQ, RSTD, T, gidx, ss_banks, xres, hres, inplace=False):
            nh = T // 512
            for c in range(NCH):
                sl = c % 2
                P.op("act", (lambda c, sl: lambda e: e.activation(out=SQ[:, sl, :], in_=XT[:, c, :], func=AF.Square))(c, sl),
                     reads=[f"{xres}{c}"], writes=[f"SQ{sl}"])
                for h in range(nh):
                    P.op("pe", (lambda c, sl, h: lambda e: e.matmul(banks[ss_banks[h]][:, :], lhsT=ones_bf,
                                                                    rhs=SQ[:, sl, h * 512:(h + 1) * 512],
                                                                    start=(c == 0), stop=(c == NCH - 1)))(c, sl, h),
                         reads=[f"SQ{sl}", "ones"], writes=[f"ps{ss_banks[h]}"], signal=(c == NCH - 1 or True))
            for h in range(nh):
                P.op("act", (lambda h: lambda e: e.activation(out=RSTD[:, h * 512:(h + 1) * 512],
                                                              in_=banks[ss_banks[h]][:, :], func=AF.Sqrt,
                                                              scale=1.0 / float(D), bias=epscol[:, 0:1]))(h),
                     reads=[f"ps{ss_banks[h]}", "epscol"], writes=[f"RSTD{h}"])
                P.op("dve", (lambda h: lambda e: e.reciprocal(out=RSTD[:, h * 512:(h + 1) * 512],
                                                              in_=RSTD[:, h * 512:(h + 1) * 512]))(h),
                     reads=[f"RSTD{h}"], writes=[f"RSTD{h}"])
            for c in range(NCH):
                dst = XT if inplace else HT
                P.op("dve", (lambda c, dst: lambda e: e.scalar_tensor_tensor(out=dst[:, c, :], in0=XT[:, c, :],
                                                                             scalar=gcols[:, gidx, c:c + 1],
                                                                             in1=RSTD[:, 0:T],
                                                                             op0=ALU.mult, op1=ALU.mult))(c, dst),
                     reads=[f"{xres}{c}", "gcols"] + [f"RSTD{h}" for h in range(nh)],
                     writes=[f"{xres if inplace else hres}{c}"])

        def emit_ffn(l, which, x_src, x_dst, final):
            TG = 1024
            NH = 2
            al = Alloc(arena, 0, ARENA_BYTES)
            XT = al([128, NCH, TG], F32)
            HT = al([128, NCH, TG], BF16)
            AT = al([128, 8, TG], BF16)
            SG = al([128, 2, 512], BF16)
            SQ = al([128, 2, TG], BF16)
            RSTD = al([128, TG], F32)
            WGU = [al([128, 2, NCH, 256], BF16) for _ in range(2)]
            WD = [al([128, 4, D], BF16) for _ in range(2)]
            wg, wu, wd = W[("g%d" % which, l)], W[("u%d" % which, l)], W[("d%d" % which, l)]
            wgv = wg.rearrange("(k p) n -> p k n", p=128)
            wuv = wu.rearrange("(k p) n -> p k n", p=128)
            wdv = wd.rearrange("(c p) n -> p c n", p=128)
            gidx = (0 if which == 1 else 2) * 4 + l
            xsv = x_src.rearrange("(c p) t -> p c t", p=128)
            xdv = x_dst.rearrange("(c p) t -> p c t", p=128)
            outv = outT.rearrange("(c p) t -> p c t", p=128)
            ps_gu = [(0, 1), (2, 3), (4, 5)]
            ps_y = [6, 7]
            gu_i = 0
            y_i = 0
            wgu_i = 0
            wd_i = 0
            sg_i = 0
            DBG = int(os.environ.get('KDBG', '9'))
            NTG_F = NTOK // TG
            cvt = []
            if which == 1 and ("mix", l) in sublayers and DBG >= 9:
                for cg in range(INCOLS // 256):
                    cvt.append((W[("in", l)][:, cg * 256:(cg + 1) * 256], winb[:, cg * 256:(cg + 1) * 256], "winb"))
                for cg in range(D // 256):
                    cvt.append((W[("out", l)][:, cg * 256:(cg + 1) * 256], woutb[:, cg * 256:(cg + 1) * 256], "woutb"))
            cvt_ops = []

            def issue_cvt(n):
                for _ in range(n):
                    if cvt:
                        src, dst, res = cvt.pop(0)
                        cvt_ops.append(P.dma("pool", "D_cvt", B(lambda e, src, dst: e.dma_start(out=dst, in_=src), src, dst),
                                             writes=[f"{res}_{len(cvt)}"]))

            def load_chunk(tg, c):
                t0 = tg * TG
                P.dma("sp", f"D_xt{c}", B(lambda e, c, t0: e.dma_start(out=XT[:, c, :], in_=xsv[:, c, t0:t0 + TG]), c, t0),
                      reads=[f"xd{tg}_{c}"], writes=[f"XT{c}"])

            def store_chunk(tg, c, dstv, dres):
                t0 = tg * TG
                P.dma("sp", f"D_xst{c}", B(lambda e, c, t0, dstv: e.dma_start(out=dstv[:, c, t0:t0 + TG], in_=XT[:, c, :]),
                                           c, t0, dstv),
                      reads=[f"XT{c}"], writes=[f"{dres}{tg}_{c}"])

            for tg in range(NTG_F):
                t0 = tg * TG
                if tg == 0:
                    for c in range(NCH):
                        load_chunk(0, c)
                if DBG >= 1:
                    emit_norm(XT, HT, SQ, RSTD, TG, gidx, [6, 7], "XT", "HT")
                if DBG >= 9:
                    supers = [[g] for g in range(NFF // 4 - 2)] + [[NFF // 4 - 2, NFF // 4 - 1]]
                elif DBG >= 2:
                    supers = [[0]]
                else:
                    supers = []
                for sgrp in supers:
                    issue_cvt(2)
                    for gi_, g in enumerate(sgrp):
                        for pair in range(2):
                            slot = wgu_i % 2
                            wgu_i += 1
                            ff0 = (g * 4 + pair * 2) * 128
                            grp = []
                            for gi, wv in enumerate((wgv, wuv)):
                                grp.append(P.dma("pool", f"D_wgu{slot}", B(lambda e, gi, wv, slot, ff0: e.dma_start(
                                    out=WGU[slot][:, gi, :, :], in_=wv[:, :, ff0:ff0 + 256]), gi, wv, slot, ff0),
                                    writes=[f"WGU{slot}"]))
                            P.alias_group(grp)
                            for cc in range(2):
                                c = gi_ * 4 + pair * 2 + cc
                                for h in range(NH):
                                    bg, bu = ps_gu[gu_i % 3]
                                    gu_i += 1
                                    for gi, bank in ((0, bg), (1, bu)):
                                        for k in range(NCH):
                                            P.op("pe", B(lambda e, slot, gi, k, cc, h, bank: e.matmul(
                                                banks[bank][:, :], lhsT=WGU[slot][:, gi, k, cc * 128:(cc + 1) * 128],
                                                rhs=HT[:, k, h * 512:(h + 1) * 512], start=(k == 0), stop=(k == NCH - 1)),
                                                slot, gi, k, cc, h, bank),
                                                reads=[f"WGU{slot}", f"HT{k}"], writes=[f"ps{bank}"], signal=(k == NCH - 1))
                                    s_ = sg_i % 2
                                    sg_i += 1
                                    P.op("act", B(lambda e, s_, bg: e.activation(out=SG[:, s_, :], in_=banks[bg][:, :],
                                                                                 func=AF.Silu), s_, bg),
                                         reads=[f"ps{bg}"], writes=[f"SG{s_}"])
                                    P.op("dve", B(lambda e, s_, bu, c, h: e.tensor_tensor(
                                        out=AT[:, c, h * 512:(h + 1) * 512], in0=SG[:, s_, :], in1=banks[bu][:, :],
                                        op=ALU.mult), s_, bu, c, h),
                                        reads=[f"SG{s_}", f"ps{bu}"], writes=[f"AT{c}_{h}"])
                    if DBG == 2:
                        continue
                    slots = []
                    for g in sgrp:
                        slot = wd_i % 2
                        wd_i += 1
                        slots.append(slot)
                        P.dma("pool", f"D_wd{slot}", B(lambda e, slot, g: e.dma_start(
                            out=WD[slot][:, :, :], in_=wdv[:, g * 4:(g + 1) * 4, :]), slot, g), writes=[f"WD{slot}"])
                    nmm = 4 * len(sgrp)
                    for j in range(NCH):
                        for h in range(NH):
                            bank = ps_y[y_i % 2]
                            y_i += 1
                            for idx in range(nmm):
                                gi_, c4 = idx // 4, idx % 4
                                slot = slots[gi_]
                                c = gi_ * 4 + c4
                                P.op("pe", B(lambda e, slot, c4, c, j, h, bank, idx: e.matmul(
                                    banks[bank][:, :], lhsT=WD[slot][:, c4, j * 128:(j + 1) * 128],
                                    rhs=AT[:, c, h * 512:(h + 1) * 512], start=(idx == 0), stop=(idx == nmm - 1)),
                                    slot, c4, c, j, h, bank, idx),
                                    reads=[f"WD{slot}", f"AT{c}_{h}"], writes=[f"ps{bank}"], signal=(idx == nmm - 1))
                            P.op("dve", B(lambda e, j, h, bank: e.scalar_tensor_tensor(
                                out=XT[:, j, h * 512:(h + 1) * 512], in0=banks[bank][:, :], scalar=0.5,
                                in1=XT[:, j, h * 512:(h + 1) * 512], op0=ALU.mult, op1=ALU.add), j, h, bank),
                                reads=[f"ps{bank}", f"XT{j}"], writes=[f"XT{j}"])
                if final:
                    emit_norm(XT, None, SQ, RSTD, TG, 12, [6, 7], "XT", None, inplace=True)
                    dstv, dres = outv, "od"
                else:
                    dstv, dres = xdv, "xd"
                LAGC = 3
                for c in range(NCH + LAGC):
                    if c < NCH:
                        store_chunk(tg, c, dstv, dres)
                    if tg + 1 < NTG_F and c >= LAGC:
                        load_chunk(tg + 1, c - LAGC)
            issue_cvt(len(cvt))
            P.alias_group(cvt_ops)
            CVT_DONE[l] = cvt_ops[-1] if cvt_ops else None

        MIX = {}
        CVT_DONE = {}

        def B(f, *a):
            return lambda e: f(e, *a)

        AX = mybir.AxisListType.X

        def emit_mix(l, x_src, x_dst):
            TG = 512
            NTG = NTOK // TG
            al = Alloc(arena, 0, ARENA_BYTES)
            CAT = al([128, NCH, NTOK], BF16)
            ZT = al([128, 4, NBLK, 130], BF16)
            base = al.off
            pre = ("ffn1", l) in sublayers and int(os.environ.get('KDBG', '9')) >= 9
            win = (winb if pre else W[("in", l)]).rearrange("(k p) n -> p k n", p=128)
            wout = (woutb if pre else W[("out", l)]).rearrange("(k p) n -> p k n", p=128)
            xsv = x_src.rearrange("(c p) t -> p c t", p=128)
            xdv = x_dst.rearrange("(c p) t -> p c t", p=128)
            linit = lambda_init(l)
            vsrc = ag_src[1024:2048, :].rearrange("r (two f) -> (r two) f", two=2)
            tsrc = ag_src[2048:2056, :].rearrange("r (a f) -> (r a) f", f=32).rearrange("(ch p) f -> p ch f", p=128)

            smalls = []

            def small(nm, t, d):
                smalls.append(P.dma("sp", "D_const", B(lambda e, t, d: e.dma_start(out=t, in_=d), t, d), writes=[nm]))
            small("gvb", gvb, W[("gv", l)].partition_broadcast(128))
            small("gsub", gsub, W[("gsub", l)].partition_broadcast(128))
            small("lamb", lamb.rearrange("p a b -> p (a b)"), W[("lq", l)].partition_broadcast(128))
            small("bsrow32", bsrow32[0:1, :], W[("bs", l)])
            small("wconv", wconv.rearrange("p a b -> p (a b)"), W[("wc", l)])
            small("wsnat", wsnat, W[("ws", l)].rearrange("g t s -> t g s"))
            P.alias_group(smalls)
            P.op("dve", lambda e: e.tensor_scalar(out=gsub, in0=gsub, scalar1=float(1.0 - linit), scalar2=None,
                                                  op0=ALU.mult), reads=["gsub"], writes=["gsub"])
            P.op("act", lambda e: e.activation(out=bsrow[0:1, :], in_=bsrow32[0:1, :], func=AF.Copy),
                 reads=["bsrow32"], writes=["bsrow"])
            P.op("dve", lambda e: e.tensor_tensor(out=lamprod[:, 0:2, :], in0=lamb[:, 0:4:2, :], in1=lamb[:, 1:4:2, :],
                                                  op=ALU.mult), reads=["lamb"], writes=["lamprod"])
            P.op("dve", lambda e: e.reduce_sum(out=lamcol[:, 0:2], in_=lamprod[:, 0:2, :], axis=AX),
                 reads=["lamprod"], writes=["lamcol"])
            P.op("act", lambda e: e.activation(out=lamcol[:, 2:4], in_=lamcol[:, 0:2], func=AF.Exp),
                 reads=["lamcol"], writes=["lamcol"])
            P.op("dve", lambda e: e.tensor_tensor(out=lamcol[:, 4:5], in0=lamcol[:, 2:3], in1=lamcol[:, 3:4],
                                                  op=ALU.subtract), reads=["lamcol"], writes=["lamcol"])
            P.op("dve", lambda e: e.tensor_scalar(out=lamcol[:, 5:6], in0=lamcol[:, 4:5], scalar1=float(linit),
                                                  scalar2=None, op0=ALU.add), reads=["lamcol"], writes=["lamcol"])
            for g in range(4):
                P.op("pe", B(lambda e, g: e.transpose(out=banks[6][:, g * 128:(g + 1) * 128], in_=wsnat[:, g, :],
                                                      identity=identf), g),
                     reads=["wsnat", "identf"], writes=["ps6"])
            for g in range(4):
                P.op("dve", B(lambda e, g: e.tensor_tensor(out=wst[:, g, :], in0=banks[6][:, g * 128:(g + 1) * 128],
                                                           in1=tril, op=ALU.mult), g),
                     reads=["ps6", "tril"], writes=["wst"])

            XT = carve(arena, 4 * NTOK * 2, [128, NCH, TG], F32)
            HT2 = [al([128, NCH, TG], BF16) for _ in range(2)]
            WSALL = al([128, 4, NCH, 256], BF16)
            WS = [WSALL[:, i, :, :] for i in range(4)]
            UT = al([128, 4, TG], BF16)
            SQ = al([128, 2, TG], BF16)
            RSTD = al([128, TG], F32)
            QKST = al([128, 3, TG], BF16)
            CZ = al([128, 4, TG], BF16)
            V32 = al([128, 4, 512], F32)
            VN = al([128, 2, 512], BF16)
            VST = al([128, 1, 4, 1024], BF16)
            TAILS = al([128, 4, NBLK, 2], BF16)
            SSV = al([128, 16], F32)
            JUNK = al([128, 512], BF16)
            m1_end = al.off
            fm_banks = [0, 1, 2, 3]
            tm_banks = [4, 5]
            CG_ORDER = list(range(INCOLS // 256))
            KMIX = int(os.environ.get("KMIX", "9"))

            def issue_ag(parts, res):
                for pi in parts:
                    r0, r1 = AG_PARTS[pi]
                    P.collective("CC_ag", B(lambda e, pi, r0, r1: e.collective_compute(
                        "AllGather", ALU.bypass, replica_groups=REPLICA_GROUPS,
                        ins=[ag_src[r0:r1, :].opt()], outs=[ag_dstp[pi].opt()]), pi, r0, r1),
                        reads=list(res), writes=[f"agdst{pi}"])
            cnt = {"fm": 0, "tm": 0, "ws": 0, "st": 0, "vn": 0}
            agsrc_res = []
            def m1_load(tg):
                t0 = tg * TG
                grp = []
                for q in range(2):
                    grp.append(P.dma("sp", "D_xt", B(lambda e, q, t0: e.dma_start(
                        out=XT[:, q * 8:(q + 1) * 8, :], in_=xsv[:, q * 8:(q + 1) * 8, t0:t0 + TG]), q, t0),
                        reads=["xdram"], writes=[f"XT{c}" for c in range(q * 8, q * 8 + 8)]))
                P.alias_group(grp)

            def m1_norm(tg):
                emit_norm(XT, HT2[tg % 2], SQ, RSTD, TG, 4 + l, [7], "XT", f"HT{tg % 2}_")

            m1_load(0)
            m1_norm(0)
            for tg in range(NTG):
                t0 = tg * TG
                HT = HT2[tg % 2]
                hb = tg % 2
                for cg in CG_ORDER:
                    if cg == 1 and tg + 1 < NTG:
                        m1_load(tg + 1)
                    if cg == 10 and tg + 1 < NTG:
                        m1_norm(tg + 1)
                    slot = cg % 4
                    P.dma("pool", f"D_ws{slot}", B(lambda e, slot, cg: e.dma_start(
                        out=WS[slot], in_=win[:, :, cg * 256:(cg + 1) * 256]), slot, cg), writes=[f"WS{slot}"])
                    ch0 = 2 * cg
                    is_tm = (4 <= ch0 < 8) or (24 <= ch0 < 32)
                    if is_tm and cg % 2 == 0:
                        continue
                    if not is_tm:
                        for cc in range(2):
                            ch = ch0 + cc
                            bank = fm_banks[cnt["fm"] % 4]
                            cnt["fm"] += 1
                            for k in range(NCH):
                                P.op("pe", B(lambda e, slot, k, cc, bank, HT: e.matmul(
                                    banks[bank][:, :], lhsT=WS[slot][:, k, cc * 128:(cc + 1) * 128], rhs=HT[:, k, :],
                                    start=(k == 0), stop=(k == NCH - 1)), slot, k, cc, bank, HT),
                                    reads=[f"WS{slot}", f"HT{hb}_{k}"], writes=[f"ps{bank}"], signal=(k == NCH - 1))
                            if ch < 4:
                                P.op("act", B(lambda e, ch, bank: e.activation(out=UT[:, ch, :], in_=banks[bank][:, :],
                                                                                func=AF.Copy), ch, bank),
                                     reads=[f"ps{bank}"], writes=[f"UT{ch}"])
                            elif ch < 16:
                                h = ch - 8
                                s = cnt["st"] % 3
                                cnt["st"] += 1
                                P.op("act", B(lambda e, s, bank, h: e.activation(out=QKST[:, s, :], in_=banks[bank][:, :],
                                                                                  func=AF.Copy, scale=float(2.0 ** (h - 2))),
                                              s, bank, h),
                                     reads=[f"ps{bank}"], writes=[f"QKST{s}"])
                                P.dma("sp", f"D_qkst{s}", B(lambda e, s, h, t0: e.dma_start(
                                    out=qs[h * 128:(h + 1) * 128, t0:t0 + TG], in_=QKST[:, s, :]), s, h, t0),
                                    reads=[f"QKST{s}"], writes=[f"qs{h}_{tg}"])
                            elif ch < 24:
                                h = ch - 16
                                s = cnt["st"] % 3
                                cnt["st"] += 1
                                P.op("dve", B(lambda e, s, bank: e.tensor_copy(out=QKST[:, s, :], in_=banks[bank][:, :]),
                                              s, bank),
                                     reads=[f"ps{bank}"], writes=[f"QKST{s}"])
                                P.dma("sp", f"D_qkst{s}", B(lambda e, s, h, t0: e.dma_start(
                                    out=ag_src[512 * (h // 2) + 128 * (h % 2):512 * (h // 2) + 128 * (h % 2) + 128, t0:t0 + TG],
                                    in_=QKST[:, s, :]), s, h, t0),
                                    reads=[f"QKST{s}"], writes=[f"agk{h}_{tg}"])
                                agsrc_res.append(f"agk{h}_{tg}")
                            elif ch < 36:
                                i = ch - 32
                                P.op("act", B(lambda e, i, bank, t0: e.activation(out=CAT[:, 12 + i, t0:t0 + TG],
                                                                                   in_=banks[bank][:, :], func=AF.Copy),
                                              i, bank, t0),
                                     reads=[f"ps{bank}"], writes=[f"CAT{12 + i}_{tg}"])
                            elif ch < 40:
                                i = ch - 36
                                P.op("act", B(lambda e, i, bank: e.activation(out=CZ[:, i, :], in_=banks[bank][:, :],
                                                                               func=AF.Copy), i, bank),
                                     reads=[f"ps{bank}"], writes=[f"CZ{i}"])
                            else:
                                i = ch - 40
                                P.op("dve", B(lambda e, i, bank, tg: e.tensor_tensor(
                                    out=ZT[:, i, 4 * tg:4 * tg + 4, 2:130],
                                    in0=CZ[:, i, :].rearrange("p (a b) -> p a b", a=4),
                                    in1=banks[bank][:, :].rearrange("p (a b) -> p a b", a=4), op=ALU.mult), i, bank, tg),
                                    reads=[f"CZ{i}", f"ps{bank}"], writes=[f"ZT{i}_{tg}"])
                    else:
                        s0 = slot - 1
                        for tb in range(4):
                            bank = tm_banks[cnt["tm"] % 2]
                            cnt["tm"] += 1
                            for k in range(NCH):
                                P.op("pe", B(lambda e, s0, k, tb, bank, HT: e.matmul(
                                    banks[bank][:, :], lhsT=HT[:, k, tb * 128:(tb + 1) * 128], rhs=WSALL[:, s0:s0 + 2, k, :],
                                    start=(k == 0), stop=(k == NCH - 1)), s0, k, tb, bank, HT),
                                    reads=[f"WS{s0}", f"WS{s0 + 1}", f"HT{hb}_{k}"], writes=[f"ps{bank}"],
                                    signal=(k == NCH - 1))
                            if ch0 < 8:
                                P.op("act", B(lambda e, tb, bank: e.activation(out=V32[:, tb, :], in_=banks[bank][:, :],
                                                                               func=AF.Copy), tb, bank),
                                     reads=[f"ps{bank}"], writes=[f"V32_{tb}"])
                            else:
                                c0 = (ch0 - 2 - 24) * 128
                                vs = 0
                                P.op("dve", B(lambda e, vs, tb, c0, bank: e.tensor_copy(out=VST[:, vs, tb, c0:c0 + 512],
                                                                                         in_=banks[bank][:, :]),
                                              vs, tb, c0, bank),
                                     reads=[f"ps{bank}"], writes=[f"VST{vs}_{tb}_{c0}"])
                        if ch0 == 6:
                            for tb in range(4):
                                blk = 4 * tg + tb
                                P.op("act", B(lambda e, tb: e.activation(out=JUNK, in_=V32[:, tb, :], func=AF.Square,
                                                                         accum_out=SSV[:, tb:tb + 1]), tb),
                                     reads=[f"V32_{tb}"], writes=["JUNK", f"SSV{tb}"])
                                P.op("act", B(lambda e, tb: e.activation(out=SSV[:, 4 + tb:5 + tb], in_=SSV[:, tb:tb + 1],
                                                                         func=AF.Sqrt, scale=1.0 / 512.0,
                                                                         bias=epscol[:, 0:1]), tb),
                                     reads=[f"SSV{tb}", "epscol"], writes=[f"SSR{tb}"])
                                P.op("dve", B(lambda e, tb: e.reciprocal(out=SSV[:, 8 + tb:9 + tb],
                                                                         in_=SSV[:, 4 + tb:5 + tb]), tb),
                                     reads=[f"SSR{tb}"], writes=[f"SSI{tb}"])
                                vn = cnt["vn"] % 2
                                cnt["vn"] += 1
                                P.op("dve", B(lambda e, tb, vn: e.scalar_tensor_tensor(
                                    out=VN[:, vn, :], in0=V32[:, tb, :], scalar=SSV[:, 8 + tb:9 + tb], in1=gvb,
                                    op0=ALU.mult, op1=ALU.mult), tb, vn),
                                    reads=[f"V32_{tb}", f"SSI{tb}", "gvb"], writes=[f"VN{vn}"])
                                for g in range(4):
                                    P.op("pe", B(lambda e, vn, g: e.matmul(
                                        banks[6][:, g * 128:(g + 1) * 128], lhsT=VN[:, vn, g * 128:(g + 1) * 128],
                                        rhs=wst[:, g, :], start=(g == 0), stop=False, skip_group_check=True), vn, g),
                                        reads=[f"VN{vn}", "wst"], writes=["ps6"], signal=False)
                                    P.op("pe", B(lambda e, g: e.matmul(
                                        banks[6][:, g * 128:(g + 1) * 128], lhsT=ones_bf[0:1, 0:128],
                                        rhs=bsrow[0:1, g * 128:(g + 1) * 128], start=False, stop=(g == 3),
                                        skip_group_check=True), g),
                                        reads=["bsrow", "ones"], writes=["ps6"], signal=(g == 3))
                                P.op("dve", B(lambda e, tb, blk: e.tensor_tensor(
                                    out=CAT[:, 0:4, blk * 128:(blk + 1) * 128], in0=UT[:, 0:4, tb * 128:(tb + 1) * 128],
                                    in1=banks[6][:, :].rearrange("p (a b) -> p a b", a=4), op=ALU.mult), tb, blk),
                                    reads=["ps6"] + [f"UT{i}" for i in range(4)], writes=[f"CATA_{blk}"])
                        if ch0 == 30:
                            vs = 0
                            grp = []
                            for pr in range(4):
                                grp.append(P.dma("sp", f"D_vst{vs}", B(lambda e, vs, t0, pr: e.dma_start(
                                    out=ag_src[512 * pr + 256:512 * pr + 512, :].rearrange("r (n f) -> (r n) f", f=256)[
                                        t0:t0 + TG, :].rearrange("(tb p) f -> p tb f", p=128),
                                    in_=VST[:, vs, :, 256 * pr:256 * pr + 256]), vs, t0, pr),
                                    reads=[f"VST{vs}_{tb}_{c0}" for tb in range(4) for c0 in (0, 512)],
                                    writes=[f"agv_{tg}_{pr}"]))
                            P.alias_group(grp)
                            agsrc_res.append(f"agv_{tg}")

                for i in range(4):
                    P.op("dve", B(lambda e, tg, i: e.tensor_copy(out=TAILS[:, i, 4 * tg:4 * tg + 4, :],
                                                                 in_=ZT[:, i, 4 * tg:4 * tg + 4, 128:130]), tg, i),
                         reads=[f"ZT{i}_{tg}"], writes=[f"TAILS{tg}_{i}"])
            P.dma("sp", "D_tail", lambda e: e.dma_start(out=tsrc, in_=TAILS.rearrange("p c b j -> p c (b j)")),
                  reads=[f"TAILS{tg}_{i}" for tg in range(NTG) for i in range(4)], writes=["agtail"])
            agsrc_res.append("agtail")

            P.barrier()
            issue_ag([0, 1, 2, 3, 4], [])
            al.off = base
            KH = [[al([128, SEQ], BF16) for m in range(2)] for b in range(2)]
            QH = [[al([128, NTOK], BF16) for m in range(2)] for b in range(2)]
            VH = [al([128, 32, 130], BF16) for b in range(2)]
            PT = al([128, 6, 512], BF16)
            RL = al([128, 2, 4], F32)
            T1 = al([128, 2, 128], F32)
            OSB = al([128, 2, 128], F32)
            SQJ = al([128, 2, 128], F32)
            SSQ = al([128, 2, 4], F32)
            YB = al([128, 8, 128], BF16)
            TA = al([128, 4, 32], BF16)
            TB = al([128, 4, 32], BF16)
            TMPH = al([128, 4, NBLK, 2], F32)
            ACC = al([128, NBLK, 128], F32)
            ACC2 = al([128, NBLK, 128], F32)
            TMPH2 = al([128, 4, NBLK, 2], F32)
            assert al.off <= ARENA_BYTES

            augs = []
            for b in range(2 if KMIX >= 3 else 0):
                for m in range(2):
                    augs.append(P.dma("sp", "D_const", B(lambda e, b, m: e.dma_start(out=KH[b][m][64:68, :], in_=kaug_d), b, m),
                                      writes=[f"KHaug{b}"]))
                    augs.append(P.dma("sp", "D_const", B(lambda e, b, m: e.dma_start(out=QH[b][m][64:68, :], in_=qaug_d), b, m),
                                      writes=[f"QHaug{b}"]))
            P.alias_group(augs)
            for b in range(2 if KMIX >= 3 else 0):
                P.op("dve", B(lambda e, b: e.memset(VH[b][:, :, 128:129], 1.0), b), writes=[f"VHaug{b}"])

            def emit_conv_prep():
                def tview(r):
                    return ag_dstp[4][r * 8:(r + 1) * 8, :].rearrange(
                        "r (a f) -> (r a) f", f=32).rearrange("(ch p) f -> p ch f", p=128)
                P.dma("sp", "D_ta", lambda e: e.dma_start(out=TA, in_=tview(0)), reads=["agdst4"], writes=["TA"])
                P.dma("sp", "D_tb", lambda e: e.dma_start(out=TB, in_=tview(1)), reads=["agdst4"], writes=["TB"])
                TB4 = TB.rearrange("p c (b j) -> p c b j", j=2)
                P.op("dve", lambda e: e.tensor_scalar(out=TMPH.rearrange("p c b j -> p (c b j)"),
                                                      in0=TA.rearrange("p c f -> p (c f)"), scalar1=sel[:, 0:1],
                                                      scalar2=None, op0=ALU.mult),
                     reads=["TA", "sel"], writes=["TMPH"])
                for i in range(4):
                    P.op("dve", B(lambda e, i: e.scalar_tensor_tensor(out=ZT[:, i, 1:NBLK, 0:2], in0=TB4[:, i, 0:NBLK - 1, :],
                                                                      scalar=sel[:, 1:2], in1=TMPH[:, i, 1:NBLK, :],
                                                                      op0=ALU.mult, op1=ALU.add), i),
                         reads=["TB", "TMPH", "sel"], writes=[f"ZTh{i}"])
                P.op("dve", lambda e: e.tensor_copy(out=ZT[:, :, 0, 0:2], in_=TMPH[:, :, 0, :]),
                     reads=["TMPH"], writes=["ZTh0"])

            def emit_conv_chunk(i):
                zres = ["ZTh0"] + [f"ZTh{i}" for i in range(4)]
                P.op("dve", B(lambda e, i: e.tensor_scalar(out=ACC, in0=ZT[:, i, :, 2:130], scalar1=wconv[:, 2, i:i + 1],
                                                           scalar2=None, op0=ALU.mult), i),
                     reads=zres + ["wconv"], writes=["ACC"])
                P.op("dve", B(lambda e, i: e.scalar_tensor_tensor(out=ACC, in0=ZT[:, i, :, 1:129],
                                                                  scalar=wconv[:, 1, i:i + 1], in1=ACC,
                                                                  op0=ALU.mult, op1=ALU.add), i),
                     reads=zres + ["ACC", "wconv"], writes=["ACC"])
                P.op("dve", B(lambda e, i: e.scalar_tensor_tensor(out=ACC, in0=ZT[:, i, :, 0:128],
                                                                  scalar=wconv[:, 0, i:i + 1], in1=ACC,
                                                                  op0=ALU.mult, op1=ALU.add), i),
                     reads=zres + ["ACC", "wconv"], writes=["ACC"])
                P.op("dve", B(lambda e, i: e.tensor_tensor(
                    out=CAT[:, 12 + i, :].rearrange("p (b t) -> p b t", t=128), in0=ACC,
                    in1=CAT[:, 12 + i, :].rearrange("p (b t) -> p b t", t=128), op=ALU.mult), i),
                    reads=["ACC"], writes=[f"CATC{i}"])

            def emit_conv():
                emit_conv_prep()
                for i in range(4):
                    emit_conv_chunk(i)

            def vview(r, pr):
                return ag_dstp[pr][r * 512 + 256:r * 512 + 512, :].rearrange("r (n f) -> (r n) f", f=256)

            def load_head(h):
                b = h % 2
                pr = h // 2
                grp = []
                for m in range(2):
                    for r in range(2):
                        row0 = r * 512 + 128 * (h % 2) + 64 * m
                        grp.append(P.dma("sp", f"D_kh{b}", B(lambda e, b, m, r, pr, row0: e.dma_start(
                            out=KH[b][m][0:64, r * NTOK:(r + 1) * NTOK], in_=ag_dstp[pr][row0:row0 + 64, :]),
                            b, m, r, pr, row0),
                            reads=[f"agdst{pr}"], writes=[f"KH{b}"]))
                P.alias_group(grp)
                grp = []
                for m in range(2):
                    row0 = h * 128 + m * 64
                    grp.append(P.dma("sp", f"D_qh{b}", B(lambda e, b, m, row0: e.dma_start(
                        out=QH[b][m][0:64, :], in_=qs[row0:row0 + 64, :]), b, m, row0),
                        reads=[f"qs{h}_{tg}" for tg in range(NTG)], writes=[f"QH{b}"]))
                P.alias_group(grp)
                grp = []
                for r in range(2):
                    for half in range(2):
                        grp.append(P.dma("sp", f"D_vh{b}", B(lambda e, b, r, h, half, pr: e.dma_start(
                            out=VH[b][:, r * 16 + half * 8:r * 16 + half * 8 + 8, 0:128],
                            in_=vview(r, pr)[half * 1024:(half + 1) * 1024, (h % 2) * 128:(h % 2) * 128 + 128].rearrange(
                                "(t p) f -> p t f", p=128)), b, r, h, half, pr),
                            reads=[f"agdst{pr}"], writes=[f"VH{b}"]))
                P.alias_group(grp)

            s_banks = [4, 5, 6, 7]
            cs = {"s": 0, "pt": 0, "sm": 0, "yb": 0, "unit": 0}
            tq = []
            TLAG = 10

            def flush_t(force=False):
                while tq and (force or cs["unit"] - tq[0][0] >= TLAG):
                    _, yb, h, t = tq.pop(0)
                    sb = s_banks[cs["s"] % 4]
                    cs["s"] += 1
                    P.op("pe", B(lambda e, yb, sb: e.transpose(out=banks[sb][:, 0:64].bitcast(BF16), in_=YB[:, yb, :],
                                                                identity=ident), yb, sb),
                         reads=[f"YB{yb}", "ident"], writes=[f"ps{sb}"], signal=True)
                    P.op("dve", B(lambda e, sb, h, t: e.tensor_copy(out=CAT[:, 4 + h, t * 128:(t + 1) * 128],
                                                                    in_=banks[sb][:, 0:64].bitcast(BF16)), sb, h, t),
                         reads=[f"ps{sb}"], writes=[f"CATB{h}_{t}"])

            def attn_head(h):
                b = h % 2
                slope = 2.0 ** (-(h + 1))
                for qg in range(4):
                    units = []
                    for tp in range(4 * qg + 4):
                        for r in range(2):
                            for m in range(2):
                                units.append((tp, r, m))
                    first_o = [True] * 4
                    pend = []

                    def qk(u):
                        tp, r, m = u
                        kt = r * 16 + tp
                        tmin = max(tp, 4 * qg)
                        n = (4 * qg + 4 - tmin) * 128
                        c0 = tmin * 128
                        sb = s_banks[cs["s"] % 4]
                        cs["s"] += 1
                        diag = tp >= 4 * qg
                        P.op("pe", B(lambda e, b, m, kt, c0, n, sb, diag: e.matmul(
                            banks[sb][:, 0:n], lhsT=KH[b][m][0:68, kt * 128:(kt + 1) * 128],
                            rhs=QH[b][m][0:68, c0:c0 + n], start=True, stop=(not diag)), b, m, kt, c0, n, sb, diag),
                            reads=[f"KH{b}", f"KHaug{b}", f"QH{b}", f"QHaug{b}"], writes=[f"ps{sb}"], signal=(not diag))
                        if diag:
                            mk = maskA if r == 0 else maskB
                            P.op("pe", B(lambda e, sb, mk: e.matmul(banks[sb][:, 0:128], lhsT=ident, rhs=mk,
                                                                     start=False, stop=True), sb, mk),
                                 reads=["ident", "maskA", "maskB"], writes=[f"ps{sb}"], signal=True)
                        pt = cs["pt"] % 6
                        cs["pt"] += 1
                        P.op("act", B(lambda e, pt, sb, n: e.activation(out=PT[:, pt, 0:n], in_=banks[sb][:, 0:n],
                                                                         func=AF.Exp, scale=float(slope)), pt, sb, n),
                             reads=[f"ps{sb}"], writes=[f"PT{pt}"])
                        return (u, pt, tmin, n)

                    def pv(info):
                        (tp, r, m), pt, tmin, n = info
                        kt = r * 16 + tp
                        cs["unit"] += 1
                        flush_t()
                        for t in range(tmin, 4 * qg + 4):
                            ob = t - 4 * qg
                            st = first_o[ob]
                            first_o[ob] = False
                            last = (tp == t and r == 1)
                            off = (t - tmin) * 128
                            P.op("pe", B(lambda e, ob, m, pt, off, b, kt, st, last: e.matmul(
                                banks[ob][:, m * 129:(m + 1) * 129], lhsT=PT[:, pt, off:off + 128],
                                rhs=VH[b][:, kt, 0:129], start=st, stop=last, skip_group_check=True),
                                ob, m, pt, off, b, kt, st, last),
                                reads=[f"PT{pt}", f"VH{b}", f"VHaug{b}"], writes=[f"ps{ob}"],
                                signal=(last or t == 4 * qg + 3))
                            if last and m == 1:
                                post(h, t, ob)

                    def post(h, t, ob):
                        sm = cs["sm"] % 2
                        cs["sm"] += 1
                        bk = banks[ob]
                        P.op("dve", B(lambda e, sm, bk: e.reciprocal(out=RL[:, sm, 0:2], in_=bk[:, 128:258:129]), sm, bk),
                             reads=[f"ps{ob}"], writes=[f"RL{sm}"])
                        P.op("dve", B(lambda e, sm: e.tensor_tensor(out=RL[:, sm, 2:3], in0=RL[:, sm, 1:2],
                                                                    in1=lamcol[:, 5:6], op=ALU.mult), sm),
                             reads=[f"RL{sm}", "lamcol"], writes=[f"RLb{sm}"])
                        P.op("dve", B(lambda e, sm, bk: e.tensor_scalar(out=T1[:, sm, :], in0=bk[:, 129:257],
                                                                        scalar1=RL[:, sm, 2:3], scalar2=None,
                                                                        op0=ALU.mult), sm, bk),
                             reads=[f"ps{ob}", f"RLb{sm}"], writes=[f"T1{sm}"])
                        P.op("dve", B(lambda e, sm, bk: e.scalar_tensor_tensor(out=OSB[:, sm, :], in0=bk[:, 0:128],
                                                                               scalar=RL[:, sm, 0:1], in1=T1[:, sm, :],
                                                                               op0=ALU.mult, op1=ALU.subtract), sm, bk),
                             reads=[f"ps{ob}", f"RL{sm}", f"T1{sm}"], writes=[f"OSB{sm}"])
                        P.op("dve", B(lambda e, sm: e.tensor_tensor(out=SQJ[:, sm, :], in0=OSB[:, sm, :],
                                                                    in1=OSB[:, sm, :], op=ALU.mult), sm),
                             reads=[f"OSB{sm}"], writes=[f"SQJ{sm}"])
                        P.op("dve", B(lambda e, sm: e.reduce_sum(out=SSQ[:, sm, 0:1], in_=SQJ[:, sm, :], axis=AX), sm),
                             reads=[f"SQJ{sm}"], writes=[f"SSQ{sm}"])
                        P.op("act", B(lambda e, sm: e.activation(out=SSQ[:, sm, 1:2], in_=SSQ[:, sm, 0:1], func=AF.Ln,
                                                                 scale=1.0 / 128.0, bias=epscol[:, 0:1]), sm),
                             reads=[f"SSQ{sm}", "epscol"], writes=[f"SSL{sm}"])
                        P.op("act", B(lambda e, sm: e.activation(out=SSQ[:, sm, 2:3], in_=SSQ[:, sm, 1:2], func=AF.Exp,
                                                                 scale=-0.5), sm),
                             reads=[f"SSL{sm}"], writes=[f"SSE{sm}"])
                        yb = cs["yb"] % 8
                        cs["yb"] += 1
                        P.op("dve", B(lambda e, sm, yb: e.scalar_tensor_tensor(out=YB[:, yb, :], in0=OSB[:, sm, :],
                                                                               scalar=SSQ[:, sm, 2:3], in1=gsub,
                                                                               op0=ALU.mult, op1=ALU.mult), sm, yb),
                             reads=[f"OSB{sm}", f"SSE{sm}", "gsub"], writes=[f"YB{yb}"])
                        tq.append((cs["unit"], yb, h, t))

                    SKEW = 3
                    for i, u in enumerate(units):
                        pend.append(qk(u))
                        if len(pend) > SKEW:
                            pv(pend.pop(0))
                    while pend:
                        pv(pend.pop(0))

            NHEADS = 8 if KMIX >= 9 else (KMIX - 3 if KMIX >= 4 else 0)
            if NHEADS:
                load_head(0)
            for h in range(NHEADS):
                if h + 1 < NHEADS:
                    load_head(h + 1)
                attn_head(h)
                if NHEADS >= 8:
                    if h == 1:
                        emit_conv_prep()
                    elif 2 <= h <= 5:
                        emit_conv_chunk(h - 2)
            if NHEADS < 8:
                emit_conv()
            flush_t(force=True)

            P.barrier()
            al.off = base
            XT6 = [al([128, NCH, TG], F32) for _ in range(2)]
            WS6 = [al([128, NCH, 256], BF16) for _ in range(3)]
            c6 = {"ws": 0, "bk": 0}
            for tg in range(NTG):
                t0 = tg * TG
                xb = tg % 2
                X6 = XT6[xb]
                grp = []
                for q in range(2):
                    grp.append(P.dma("sp", f"D_xt6{xb}", B(lambda e, q, t0, X6: e.dma_start(
                        out=X6[:, q * 8:(q + 1) * 8, :], in_=xsv[:, q * 8:(q + 1) * 8, t0:t0 + TG]), q, t0, X6),
                        reads=[f"xdram{tg}"], writes=[f"X6{xb}_{c}" for c in range(q * 8, q * 8 + 8)]))
                P.alias_group(grp)
                for cg in range(D // 256):
                    slot = c6["ws"] % 3
                    c6["ws"] += 1
                    P.dma("pool", f"D_ws{slot}", B(lambda e, slot, cg: e.dma_start(
                        out=WS6[slot][:, :, :], in_=wout[:, :, cg * 256:(cg + 1) * 256]), slot, cg), writes=[f"WS{slot}"])
                    for cc in range(2):
                        j = 2 * cg + cc
                        bank = c6["bk"] % 4
                        c6["bk"] += 1
                        for k in range(NCH):
                            P.op("pe", B(lambda e, slot, k, cc, bank, t0: e.matmul(
                                banks[bank][:, :], lhsT=WS6[slot][:, k, cc * 128:(cc + 1) * 128],
                                rhs=CAT[:, k, t0:t0 + TG], start=(k == 0), stop=(k == NCH - 1)), slot, k, cc, bank, t0),
                                reads=[f"WS{slot}"], writes=[f"ps{bank}"], signal=(k == NCH - 1))
                        P.op("dve", B(lambda e, j, bank, X6: e.tensor_tensor(out=X6[:, j, :], in0=banks[bank][:, :],
                                                                             in1=X6[:, j, :], op=ALU.add), j, bank, X6),
                             reads=[f"ps{bank}", f"X6{xb}_{j}"], writes=[f"X6{xb}_{j}"])
                grp = []
                for q in range(2):
                    grp.append(P.dma("sp", f"D_xst6{xb}", B(lambda e, q, t0, X6: e.dma_start(
                        out=xdv[:, q * 8:(q + 1) * 8, t0:t0 + TG], in_=X6[:, q * 8:(q + 1) * 8, :]), q, t0, X6),
                        reads=[f"X6{xb}_{c}" for c in range(q * 8, q * 8 + 8)], writes=[f"xdram{tg}"]))
                P.alias_group(grp)

        MIX["emit"] = emit_mix

        setup()
        P.barrier()
        cur_src = xT_in
        n_sub = len(sublayers)
        for i, (kind, l) in enumerate(sublayers):
            last = (i == n_sub - 1)
            if kind in ("ffn1", "ffn2"):
                emit_ffn(l, 1 if kind == "ffn1" else 2, cur_src, xs, final=(last and final_norm))
            else:
                MIX["emit"](l, cur_src, xs)
            cur_src = xs
            P.barrier()
        if not final_norm:
            al = Alloc(arena, 0, ARENA_BYTES)
            XT = al([128, NCH, 1024], F32)
            xsv = xs.rearrange("(c p) t -> p c t", p=128)
            outv = outT.rearrange("(c p) t -> p c t", p=128)
            for tg in range(2):
                t0 = tg * 1024
                P.dma("sp", "D_xt", (lambda t0: lambda e: e.dma_start(out=XT, in_=xsv[:, :, t0:t0 + 1024]))(t0),
                      reads=["xdram"], writes=["XTall"])
                P.dma("sp", "D_xst", (lambda t0: lambda e: e.dma_start(out=outv[:, :, t0:t0 + 1024], in_=XT))(t0),
                      reads=["XTall"], writes=["outdram"])
            P.barrier()

        P.barrier()
        P.finalize()

        sems = {}
        for k in P.sem_names:
            sems[k] = ctx.enter_context(nc.semaphore(k))
        block = ctx.enter_context(nc.Block())

        @block.tensor
        def _(e):
            _emit_engine(P, "pe", e, sems)

        @block.scalar
        def _(e):
            _emit_engine(P, "act", e, sems)

        @block.vector
        def _(e):
            _emit_engine(P, "dve", e, sems)

        @block.gpsimd
        def _(e):
            _emit_engine(P, "pool", e, sems)

        @block.sync
        def _(e):
            _emit_engine(P, "sp", e, sems)

    nc._planner_stats = {e: len(P.ops[e]) for e in P.ENGS}
    nc._sem_counts = dict(P.sem_count)
    return nc


def core_token_index(r):
    t = np.arange(NBLK)
    return ((2 * t[:, None] + r) * 128 + np.arange(128)[None, :]).reshape(-1)


def make_common_inputs(inputs, sublayers):
    f32 = np.float32
    com = {}
    g = np.zeros((13, D), f32)
    g[0:4] = np.asarray(inputs["g_ffn1"], f32)
    g[4:8] = np.asarray(inputs["g_mix"], f32)
    g[8:12] = np.asarray(inputs["g_ffn2"], f32)
    g[12] = np.asarray(inputs["g_final"], f32)
    com["gcols"] = np.ascontiguousarray(g.reshape(13, NCH, 128).transpose(2, 0, 1).reshape(128, 13 * NCH))
    if any(k == "mix" for k, _ in sublayers):
        bf = ml_dtypes.bfloat16
        com["ident"] = np.eye(128, dtype=f32).astype(bf)
        com["identf"] = np.eye(128, dtype=f32)
        ii = np.arange(128)
        com["tril01"] = (ii[:, None] <= ii[None, :]).astype(f32)
        n = np.arange(SEQ)
        rr, tt, i_ = n // NTOK, (n % NTOK) // 128, n % 128
        com["kaug"] = np.stack([np.ones(SEQ), np.ones(SEQ), 2 * tt + rr, i_]).astype(f32).astype(bf)
    for kind, l in sublayers:
        if kind == "mix":
            com[f"win_{l}"] = np.asarray(inputs["w_in"][l], f32)
            com[f"wout_{l}"] = np.asarray(inputs["w_out"][l], f32)
            com[f"gv_{l}"] = np.asarray(inputs["g_sga_v"][l], f32).reshape(1, 512)
            com[f"ws_{l}"] = np.asarray(inputs["w_sga_s"][l], f32)
            com[f"bs_{l}"] = np.asarray(inputs["b_sga_s"][l], f32).reshape(1, 512)
            com[f"lq_{l}"] = np.asarray(inputs["lambda_qk"][l], f32).reshape(1, 256)
            com[f"gsub_{l}"] = np.asarray(inputs["g_diff_sub"][l], f32).reshape(1, 128)
            com[f"wc_{l}"] = np.ascontiguousarray(
                np.asarray(inputs["w_conv"][l], f32).reshape(3, 4, 128).transpose(2, 0, 1).reshape(128, 12))
    for kind, l in sublayers:
        if kind == "ffn1":
            com[f"w1g_{l}"] = np.asarray(inputs["w_ffn1_gate"][l], f32)
            com[f"w1u_{l}"] = np.asarray(inputs["w_ffn1_up"][l], f32)
            com[f"w1d_{l}"] = np.asarray(inputs["w_ffn1_down"][l], f32)
        elif kind == "ffn2":
            com[f"w2g_{l}"] = np.asarray(inputs["w_ffn2_gate"][l], f32)
            com[f"w2u_{l}"] = np.asarray(inputs["w_ffn2_up"][l], f32)
            com[f"w2d_{l}"] = np.asarray(inputs["w_ffn2_down"][l], f32)
    return com


def core_constants(r):
    f32 = np.float32
    bf = ml_dtypes.bfloat16
    n = np.arange(NTOK)
    t, i_ = n // 128, n % 128
    qaug = np.stack([-128.0 * (2 * t + r), -1.0 * i_, np.full(NTOK, 128.0), np.ones(NTOK)]).astype(f32).astype(bf)
    ii = np.arange(128)
    tri = np.where(ii[:, None] > ii[None, :], -MASK_BIG, 0.0).astype(f32)
    allm = np.full((128, 128), -MASK_BIG, f32)
    zero = np.zeros((128, 128), f32)
    if r == 0:
        mA, mB = tri, allm
        sel = np.tile(np.array([[0.0, 1.0]], f32), (128, 1))
    else:
        mA, mB = zero, tri
        sel = np.tile(np.array([[1.0, 0.0]], f32), (128, 1))
    return {"qaug": qaug, "maskA": mA.astype(bf), "maskB": mB.astype(bf), "sel": np.ascontiguousarray(sel)}


_NC_CACHE = {}


def run_sublayers(inputs, sublayers, x_cores, final_norm):
    key = (tuple(sublayers), final_norm)
    if key not in _NC_CACHE:
        _NC_CACHE[key] = build_program(list(sublayers), final_norm=final_norm)
    nc = _NC_CACHE[key]
    com = make_common_inputs(inputs, sublayers)
    in_maps = []
    has_mix = any(k == "mix" for k, _ in sublayers)
    for c in range(8):
        m = dict(com)
        m["xT"] = x_cores[c]
        if has_mix:
            m.update(core_constants(c % 2))
        in_maps.append(m)
    if os.environ.get("KTRACE"):
        res = run_bass_kernel_spmd(nc, in_maps, core_ids=list(range(8)), trace=True)
        print("KTRACE exec_time_ns", res.exec_time_ns, flush=True)
    else:
        res = run_bass_kernel_spmd(nc, in_maps, core_ids=list(range(8)))
    return [np.asarray(res.results[c]["outT"]) for c in range(8)]


def shard_x(x):
    x = np.asarray(x, np.float32)
    outs = []
    for c in range(8):
        b, r = c // 2, c % 2
        idx = core_token_index(r)
        outs.append(np.ascontiguousarray(x[b][idx].T))
    return outs


def unshard_x(outs):
    y = np.zeros((4, SEQ, D), np.float32)
    for c in range(8):
        b, r = c // 2, c % 2
        idx = core_token_index(r)
        y[b][idx] = outs[c].T
    return y


def kernel(**inputs):
    subl = []
    for l in range(DEPTH):
        subl += [("ffn1", l), ("mix", l), ("ffn2", l)]
    xc = shard_x(inputs["x"])
    outs = run_sublayers(inputs, subl, xc, final_norm=True)
    return unshard_x(outs)
```

```python
import math
import os
import numpy as np
import ml_dtypes
import concourse.bass as bass
import concourse.mybir as mybir
from concourse.bass_utils import run_bass_kernel_spmd

F32 = mybir.dt.float32
BF16 = mybir.dt.bfloat16
AF = mybir.ActivationFunctionType
ALU = mybir.AluOpType

D = 2048
DFF = 5632
NCH = D // 128
NFF = DFF // 128
SEQ = 4096
NTOK = 2048
NBLK = 16
DEPTH = 4
INCOLS = 5632
EPS = 1e-6
SQRT_D = math.sqrt(float(D))
AG_ROWS = 2056
MASK_BIG = 131072.0
KIB = 1024
REPLICA_GROUPS = [[0, 1]] if os.environ.get('KSIM2') else [[0, 1], [2, 3], [4, 5], [6, 7]]


class Op:
    __slots__ = ("eng", "fn", "deps", "sem", "val", "signal", "idx", "inc")

    def __init__(self, eng, fn):
        self.eng = eng
        self.fn = fn
        self.deps = []
        self.sem = None
        self.val = None
        self.signal = True
        self.inc = 1


class Planner:
    ENGS = ("pe", "act", "dve", "pool", "sp")

    def __init__(self):
        self.ops = {e: [] for e in self.ENGS}
        self.last_writer = {}
        self.readers = {}
        self.sem_names = {}
        self.sem_count = {}
        self.last_on_sem = {}
        for e in ("pe", "act", "dve", "pool"):
            self.sem_names["E_" + e] = None

    def _collect(self, reads, writes):
        deps = {}

        def add(op):
            if op is None:
                return
            deps[id(op)] = op

        for r in reads:
            add(self.last_writer.get(r))
        for w in writes:
            add(self.last_writer.get(w))
            for op in self.readers.get(w, {}).values():
                add(op)
        return list(deps.values())

    def _register(self, op, reads, writes):
        for r in reads:
            self.readers.setdefault(r, {})[op.sem if op.sem else ("E", op.eng)] = op
        for w in writes:
            self.last_writer[w] = op
            self.readers[w] = {}

    def op(self, eng, fn, reads=(), writes=(), signal=True):
        o = Op(eng, fn)
        o.signal = signal
        o.sem = "E_" + eng
        deps = self._collect(reads, writes)
        if eng == "pe":
            deps = [d for d in deps if not (d.eng == "pe" and d.sem == "E_pe")]
        o.deps = deps
        self.ops[eng].append(o)
        self._register(o, reads, writes)
        self.last_on_sem[o.sem] = o
        return o

    def dma(self, queue, semkey, fn, reads=(), writes=(), extra_deps=()):
        o = Op(queue, fn)
        o.sem = semkey
        o.inc = 16
        self.sem_names.setdefault(semkey, None)
        self.sem_count[semkey] = self.sem_count.get(semkey, 0) + 16
        o.val = self.sem_count[semkey]
        o.deps = self._collect(reads, writes) + list(extra_deps)
        self.ops[queue].append(o)
        self._register(o, reads, writes)
        self.last_on_sem[semkey] = o
        return o

    def alias_group(self, ops):
        if not ops:
            return
        v = max(o.val for o in ops)
        ids = set(id(o) for o in ops)
        for o in ops:
            o.val = v
            o.deps = [d for d in o.deps if id(d) not in ids]

    def collective(self, semkey, fn, reads=(), writes=()):
        o = Op("pool", fn)
        o.sem = semkey
        o.inc = 1
        self.sem_names.setdefault(semkey, None)
        self.sem_count[semkey] = self.sem_count.get(semkey, 0) + 1
        o.val = self.sem_count[semkey]
        o.deps = self._collect(reads, writes)
        self.ops["pool"].append(o)
        self._register(o, reads, writes)
        self.last_on_sem[semkey] = o
        return o

    def barrier(self):
        lasts = []
        for k, o in self.last_on_sem.items():
            if o is not None:
                if o.sem.startswith("E_"):
                    o.signal = True
                lasts.append(o)
        for e in self.ENGS:
            b = Op(e, None)
            b.signal = False
            b.sem = None
            b.deps = [o for o in lasts if not (o.eng == e and e == "pe" and o.sem == "E_pe")]
            self.ops[e].append(b)
        self.last_writer = {}
        self.readers = {}

    def finalize(self):
        for e in ("pe", "act", "dve", "pool"):
            ops = [o for o in self.ops[e] if o.sem == "E_" + e]
            cnt = 0
            for o in ops:
                if o.signal:
                    cnt += 1
                    o.val = cnt
            nxt = None
            for o in reversed(ops):
                if o.signal:
                    nxt = o.val
                else:
                    assert nxt is not None, "trailing non-signalling op with dependants?"
                    o.val = nxt
            self.sem_count["E_" + e] = cnt

    def replay(self, nc, sems, engines):
        pass


def _emit_engine(planner, eng_name, e, sems):
    waited = {}
    for o in planner.ops[eng_name]:
        need = {}
        for d in o.deps:
            if d.val is None:
                continue
            if need.get(d.sem, 0) < d.val:
                need[d.sem] = d.val
        for s, v in need.items():
            if waited.get(s, 0) < v:
                e.wait_ge(sems[s], v)
                waited[s] = v
        if o.fn is None:
            continue
        ins = o.fn(e)
        if o.sem is not None and (o.signal or not o.sem.startswith("E_")):
            if o.sem.startswith("CC"):
                ins.then_inc(sems[o.sem])
            else:
                ins.then_inc(sems[o.sem], o.inc)


def lambda_init(l):
    return 0.8 - 0.6 * math.exp(-0.3 * l)


def build_program(sublayers, final_norm=True, x_in_name="xT", debug=False):
    nc = bass.Bass("TRN2", target_bir_lowering=False)
    P = Planner()

    def din(name, shape, dt=F32):
        return nc.dram_tensor(name, list(shape), dt, kind="ExternalInput").ap()

    layers = sorted(set(l for _, l in sublayers))
    kinds = set(k for k, _ in sublayers)
    xT_in = din(x_in_name, [D, NTOK])
    outT = nc.dram_tensor("outT", [D, NTOK], F32, kind="ExternalOutput").ap()
    gcols_d = din("gcols", [128, 13 * NCH])
    W = {}
    for l in layers:
        if ("ffn1", l) in sublayers:
            W[("g1", l)] = din(f"w1g_{l}", [D, DFF])
            W[("u1", l)] = din(f"w1u_{l}", [D, DFF])
            W[("d1", l)] = din(f"w1d_{l}", [DFF, D])
        if ("ffn2", l) in sublayers:
            W[("g2", l)] = din(f"w2g_{l}", [D, DFF])
            W[("u2", l)] = din(f"w2u_{l}", [D, DFF])
            W[("d2", l)] = din(f"w2d_{l}", [DFF, D])
        if ("mix", l) in sublayers:
            W[("in", l)] = din(f"win_{l}", [D, INCOLS])
            W[("out", l)] = din(f"wout_{l}", [D, D])
            W[("gv", l)] = din(f"gv_{l}", [1, 512])
            W[("ws", l)] = din(f"ws_{l}", [4, 128, 128])
            W[("bs", l)] = din(f"bs_{l}", [1, 512])
            W[("lq", l)] = din(f"lq_{l}", [1, 256])
            W[("gsub", l)] = din(f"gsub_{l}", [1, 128])
            W[("wc", l)] = din(f"wc_{l}", [128, 12])
    has_mix = "mix" in kinds
    if has_mix:
        ident_d = din("ident", [128, 128], BF16)
        identf_d = din("identf", [128, 128], F32)
        tril_d = din("tril01", [128, 128], F32)
        maskA_d = din("maskA", [128, 128], BF16)
        maskB_d = din("maskB", [128, 128], BF16)
        kaug_d = din("kaug", [4, SEQ], BF16)
        qaug_d = din("qaug", [4, NTOK], BF16)
        sel_d = din("sel", [128, 2])
    xs = nc.dram_tensor("xs", [D, NTOK], F32).ap()
    if has_mix:
        qs = nc.dram_tensor("qs", [1024, NTOK], BF16).ap()
        winb = nc.dram_tensor("winb", [D, INCOLS], BF16).ap()
        woutb = nc.dram_tensor("woutb", [D, D], BF16).ap()
        ag_src = nc.dram_tensor("ag_src", [AG_ROWS, NTOK], BF16).ap()
        AG_PARTS = [(0, 512), (512, 1024), (1024, 1536), (1536, 2048), (2048, 2056)]
        ag_dstp = [nc.dram_tensor(f"ag_dst{i}", [2 * (r1 - r0), NTOK], BF16).ap() for i, (r0, r1) in enumerate(AG_PARTS)]
    dbg = {}

    ARENA_BYTES = 190 * KIB
    PERS_BYTES = 15 * KIB

    import contextlib
    with contextlib.ExitStack() as ctx:
        arena = ctx.enter_context(nc.sbuf_tensor("arena", [128, ARENA_BYTES // 2], BF16))
        pers = ctx.enter_context(nc.sbuf_tensor("pers", [128, PERS_BYTES // 2], BF16))
        banks = [ctx.enter_context(nc.psum_tensor(f"bank{i}", [128, 512], F32)) for i in range(8)]

        def carve(base, off, shape, dt):
            n = int(np.prod(shape[1:]))
            esz = 4 if dt == F32 else 2
            assert off % 4 == 0
            a = base[:, off // 2: off // 2 + n * esz // 2]
            if dt == F32:
                a = a.bitcast(F32)
            if len(shape) == 3:
                a = a.rearrange("p (a b) -> p a b", a=shape[1])
            elif len(shape) == 4:
                a = a.rearrange("p (a b c) -> p a b c", a=shape[1], b=shape[2])
            return a

        class Alloc:
            def __init__(self, base, start, limit):
                self.base, self.off, self.limit = base, start, limit

            def __call__(self, shape, dt):
                n = int(np.prod(shape[1:])) * (4 if dt == F32 else 2)
                n = (n + 31) // 32 * 32
                a = carve(self.base, self.off, shape, dt)
                self.off += n
                assert self.off <= self.limit, (self.off, self.limit)
                return a

        pal = Alloc(pers, 0, PERS_BYTES)
        ones_bf = pal([128, 128], BF16)
        gcols = pal([128, 13, NCH], F32)
        epscol = pal([128, 8], F32)
        if has_mix:
            ident = pal([128, 128], BF16)
            identf = pal([128, 128], F32)
            tril = pal([128, 128], F32)
            maskA = pal([128, 128], BF16)
            maskB = pal([128, 128], BF16)
            sel = pal([128, 2], F32)
            gvb = pal([128, 512], F32)
            gsub = pal([128, 128], F32)
            wst = pal([128, 4, 128], BF16)
            bsrow = pal([128, 512], BF16)
            bsrow32 = pal([128, 512], F32)
            lamb = pal([128, 4, 64], F32)
            lamprod = pal([128, 4, 64], F32)
            lamcol = pal([128, 8], F32)
            wconv = pal([128, 3, 4], F32)
            wsnat = pal([128, 4, 128], F32)

        def setup():
            P.op("dve", lambda e: e.memset(ones_bf, 1.0), writes=["ones"])
            P.dma("sp", "D_const", lambda e: e.dma_start(out=gcols.rearrange("p a c -> p (a c)"), in_=gcols_d),
                  writes=["gcols"])
            P.op("dve", lambda e: e.memset(epscol, EPS), writes=["epscol"])
            if has_mix:
                for nm, t, d in (("ident", ident, ident_d), ("identf", identf, identf_d), ("tril", tril, tril_d),
                                 ("maskA", maskA, maskA_d), ("maskB", maskB, maskB_d), ("sel", sel, sel_d)):
                    P.dma("sp", "D_const", (lambda t, d: (lambda e: e.dma_start(out=t, in_=d)))(t, d), writes=[nm])

        def emit_norm(XT, HT, SQ, RSTD, T, gidx, ss_banks, xres, hres, inplace=False):
            nh = T // 512
            for c in range(NCH):
                sl = c % 2
                P.op("act", (lambda c, sl: lambda e: e.activation(out=SQ[:, sl, :], in_=XT[:, c, :], func=AF.Square))(c, sl),
                     reads=[f"{xres}{c}"], writes=[f"SQ{sl}"])
                for h in range(nh):
                    P.op("pe", (lambda c, sl, h: lambda e: e.matmul(banks[ss_banks[h]][:, :], lhsT=ones_bf,
                                                                    rhs=SQ[:, sl, h * 512:(h + 1) * 512],
                                                                    start=(c == 0), stop=(c == NCH - 1)))(c, sl, h),
                         reads=[f"SQ{sl}", "ones"], writes=[f"ps{ss_banks[h]}"], signal=(c == NCH - 1 or True))
            for h in range(nh):
                P.op("act", (lambda h: lambda e: e.activation(out=RSTD[:, h * 512:(h + 1) * 512],
                                                              in_=banks[ss_banks[h]][:, :], func=AF.Sqrt,
                                                              scale=1.0 / float(D), bias=epscol[:, 0:1]))(h),
                     reads=[f"ps{ss_banks[h]}", "epscol"], writes=[f"RSTD{h}"])
                P.op("dve", (lambda h: lambda e: e.reciprocal(out=RSTD[:, h * 512:(h + 1) * 512],
                                                              in_=RSTD[:, h * 512:(h + 1) * 512]))(h),
                     reads=[f"RSTD{h}"], writes=[f"RSTD{h}"])
            for c in range(NCH):
                dst = XT if inplace else HT
                P.op("dve", (lambda c, dst: lambda e: e.scalar_tensor_tensor(out=dst[:, c, :], in0=XT[:, c, :],
                                                                             scalar=gcols[:, gidx, c:c + 1],
                                                                             in1=RSTD[:, 0:T],
                                                                             op0=ALU.mult, op1=ALU.mult))(c, dst),
                     reads=[f"{xres}{c}", "gcols"] + [f"RSTD{h}" for h in range(nh)],
                     writes=[f"{xres if inplace else hres}{c}"])

        def emit_ffn(l, which, x_src, x_dst, final):
            TG = 1024
            NH = 2
            al = Alloc(arena, 0, ARENA_BYTES)
            XT = al([128, NCH, TG], F32)
            HT = al([128, NCH, TG], BF16)
            AT = al([128, 8, TG], BF16)
            SG = al([128, 2, 512], BF16)
            SQ = al([128, 2, TG], BF16)
            RSTD = al([128, TG], F32)
            WGU = [al([128, 2, NCH, 256], BF16) for _ in range(2)]
            WD = [al([128, 4, D], BF16) for _ in range(2)]
            wg, wu, wd = W[("g%d" % which, l)], W[("u%d" % which, l)], W[("d%d" % which, l)]
            wgv = wg.rearrange("(k p) n -> p k n", p=128)
            wuv = wu.rearrange("(k p) n -> p k n", p=128)
            wdv = wd.rearrange("(c p) n -> p c n", p=128)
            gidx = (0 if which == 1 else 2) * 4 + l
            xsv = x_src.rearrange("(c p) t -> p c t", p=128)
            xdv = x_dst.rearrange("(c p) t -> p c t", p=128)
            outv = outT.rearrange("(c p) t -> p c t", p=128)
            ps_gu = [(0, 1), (2, 3), (4, 5)]
            ps_y = [6, 7]
            gu_i = 0
            y_i = 0
            wgu_i = 0
            wd_i = 0
            sg_i = 0
            DBG = int(os.environ.get('KDBG', '9'))
            NTG_F = NTOK // TG
            cvt = []
            if which == 1 and ("mix", l) in sublayers and DBG >= 9:
                for cg in range(INCOLS // 256):
                    cvt.append((W[("in", l)][:, cg * 256:(cg + 1) * 256], winb[:, cg * 256:(cg + 1) * 256], "winb"))
                for cg in range(D // 256):
                    cvt.append((W[("out", l)][:, cg * 256:(cg + 1) * 256], woutb[:, cg * 256:(cg + 1) * 256], "woutb"))
            cvt_ops = []

            def issue_cvt(n):
                for _ in range(n):
                    if cvt:
                        src, dst, res = cvt.pop(0)
                        cvt_ops.append(P.dma("pool", "D_cvt", B(lambda e, src, dst: e.dma_start(out=dst, in_=src), src, dst),
                                             writes=[f"{res}_{len(cvt)}"]))

            def load_chunk(tg, c):
                t0 = tg * TG
                P.dma("sp", f"D_xt{c}", B(lambda e, c, t0: e.dma_start(out=XT[:, c, :], in_=xsv[:, c, t0:t0 + TG]), c, t0),
                      reads=[f"xd{tg}_{c}"], writes=[f"XT{c}"])

            def store_chunk(tg, c, dstv, dres):
                t0 = tg * TG
                P.dma("sp", f"D_xst{c}", B(lambda e, c, t0, dstv: e.dma_start(out=dstv[:, c, t0:t0 + TG], in_=XT[:, c, :]),
                                           c, t0, dstv),
                      reads=[f"XT{c}"], writes=[f"{dres}{tg}_{c}"])

            for tg in range(NTG_F):
                t0 = tg * TG
                if tg == 0:
                    for c in range(NCH):
                        load_chunk(0, c)
                if DBG >= 1:
                    emit_norm(XT, HT, SQ, RSTD, TG, gidx, [6, 7], "XT", "HT")
                if DBG >= 9:
                    supers = [[g] for g in range(NFF // 4 - 2)] + [[NFF // 4 - 2, NFF // 4 - 1]]
                elif DBG >= 2:
                    supers = [[0]]
                else:
                    supers = []
                for sgrp in supers:
                    issue_cvt(2)
                    for gi_, g in enumerate(sgrp):
                        for pair in range(2):
                            slot = wgu_i % 2
                            wgu_i += 1
                            ff0 = (g * 4 + pair * 2) * 128
                            grp = []
                            for gi, wv in enumerate((wgv, wuv)):
                                grp.append(P.dma("pool", f"D_wgu{slot}", B(lambda e, gi, wv, slot, ff0: e.dma_start(
                                    out=WGU[slot][:, gi, :, :], in_=wv[:, :, ff0:ff0 + 256]), gi, wv, slot, ff0),
                                    writes=[f"WGU{slot}"]))
                            P.alias_group(grp)
                            for cc in range(2):
                                c = gi_ * 4 + pair * 2 + cc
                                for h in range(NH):
                                    bg, bu = ps_gu[gu_i % 3]
                                    gu_i += 1
                                    for gi, bank in ((0, bg), (1, bu)):
                                        for k in range(NCH):
                                            P.op("pe", B(lambda e, slot, gi, k, cc, h, bank: e.matmul(
                                                banks[bank][:, :], lhsT=WGU[slot][:, gi, k, cc * 128:(cc + 1) * 128],
                                                rhs=HT[:, k, h * 512:(h + 1) * 512], start=(k == 0), stop=(k == NCH - 1)),
                                                slot, gi, k, cc, h, bank),
                                                reads=[f"WGU{slot}", f"HT{k}"], writes=[f"ps{bank}"], signal=(k == NCH - 1))
                                    s_ = sg_i % 2
                                    sg_i += 1
                                    P.op("act", B(lambda e, s_, bg: e.activation(out=SG[:, s_, :], in_=banks[bg][:, :],
                                                                                 func=AF.Silu), s_, bg),
                                         reads=[f"ps{bg}"], writes=[f"SG{s_}"])
                                    P.op("dve", B(lambda e, s_, bu, c, h: e.tensor_tensor(
                                        out=AT[:, c, h * 512:(h + 1) * 512], in0=SG[:, s_, :], in1=banks[bu][:, :],
                                        op=ALU.mult), s_, bu, c, h),
                                        reads=[f"SG{s_}", f"ps{bu}"], writes=[f"AT{c}_{h}"])
                    if DBG == 2:
                        continue
                    slots = []
                    for g in sgrp:
                        slot = wd_i % 2
                        wd_i += 1
                        slots.append(slot)
                        P.dma("pool", f"D_wd{slot}", B(lambda e, slot, g: e.dma_start(
                            out=WD[slot][:, :, :], in_=wdv[:, g * 4:(g + 1) * 4, :]), slot, g), writes=[f"WD{slot}"])
                    nmm = 4 * len(sgrp)
                    for j in range(NCH):
                        for h in range(NH):
                            bank = ps_y[y_i % 2]
                            y_i += 1
                            for idx in range(nmm):
                                gi_, c4 = idx // 4, idx % 4
                                slot = slots[gi_]
                                c = gi_ * 4 + c4
                                P.op("pe", B(lambda e, slot, c4, c, j, h, bank, idx: e.matmul(
                                    banks[bank][:, :], lhsT=WD[slot][:, c4, j * 128:(j + 1) * 128],
                                    rhs=AT[:, c, h * 512:(h + 1) * 512], start=(idx == 0), stop=(idx == nmm - 1)),
                                    slot, c4, c, j, h, bank, idx),
                                    reads=[f"WD{slot}", f"AT{c}_{h}"], writes=[f"ps{bank}"], signal=(idx == nmm - 1))
                            P.op("dve", B(lambda e, j, h, bank: e.scalar_tensor_tensor(
                                out=XT[:, j, h * 512:(h + 1) * 512], in0=banks[bank][:, :], scalar=0.5,
                                in1=XT[:, j, h * 512:(h + 1) * 512], op0=ALU.mult, op1=ALU.add), j, h, bank),
                                reads=[f"ps{bank}", f"XT{j}"], writes=[f"XT{j}"])
                if final:
                    emit_norm(XT, None, SQ, RSTD, TG, 12, [6, 7], "XT", None, inplace=True)
                    dstv, dres = outv, "od"
                else:
                    dstv, dres = xdv, "xd"
                LAGC = 3
                for c in range(NCH + LAGC):
                    if c < NCH:
                        store_chunk(tg, c, dstv, dres)
                    if tg + 1 < NTG_F and c >= LAGC:
                        load_chunk(tg + 1, c - LAGC)
            issue_cvt(len(cvt))
            P.alias_group(cvt_ops)
            CVT_DONE[l] = cvt_ops[-1] if cvt_ops else None

        MIX = {}
        CVT_DONE = {}

        def B(f, *a):
            return lambda e: f(e, *a)

        AX = mybir.AxisListType.X

        def emit_mix(l, x_src, x_dst):
            TG = 512
            NTG = NTOK // TG
            al = Alloc(arena, 0, ARENA_BYTES)
            CAT = al([128, NCH, NTOK], BF16)
            ZT = al([128, 4, NBLK, 130], BF16)
            base = al.off
            pre = ("ffn1", l) in sublayers and int(os.environ.get('KDBG', '9')) >= 9
            win = (winb if pre else W[("in", l)]).rearrange("(k p) n -> p k n", p=128)
            wout = (woutb if pre else W[("out", l)]).rearrange("(k p) n -> p k n", p=128)
            xsv = x_src.rearrange("(c p) t -> p c t", p=128)
            xdv = x_dst.rearrange("(c p) t -> p c t", p=128)
            linit = lambda_init(l)
            vsrc = ag_src[1024:2048, :].rearrange("r (two f) -> (r two) f", two=2)
            tsrc = ag_src[2048:2056, :].rearrange("r (a f) -> (r a) f", f=32).rearrange("(ch p) f -> p ch f", p=128)

            smalls = []

            def small(nm, t, d):
                smalls.append(P.dma("sp", "D_const", B(lambda e, t, d: e.dma_start(out=t, in_=d), t, d), writes=[nm]))
            small("gvb", gvb, W[("gv", l)].partition_broadcast(128))
            small("gsub", gsub, W[("gsub", l)].partition_broadcast(128))
            small("lamb", lamb.rearrange("p a b -> p (a b)"), W[("lq", l)].partition_broadcast(128))
            small("bsrow32", bsrow32[0:1, :], W[("bs", l)])
            small("wconv", wconv.rearrange("p a b -> p (a b)"), W[("wc", l)])
            small("wsnat", wsnat, W[("ws", l)].rearrange("g t s -> t g s"))
            P.alias_group(smalls)
            P.op("dve", lambda e: e.tensor_scalar(out=gsub, in0=gsub, scalar1=float(1.0 - linit), scalar2=None,
                                                  op0=ALU.mult), reads=["gsub"], writes=["gsub"])
            P.op("act", lambda e: e.activation(out=bsrow[0:1, :], in_=bsrow32[0:1, :], func=AF.Copy),
                 reads=["bsrow32"], writes=["bsrow"])
            P.op("dve", lambda e: e.tensor_tensor(out=lamprod[:, 0:2, :], in0=lamb[:, 0:4:2, :], in1=lamb[:, 1:4:2, :],
                                                  op=ALU.mult), reads=["lamb"], writes=["lamprod"])
            P.op("dve", lambda e: e.reduce_sum(out=lamcol[:, 0:2], in_=lamprod[:, 0:2, :], axis=AX),
                 reads=["lamprod"], writes=["lamcol"])
            P.op("act", lambda e: e.activation(out=lamcol[:, 2:4], in_=lamcol[:, 0:2], func=AF.Exp),
                 reads=["lamcol"], writes=["lamcol"])
            P.op("dve", lambda e: e.tensor_tensor(out=lamcol[:, 4:5], in0=lamcol[:, 2:3], in1=lamcol[:, 3:4],
                                                  op=ALU.subtract), reads=["lamcol"], writes=["lamcol"])
            P.op("dve", lambda e: e.tensor_scalar(out=lamcol[:, 5:6], in0=lamcol[:, 4:5], scalar1=float(linit),
                                                  scalar2=None, op0=ALU.add), reads=["lamcol"], writes=["lamcol"])
            for g in range(4):
                P.op("pe", B(lambda e, g: e.transpose(out=banks[6][:, g * 128:(g + 1) * 128], in_=wsnat[:, g, :],
                                                      identity=identf), g),
                     reads=["wsnat", "identf"], writes=["ps6"])
            for g in range(4):
                P.op("dve", B(lambda e, g: e.tensor_tensor(out=wst[:, g, :], in0=banks[6][:, g * 128:(g + 1) * 128],
                                                           in1=tril, op=ALU.mult), g),
                     reads=["ps6", "tril"], writes=["wst"])

            XT = carve(arena, 4 * NTOK * 2, [128, NCH, TG], F32)
            HT2 = [al([128, NCH, TG], BF16) for _ in range(2)]
            WSALL = al([128, 4, NCH, 256], BF16)
            WS = [WSALL[:, i, :, :] for i in range(4)]
            UT = al([128, 4, TG], BF16)
            SQ = al([128, 2, TG], BF16)
            RSTD = al([128, TG], F32)
            QKST = al([128, 3, TG], BF16)
            CZ = al([128, 4, TG], BF16)
            V32 = al([128, 4, 512], F32)
            VN = al([128, 2, 512], BF16)
            VST = al([128, 1, 4, 1024], BF16)
            TAILS = al([128, 4, NBLK, 2], BF16)
            SSV = al([128, 16], F32)
            JUNK = al([128, 512], BF16)
            m1_end = al.off
            fm_banks = [0, 1, 2, 3]
            tm_banks = [4, 5]
            CG_ORDER = list(range(INCOLS // 256))
            KMIX = int(os.environ.get("KMIX", "9"))

            def issue_ag(parts, res):
                for pi in parts:
                    r0, r1 = AG_PARTS[pi]
                    P.collective("CC_ag", B(lambda e, pi, r0, r1: e.collective_compute(
                        "AllGather", ALU.bypass, replica_groups=REPLICA_GROUPS,
                        ins=[ag_src[r0:r1, :].opt()], outs=[ag_dstp[pi].opt()]), pi, r0, r1),
                        reads=list(res), writes=[f"agdst{pi}"])
            cnt = {"fm": 0, "tm": 0, "ws": 0, "st": 0, "vn": 0}
            agsrc_res = []
            def m1_load(tg):
                t0 = tg * TG
                grp = []
                for q in range(2):
                    grp.append(P.dma("sp", "D_xt", B(lambda e, q, t0: e.dma_start(
                        out=XT[:, q * 8:(q + 1) * 8, :], in_=xsv[:, q * 8:(q + 1) * 8, t0:t0 + TG]), q, t0),
                        reads=["xdram"], writes=[f"XT{c}" for c in range(q * 8, q * 8 + 8)]))
                P.alias_group(grp)

            def m1_norm(tg):
                emit_norm(XT, HT2[tg % 2], SQ, RSTD, TG, 4 + l, [7], "XT", f"HT{tg % 2}_")

            m1_load(0)
            m1_norm(0)
            for tg in range(NTG):
                t0 = tg * TG
                HT = HT2[tg % 2]
                hb = tg % 2
                for cg in CG_ORDER:
                    if cg == 1 and tg + 1 < NTG:
                        m1_load(tg + 1)
                    if cg == 10 and tg + 1 < NTG:
                        m1_norm(tg + 1)
                    slot = cg % 4
                    P.dma("pool", f"D_ws{slot}", B(lambda e, slot, cg: e.dma_start(
                        out=WS[slot], in_=win[:, :, cg * 256:(cg + 1) * 256]), slot, cg), writes=[f"WS{slot}"])
                    ch0 = 2 * cg
                    is_tm = (4 <= ch0 < 8) or (24 <= ch0 < 32)
                    if is_tm and cg % 2 == 0:
                        continue
                    if not is_tm:
                        for cc in range(2):
                            ch = ch0 + cc
                            bank = fm_banks[cnt["fm"] % 4]
                            cnt["fm"] += 1
                            for k in range(NCH):
                                P.op("pe", B(lambda e, slot, k, cc, bank, HT: e.matmul(
                                    banks[bank][:, :], lhsT=WS[slot][:, k, cc * 128:(cc + 1) * 128], rhs=HT[:, k, :],
                                    start=(k == 0), stop=(k == NCH - 1)), slot, k, cc, bank, HT),
                                    reads=[f"WS{slot}", f"HT{hb}_{k}"], writes=[f"ps{bank}"], signal=(k == NCH - 1))
                            if ch < 4:
                                P.op("act", B(lambda e, ch, bank: e.activation(out=UT[:, ch, :], in_=banks[bank][:, :],
                                                                                func=AF.Copy), ch, bank),
                                     reads=[f"ps{bank}"], writes=[f"UT{ch}"])
                            elif ch < 16:
                                h = ch - 8
                                s = cnt["st"] % 3
                                cnt["st"] += 1
                                P.op("act", B(lambda e, s, bank, h: e.activation(out=QKST[:, s, :], in_=banks[bank][:, :],
                                                                                  func=AF.Copy, scale=float(2.0 ** (h - 2))),
                                              s, bank, h),
                                     reads=[f"ps{bank}"], writes=[f"QKST{s}"])
                                P.dma("sp", f"D_qkst{s}", B(lambda e, s, h, t0: e.dma_start(
                                    out=qs[h * 128:(h + 1) * 128, t0:t0 + TG], in_=QKST[:, s, :]), s, h, t0),
                                    reads=[f"QKST{s}"], writes=[f"qs{h}_{tg}"])
                            elif ch < 24:
                                h = ch - 16
                                s = cnt["st"] % 3
                                cnt["st"] += 1
                                P.op("dve", B(lambda e, s, bank: e.tensor_copy(out=QKST[:, s, :], in_=banks[bank][:, :]),
                                              s, bank),
                                     reads=[f"ps{bank}"], writes=[f"QKST{s}"])
                                P.dma("sp", f"D_qkst{s}", B(lambda e, s, h, t0: e.dma_start(
                                    out=ag_src[512 * (h // 2) + 128 * (h % 2):512 * (h // 2) + 128 * (h % 2) + 128, t0:t0 + TG],
                                    in_=QKST[:, s, :]), s, h, t0),
                                    reads=[f"QKST{s}"], writes=[f"agk{h}_{tg}"])
                                agsrc_res.append(f"agk{h}_{tg}")
                            elif ch < 36:
                                i = ch - 32
                                P.op("act", B(lambda e, i, bank, t0: e.activation(out=CAT[:, 12 + i, t0:t0 + TG],
                                                                                   in_=banks[bank][:, :], func=AF.Copy),
                                              i, bank, t0),
                                     reads=[f"ps{bank}"], writes=[f"CAT{12 + i}_{tg}"])
                            elif ch < 40:
                                i = ch - 36
                                P.op("act", B(lambda e, i, bank: e.activation(out=CZ[:, i, :], in_=banks[bank][:, :],
                                                                               func=AF.Copy), i, bank),
                                     reads=[f"ps{bank}"], writes=[f"CZ{i}"])
                            else:
                                i = ch - 40
                                P.op("dve", B(lambda e, i, bank, tg: e.tensor_tensor(
                                    out=ZT[:, i, 4 * tg:4 * tg + 4, 2:130],
                                    in0=CZ[:, i, :].rearrange("p (a b) -> p a b", a=4),
                                    in1=banks[bank][:, :].rearrange("p (a b) -> p a b", a=4), op=ALU.mult), i, bank, tg),
                                    reads=[f"CZ{i}", f"ps{bank}"], writes=[f"ZT{i}_{tg}"])
                    else:
                        s0 = slot - 1
                        for tb in range(4):
                            bank = tm_banks[cnt["tm"] % 2]
                            cnt["tm"] += 1
                            for k in range(NCH):
                                P.op("pe", B(lambda e, s0, k, tb, bank, HT: e.matmul(
                                    banks[bank][:, :], lhsT=HT[:, k, tb * 128:(tb + 1) * 128], rhs=WSALL[:, s0:s0 + 2, k, :],
                                    start=(k == 0), stop=(k == NCH - 1)), s0, k, tb, bank, HT),
                                    reads=[f"WS{s0}", f"WS{s0 + 1}", f"HT{hb}_{k}"], writes=[f"ps{bank}"],
                                    signal=(k == NCH - 1))
                            if ch0 < 8:
                                P.op("act", B(lambda e, tb, bank: e.activation(out=V32[:, tb, :], in_=banks[bank][:, :],
                                                                               func=AF.Copy), tb, bank),
                                     reads=[f"ps{bank}"], writes=[f"V32_{tb}"])
                            else:
                                c0 = (ch0 - 2 - 24) * 128
                                vs = 0
                                P.op("dve", B(lambda e, vs, tb, c0, bank: e.tensor_copy(out=VST[:, vs, tb, c0:c0 + 512],
                                                                                         in_=banks[bank][:, :]),
                                              vs, tb, c0, bank),
                                     reads=[f"ps{bank}"], writes=[f"VST{vs}_{tb}_{c0}"])
                        if ch0 == 6:
                            for tb in range(4):
                                blk = 4 * tg + tb
                                P.op("act", B(lambda e, tb: e.activation(out=JUNK, in_=V32[:, tb, :], func=AF.Square,
                                                                         accum_out=SSV[:, tb:tb + 1]), tb),
                                     reads=[f"V32_{tb}"], writes=["JUNK", f"SSV{tb}"])
                                P.op("act", B(lambda e, tb: e.activation(out=SSV[:, 4 + tb:5 + tb], in_=SSV[:, tb:tb + 1],
                                                                         func=AF.Sqrt, scale=1.0 / 512.0,
                                                                         bias=epscol[:, 0:1]), tb),
                                     reads=[f"SSV{tb}", "epscol"], writes=[f"SSR{tb}"])
                                P.op("dve", B(lambda e, tb: e.reciprocal(out=SSV[:, 8 + tb:9 + tb],
                                                                         in_=SSV[:, 4 + tb:5 + tb]), tb),
                                     reads=[f"SSR{tb}"], writes=[f"SSI{tb}"])
                                vn = cnt["vn"] % 2
                                cnt["vn"] += 1
                                P.op("dve", B(lambda e, tb, vn: e.scalar_tensor_tensor(
                                    out=VN[:, vn, :], in0=V32[:, tb, :], scalar=SSV[:, 8 + tb:9 + tb], in1=gvb,
                                    op0=ALU.mult, op1=ALU.mult), tb, vn),
                                    reads=[f"V32_{tb}", f"SSI{tb}", "gvb"], writes=[f"VN{vn}"])
                                for g in range(4):
                                    P.op("pe", B(lambda e, vn, g: e.matmul(
                                        banks[6][:, g * 128:(g + 1) * 128], lhsT=VN[:, vn, g * 128:(g + 1) * 128],
                                        rhs=wst[:, g, :], start=(g == 0), stop=False, skip_group_check=True), vn, g),
                                        reads=[f"VN{vn}", "wst"], writes=["ps6"], signal=False)
                                    P.op("pe", B(lambda e, g: e.matmul(
                                        banks[6][:, g * 128:(g + 1) * 128], lhsT=ones_bf[0:1, 0:128],
                                        rhs=bsrow[0:1, g * 128:(g + 1) * 128], start=False, stop=(g == 3),
                                        skip_group_check=True), g),
                                        reads=["bsrow", "ones"], writes=["ps6"], signal=(g == 3))
                                P.op("dve", B(lambda e, tb, blk: e.tensor_tensor(
                                    out=CAT[:, 0:4, blk * 128:(blk + 1) * 128], in0=UT[:, 0:4, tb * 128:(tb + 1) * 128],
                                    in1=banks[6][:, :].rearrange("p (a b) -> p a b", a=4), op=ALU.mult), tb, blk),
                                    reads=["ps6"] + [f"UT{i}" for i in range(4)], writes=[f"CATA_{blk}"])
                        if ch0 == 30:
                            vs = 0
                            grp = []
                            for pr in range(4):
                                grp.append(P.dma("sp", f"D_vst{vs}", B(lambda e, vs, t0, pr: e.dma_start(
                                    out=ag_src[512 * pr + 256:512 * pr + 512, :].rearrange("r (n f) -> (r n) f", f=256)[
                                        t0:t0 + TG, :].rearrange("(tb p) f -> p tb f", p=128),
                                    in_=VST[:, vs, :, 256 * pr:256 * pr + 256]), vs, t0, pr),
                                    reads=[f"VST{vs}_{tb}_{c0}" for tb in range(4) for c0 in (0, 512)],
                                    writes=[f"agv_{tg}_{pr}"]))
                            P.alias_group(grp)
                            agsrc_res.append(f"agv_{tg}")

                for i in range(4):
                    P.op("dve", B(lambda e, tg, i: e.tensor_copy(out=TAILS[:, i, 4 * tg:4 * tg + 4, :],
                                                                 in_=ZT[:, i, 4 * tg:4 * tg + 4, 128:130]), tg, i),
                         reads=[f"ZT{i}_{tg}"], writes=[f"TAILS{tg}_{i}"])
            P.dma("sp", "D_tail", lambda e: e.dma_start(out=tsrc, in_=TAILS.rearrange("p c b j -> p c (b j)")),
                  reads=[f"TAILS{tg}_{i}" for tg in range(NTG) for i in range(4)], writes=["agtail"])
            agsrc_res.append("agtail")

            P.barrier()
            issue_ag([0, 1, 2, 3, 4], [])
            al.off = base
            KH = [[al([128, SEQ], BF16) for m in range(2)] for b in range(2)]
            QH = [[al([128, NTOK], BF16) for m in range(2)] for b in range(2)]
            VH = [al([128, 32, 130], BF16) for b in range(2)]
            PT = al([128, 6, 512], BF16)
            RL = al([128, 2, 4], F32)
            T1 = al([128, 2, 128], F32)
            OSB = al([128, 2, 128], F32)
            SQJ = al([128, 2, 128], F32)
            SSQ = al([128, 2, 4], F32)
            YB = al([128, 8, 128], BF16)
            TA = al([128, 4, 32], BF16)
            TB = al([128, 4, 32], BF16)
            TMPH = al([128, 4, NBLK, 2], F32)
            ACC = al([128, NBLK, 128], F32)
            ACC2 = al([128, NBLK, 128], F32)
            TMPH2 = al([128, 4, NBLK, 2], F32)
            assert al.off <= ARENA_BYTES

            augs = []
            for b in range(2 if KMIX >= 3 else 0):
                for m in range(2):
                    augs.append(P.dma("sp", "D_const", B(lambda e, b, m: e.dma_start(out=KH[b][m][64:68, :], in_=kaug_d), b, m),
                                      writes=[f"KHaug{b}"]))
                    augs.append(P.dma("sp", "D_const", B(lambda e, b, m: e.dma_start(out=QH[b][m][64:68, :], in_=qaug_d), b, m),
                                      writes=[f"QHaug{b}"]))
            P.alias_group(augs)
            for b in range(2 if KMIX >= 3 else 0):
                P.op("dve", B(lambda e, b: e.memset(VH[b][:, :, 128:129], 1.0), b), writes=[f"VHaug{b}"])

            def emit_conv_prep():
                def tview(r):
                    return ag_dstp[4][r * 8:(r + 1) * 8, :].rearrange(
                        "r (a f) -> (r a) f", f=32).rearrange("(ch p) f -> p ch f", p=128)
                P.dma("sp", "D_ta", lambda e: e.dma_start(out=TA, in_=tview(0)), reads=["agdst4"], writes=["TA"])
                P.dma("sp", "D_tb", lambda e: e.dma_start(out=TB, in_=tview(1)), reads=["agdst4"], writes=["TB"])
                TB4 = TB.rearrange("p c (b j) -> p c b j", j=2)
                P.op("dve", lambda e: e.tensor_scalar(out=TMPH.rearrange("p c b j -> p (c b j)"),
                                                      in0=TA.rearrange("p c f -> p (c f)"), scalar1=sel[:, 0:1],
                                                      scalar2=None, op0=ALU.mult),
                     reads=["TA", "sel"], writes=["TMPH"])
                for i in range(4):
                    P.op("dve", B(lambda e, i: e.scalar_tensor_tensor(out=ZT[:, i, 1:NBLK, 0:2], in0=TB4[:, i, 0:NBLK - 1, :],
                                                                      scalar=sel[:, 1:2], in1=TMPH[:, i, 1:NBLK, :],
                                                                      op0=ALU.mult, op1=ALU.add), i),
                         reads=["TB", "TMPH", "sel"], writes=[f"ZTh{i}"])
                P.op("dve", lambda e: e.tensor_copy(out=ZT[:, :, 0, 0:2], in_=TMPH[:, :, 0, :]),
                     reads=["TMPH"], writes=["ZTh0"])

            def emit_conv_chunk(i):
                zres = ["ZTh0"] + [f"ZTh{i}" for i in range(4)]
                P.op("dve", B(lambda e, i: e.tensor_scalar(out=ACC, in0=ZT[:, i, :, 2:130], scalar1=wconv[:, 2, i:i + 1],
                                                           scalar2=None, op0=ALU.mult), i),
                     reads=zres + ["wconv"], writes=["ACC"])
                P.op("dve", B(lambda e, i: e.scalar_tensor_tensor(out=ACC, in0=ZT[:, i, :, 1:129],
                                                                  scalar=wconv[:, 1, i:i + 1], in1=ACC,
                                                                  op0=ALU.mult, op1=ALU.add), i),
                     reads=zres + ["ACC", "wconv"], writes=["ACC"])
                P.op("dve", B(lambda e, i: e.scalar_tensor_tensor(out=ACC, in0=ZT[:, i, :, 0:128],
                                                                  scalar=wconv[:, 0, i:i + 1], in1=ACC,
                                                                  op0=ALU.mult, op1=ALU.add), i),
                     reads=zres + ["ACC", "wconv"], writes=["ACC"])
                P.op("dve", B(lambda e, i: e.tensor_tensor(
                    out=CAT[:, 12 + i, :].rearrange("p (b t) -> p b t", t=128), in0=ACC,
                    in1=CAT[:, 12 + i, :].rearrange("p (b t) -> p b t", t=128), op=ALU.mult), i),
                    reads=["ACC"], writes=[f"CATC{i}"])

            def emit_conv():
                emit_conv_prep()
                for i in range(4):
                    emit_conv_chunk(i)

            def vview(r, pr):
                return ag_dstp[pr][r * 512 + 256:r * 512 + 512, :].rearrange("r (n f) -> (r n) f", f=256)

            def load_head(h):
                b = h % 2
                pr = h // 2
                grp = []
                for m in range(2):
                    for r in range(2):
                        row0 = r * 512 + 128 * (h % 2) + 64 * m
                        grp.append(P.dma("sp", f"D_kh{b}", B(lambda e, b, m, r, pr, row0: e.dma_start(
                            out=KH[b][m][0:64, r * NTOK:(r + 1) * NTOK], in_=ag_dstp[pr][row0:row0 + 64, :]),
                            b, m, r, pr, row0),
                            reads=[f"agdst{pr}"], writes=[f"KH{b}"]))
                P.alias_group(grp)
                grp = []
                for m in range(2):
                    row0 = h * 128 + m * 64
                    grp.append(P.dma("sp", f"D_qh{b}", B(lambda e, b, m, row0: e.dma_start(
                        out=QH[b][m][0:64, :], in_=qs[row0:row0 + 64, :]), b, m, row0),
                        reads=[f"qs{h}_{tg}" for tg in range(NTG)], writes=[f"QH{b}"]))
                P.alias_group(grp)
                grp = []
                for r in range(2):
                    for half in range(2):
                        grp.append(P.dma("sp", f"D_vh{b}", B(lambda e, b, r, h, half, pr: e.dma_start(
                            out=VH[b][:, r * 16 + half * 8:r * 16 + half * 8 + 8, 0:128],
                            in_=vview(r, pr)[half * 1024:(half + 1) * 1024, (h % 2) * 128:(h % 2) * 128 + 128].rearrange(
                                "(t p) f -> p t f", p=128)), b, r, h, half, pr),
                            reads=[f"agdst{pr}"], writes=[f"VH{b}"]))
                P.alias_group(grp)

            s_banks = [4, 5, 6, 7]
            cs = {"s": 0, "pt": 0, "sm": 0, "yb": 0, "unit": 0}
            tq = []
            TLAG = 10

            def flush_t(force=False):
                while tq and (force or cs["unit"] - tq[0][0] >= TLAG):
                    _, yb, h, t = tq.pop(0)
                    sb = s_banks[cs["s"] % 4]
                    cs["s"] += 1
                    P.op("pe", B(lambda e, yb, sb: e.transpose(out=banks[sb][:, 0:64].bitcast(BF16), in_=YB[:, yb, :],
                                                                identity=ident), yb, sb),
                         reads=[f"YB{yb}", "ident"], writes=[f"ps{sb}"], signal=True)
                    P.op("dve", B(lambda e, sb, h, t: e.tensor_copy(out=CAT[:, 4 + h, t * 128:(t + 1) * 128],
                                                                    in_=banks[sb][:, 0:64].bitcast(BF16)), sb, h, t),
                         reads=[f"ps{sb}"], writes=[f"CATB{h}_{t}"])

            def attn_head(h):
                b = h % 2
                slope = 2.0 ** (-(h + 1))
                for qg in range(4):
                    units = []
                    for tp in range(4 * qg + 4):
                        for r in range(2):
                            for m in range(2):
                                units.append((tp, r, m))
                    first_o = [True] * 4
                    pend = []

                    def qk(u):
                        tp, r, m = u
                        kt = r * 16 + tp
                        tmin = max(tp, 4 * qg)
                        n = (4 * qg + 4 - tmin) * 128
                        c0 = tmin * 128
                        sb = s_banks[cs["s"] % 4]
                        cs["s"] += 1
                        diag = tp >= 4 * qg
                        P.op("pe", B(lambda e, b, m, kt, c0, n, sb, diag: e.matmul(
                            banks[sb][:, 0:n], lhsT=KH[b][m][0:68, kt * 128:(kt + 1) * 128],
                            rhs=QH[b][m][0:68, c0:c0 + n], start=True, stop=(not diag)), b, m, kt, c0, n, sb, diag),
                            reads=[f"KH{b}", f"KHaug{b}", f"QH{b}", f"QHaug{b}"], writes=[f"ps{sb}"], signal=(not diag))
                        if diag:
                            mk = maskA if r == 0 else maskB
                            P.op("pe", B(lambda e, sb, mk: e.matmul(banks[sb][:, 0:128], lhsT=ident, rhs=mk,
                                                                     start=False, stop=True), sb, mk),
                                 reads=["ident", "maskA", "maskB"], writes=[f"ps{sb}"], signal=True)
                        pt = cs["pt"] % 6
                        cs["pt"] += 1
                        P.op("act", B(lambda e, pt, sb, n: e.activation(out=PT[:, pt, 0:n], in_=banks[sb][:, 0:n],
                                                                         func=AF.Exp, scale=float(slope)), pt, sb, n),
                             reads=[f"ps{sb}"], writes=[f"PT{pt}"])
                        return (u, pt, tmin, n)

                    def pv(info):
                        (tp, r, m), pt, tmin, n = info
                        kt = r * 16 + tp
                        cs["unit"] += 1
                        flush_t()
                        for t in range(tmin, 4 * qg + 4):
                            ob = t - 4 * qg
                            st = first_o[ob]
                            first_o[ob] = False
                            last = (tp == t and r == 1)
                            off = (t - tmin) * 128
                            P.op("pe", B(lambda e, ob, m, pt, off, b, kt, st, last: e.matmul(
                                banks[ob][:, m * 129:(m + 1) * 129], lhsT=PT[:, pt, off:off + 128],
                                rhs=VH[b][:, kt, 0:129], start=st, stop=last, skip_group_check=True),
                                ob, m, pt, off, b, kt, st, last),
                                reads=[f"PT{pt}", f"VH{b}", f"VHaug{b}"], writes=[f"ps{ob}"],
                                signal=(last or t == 4 * qg + 3))
                            if last and m == 1:
                                post(h, t, ob)

                    def post(h, t, ob):
                        sm = cs["sm"] % 2
                        cs["sm"] += 1
                        bk = banks[ob]
                        P.op("dve", B(lambda e, sm, bk: e.reciprocal(out=RL[:, sm, 0:2], in_=bk[:, 128:258:129]), sm, bk),
                             reads=[f"ps{ob}"], writes=[f"RL{sm}"])
                        P.op("dve", B(lambda e, sm: e.tensor_tensor(out=RL[:, sm, 2:3], in0=RL[:, sm, 1:2],
                                                                    in1=lamcol[:, 5:6], op=ALU.mult), sm),
                             reads=[f"RL{sm}", "lamcol"], writes=[f"RLb{sm}"])
                        P.op("dve", B(lambda e, sm, bk: e.tensor_scalar(out=T1[:, sm, :], in0=bk[:, 129:257],
                                                                        scalar1=RL[:, sm, 2:3], scalar2=None,
                                                                        op0=ALU.mult), sm, bk),
                             reads=[f"ps{ob}", f"RLb{sm}"], writes=[f"T1{sm}"])
                        P.op("dve", B(lambda e, sm, bk: e.scalar_tensor_tensor(out=OSB[:, sm, :], in0=bk[:, 0:128],
                                                                               scalar=RL[:, sm, 0:1], in1=T1[:, sm, :],
                                                                               op0=ALU.mult, op1=ALU.subtract), sm, bk),
                             reads=[f"ps{ob}", f"RL{sm}", f"T1{sm}"], writes=[f"OSB{sm}"])
                        P.op("dve", B(lambda e, sm: e.tensor_tensor(out=SQJ[:, sm, :], in0=OSB[:, sm, :],
                                                                    in1=OSB[:, sm, :], op=ALU.mult), sm),
                             reads=[f"OSB{sm}"], writes=[f"SQJ{sm}"])
                        P.op("dve", B(lambda e, sm: e.reduce_sum(out=SSQ[:, sm, 0:1], in_=SQJ[:, sm, :], axis=AX), sm),
                             reads=[f"SQJ{sm}"], writes=[f"SSQ{sm}"])
                        P.op("act", B(lambda e, sm: e.activation(out=SSQ[:, sm, 1:2], in_=SSQ[:, sm, 0:1], func=AF.Ln,
                                                                 scale=1.0 / 128.0, bias=epscol[:, 0:1]), sm),
                             reads=[f"SSQ{sm}", "epscol"], writes=[f"SSL{sm}"])
                        P.op("act", B(lambda e, sm: e.activation(out=SSQ[:, sm, 2:3], in_=SSQ[:, sm, 1:2], func=AF.Exp,
                                                                 scale=-0.5), sm),
                             reads=[f"SSL{sm}"], writes=[f"SSE{sm}"])
                        yb = cs["yb"] % 8
                        cs["yb"] += 1
                        P.op("dve", B(lambda e, sm, yb: e.scalar_tensor_tensor(out=YB[:, yb, :], in0=OSB[:, sm, :],
                                                                               scalar=SSQ[:, sm, 2:3], in1=gsub,
                                                                               op0=ALU.mult, op1=ALU.mult), sm, yb),
                             reads=[f"OSB{sm}", f"SSE{sm}", "gsub"], writes=[f"YB{yb}"])
                        tq.append((cs["unit"], yb, h, t))

                    SKEW = 3
                    for i, u in enumerate(units):
                        pend.append(qk(u))
                        if len(pend) > SKEW:
                            pv(pend.pop(0))
                    while pend:
                        pv(pend.pop(0))

            NHEADS = 8 if KMIX >= 9 else (KMIX - 3 if KMIX >= 4 else 0)
            if NHEADS:
                load_head(0)
            for h in range(NHEADS):
                if h + 1 < NHEADS:
                    load_head(h + 1)
                attn_head(h)
                if NHEADS >= 8:
                    if h == 1:
                        emit_conv_prep()
                    elif 2 <= h <= 5:
                        emit_conv_chunk(h - 2)
            if NHEADS < 8:
                emit_conv()
            flush_t(force=True)

            P.barrier()
            al.off = base
            XT6 = [al([128, NCH, TG], F32) for _ in range(2)]
            WS6 = [al([128, NCH, 256], BF16) for _ in range(3)]
            c6 = {"ws": 0, "bk": 0}
            for tg in range(NTG):
                t0 = tg * TG
                xb = tg % 2
                X6 = XT6[xb]
                grp = []
                for q in range(2):
                    grp.append(P.dma("sp", f"D_xt6{xb}", B(lambda e, q, t0, X6: e.dma_start(
                        out=X6[:, q * 8:(q + 1) * 8, :], in_=xsv[:, q * 8:(q + 1) * 8, t0:t0 + TG]), q, t0, X6),
                        reads=[f"xdram{tg}"], writes=[f"X6{xb}_{c}" for c in range(q * 8, q * 8 + 8)]))
                P.alias_group(grp)
                for cg in range(D // 256):
                    slot = c6["ws"] % 3
                    c6["ws"] += 1
                    P.dma("pool", f"D_ws{slot}", B(lambda e, slot, cg: e.dma_start(
                        out=WS6[slot][:, :, :], in_=wout[:, :, cg * 256:(cg + 1) * 256]), slot, cg), writes=[f"WS{slot}"])
                    for cc in range(2):
                        j = 2 * cg + cc
                        bank = c6["bk"] % 4
                        c6["bk"] += 1
                        for k in range(NCH):
                            P.op("pe", B(lambda e, slot, k, cc, bank, t0: e.matmul(
                                banks[bank][:, :], lhsT=WS6[slot][:, k, cc * 128:(cc + 1) * 128],
                                rhs=CAT[:, k, t0:t0 + TG], start=(k == 0), stop=(k == NCH - 1)), slot, k, cc, bank, t0),
                                reads=[f"WS{slot}"], writes=[f"ps{bank}"], signal=(k == NCH - 1))
                        P.op("dve", B(lambda e, j, bank, X6: e.tensor_tensor(out=X6[:, j, :], in0=banks[bank][:, :],
                                                                             in1=X6[:, j, :], op=ALU.add), j, bank, X6),
                             reads=[f"ps{bank}", f"X6{xb}_{j}"], writes=[f"X6{xb}_{j}"])
                grp = []
                for q in range(2):
                    grp.append(P.dma("sp", f"D_xst6{xb}", B(lambda e, q, t0, X6: e.dma_start(
                        out=xdv[:, q * 8:(q + 1) * 8, t0:t0 + TG], in_=X6[:, q * 8:(q + 1) * 8, :]), q, t0, X6),
                        reads=[f"X6{xb}_{c}" for c in range(q * 8, q * 8 + 8)], writes=[f"xdram{tg}"]))
                P.alias_group(grp)

        MIX["emit"] = emit_mix

        setup()
        P.barrier()
        cur_src = xT_in
        n_sub = len(sublayers)
        for i, (kind, l) in enumerate(sublayers):
            last = (i == n_sub - 1)
            if kind in ("ffn1", "ffn2"):
                emit_ffn(l, 1 if kind == "ffn1" else 2, cur_src, xs, final=(last and final_norm))
            else:
                MIX["emit"](l, cur_src, xs)
            cur_src = xs
            nxt = sublayers[i + 1][0] if i + 1 < n_sub else None
            if not (kind in ("ffn1", "ffn2") and nxt in ("ffn1", "ffn2")):
                P.barrier()
        if not final_norm:
            al = Alloc(arena, 0, ARENA_BYTES)
            XT = al([128, NCH, 1024], F32)
            xsv = xs.rearrange("(c p) t -> p c t", p=128)
            outv = outT.rearrange("(c p) t -> p c t", p=128)
            for tg in range(2):
                t0 = tg * 1024
                P.dma("sp", "D_xt", (lambda t0: lambda e: e.dma_start(out=XT, in_=xsv[:, :, t0:t0 + 1024]))(t0),
                      reads=["xdram"], writes=["XTall"])
                P.dma("sp", "D_xst", (lambda t0: lambda e: e.dma_start(out=outv[:, :, t0:t0 + 1024], in_=XT))(t0),
                      reads=["XTall"], writes=["outdram"])
            P.barrier()

        P.barrier()
        P.finalize()

        sems = {}
        for k in P.sem_names:
            sems[k] = ctx.enter_context(nc.semaphore(k))
        block = ctx.enter_context(nc.Block())

        @block.tensor
        def _(e):
            _emit_engine(P, "pe", e, sems)

        @block.scalar
        def _(e):
            _emit_engine(P, "act", e, sems)

        @block.vector
        def _(e):
            _emit_engine(P, "dve", e, sems)

        @block.gpsimd
        def _(e):
            _emit_engine(P, "pool", e, sems)

        @block.sync
        def _(e):
            _emit_engine(P, "sp", e, sems)

    nc._planner_stats = {e: len(P.ops[e]) for e in P.ENGS}
    nc._sem_counts = dict(P.sem_count)
    return nc


def core_token_index(r):
    t = np.arange(NBLK)
    return ((2 * t[:, None] + r) * 128 + np.arange(128)[None, :]).reshape(-1)


def make_common_inputs(inputs, sublayers):
    f32 = np.float32
    com = {}
    g = np.zeros((13, D), f32)
    g[0:4] = np.asarray(inputs["g_ffn1"], f32)
    g[4:8] = np.asarray(inputs["g_mix"], f32)
    g[8:12] = np.asarray(inputs["g_ffn2"], f32)
    g[12] = np.asarray(inputs["g_final"], f32)
    com["gcols"] = np.ascontiguousarray(g.reshape(13, NCH, 128).transpose(2, 0, 1).reshape(128, 13 * NCH))
    if any(k == "mix" for k, _ in sublayers):
        bf = ml_dtypes.bfloat16
        com["ident"] = np.eye(128, dtype=f32).astype(bf)
        com["identf"] = np.eye(128, dtype=f32)
        ii = np.arange(128)
        com["tril01"] = (ii[:, None] <= ii[None, :]).astype(f32)
        n = np.arange(SEQ)
        rr, tt, i_ = n // NTOK, (n % NTOK) // 128, n % 128
        com["kaug"] = np.stack([np.ones(SEQ), np.ones(SEQ), 2 * tt + rr, i_]).astype(f32).astype(bf)
    for kind, l in sublayers:
        if kind == "mix":
            com[f"win_{l}"] = np.asarray(inputs["w_in"][l], f32)
            com[f"wout_{l}"] = np.asarray(inputs["w_out"][l], f32)
            com[f"gv_{l}"] = np.asarray(inputs["g_sga_v"][l], f32).reshape(1, 512)
            com[f"ws_{l}"] = np.asarray(inputs["w_sga_s"][l], f32)
            com[f"bs_{l}"] = np.asarray(inputs["b_sga_s"][l], f32).reshape(1, 512)
            com[f"lq_{l}"] = np.asarray(inputs["lambda_qk"][l], f32).reshape(1, 256)
            com[f"gsub_{l}"] = np.asarray(inputs["g_diff_sub"][l], f32).reshape(1, 128)
            com[f"wc_{l}"] = np.ascontiguousarray(
                np.asarray(inputs["w_conv"][l], f32).reshape(3, 4, 128).transpose(2, 0, 1).reshape(128, 12))
    for kind, l in sublayers:
        if kind == "ffn1":
            com[f"w1g_{l}"] = np.asarray(inputs["w_ffn1_gate"][l], f32)
            com[f"w1u_{l}"] = np.asarray(inputs["w_ffn1_up"][l], f32)
            com[f"w1d_{l}"] = np.asarray(inputs["w_ffn1_down"][l], f32)
        elif kind == "ffn2":
            com[f"w2g_{l}"] = np.asarray(inputs["w_ffn2_gate"][l], f32)
            com[f"w2u_{l}"] = np.asarray(inputs["w_ffn2_up"][l], f32)
            com[f"w2d_{l}"] = np.asarray(inputs["w_ffn2_down"][l], f32)
    return com


def core_constants(r):
    f32 = np.float32
    bf = ml_dtypes.bfloat16
    n = np.arange(NTOK)
    t, i_ = n // 128, n % 128
    qaug = np.stack([-128.0 * (2 * t + r), -1.0 * i_, np.full(NTOK, 128.0), np.ones(NTOK)]).astype(f32).astype(bf)
    ii = np.arange(128)
    tri = np.where(ii[:, None] > ii[None, :], -MASK_BIG, 0.0).astype(f32)
    allm = np.full((128, 128), -MASK_BIG, f32)
    zero = np.zeros((128, 128), f32)
    if r == 0:
        mA, mB = tri, allm
        sel = np.tile(np.array([[0.0, 1.0]], f32), (128, 1))
    else:
        mA, mB = zero, tri
        sel = np.tile(np.array([[1.0, 0.0]], f32), (128, 1))
    return {"qaug": qaug, "maskA": mA.astype(bf), "maskB": mB.astype(bf), "sel": np.ascontiguousarray(sel)}


_NC_CACHE = {}


def run_sublayers(inputs, sublayers, x_cores, final_norm):
    key = (tuple(sublayers), final_norm)
    if key not in _NC_CACHE:
        _NC_CACHE[key] = build_program(list(sublayers), final_norm=final_norm)
    nc = _NC_CACHE[key]
    com = make_common_inputs(inputs, sublayers)
    in_maps = []
    has_mix = any(k == "mix" for k, _ in sublayers)
    for c in range(8):
        m = dict(com)
        m["xT"] = x_cores[c]
        if has_mix:
            m.update(core_constants(c % 2))
        in_maps.append(m)
    if os.environ.get("KTRACE"):
        res = run_bass_kernel_spmd(nc, in_maps, core_ids=list(range(8)), trace=True)
        print("KTRACE exec_time_ns", res.exec_time_ns, flush=True)
    else:
        res = run_bass_kernel_spmd(nc, in_maps, core_ids=list(range(8)))
    return [np.asarray(res.results[c]["outT"]) for c in range(8)]


def shard_x(x):
    x = np.asarray(x, np.float32)
    outs = []
    for c in range(8):
        b, r = c // 2, c % 2
        idx = core_token_index(r)
        outs.append(np.ascontiguousarray(x[b][idx].T))
    return outs


def unshard_x(outs):
    y = np.zeros((4, SEQ, D), np.float32)
    for c in range(8):
        b, r = c // 2, c % 2
        idx = core_token_index(r)
        y[b][idx] = outs[c].T
    return y


def kernel(**inputs):
    subl = []
    for l in range(DEPTH):
        subl += [("ffn1", l), ("mix", l), ("ffn2", l)]
    xc = shard_x(inputs["x"])
    outs = run_sublayers(inputs, subl, xc, final_norm=True)
    return unshard_x(outs)
```
